# Optimizing a Trainium2 kernel written in Bass

```python
import jax
import jax.numpy as jnp
from jax import lax
import numpy as np


D_MODEL = 1024
BATCH = 8
SEQ = 2048
DEPTH = 2

GRID_W = 64
CTX_LEN = 256
N_BRANCH = 4
BRANCH_W = 512

LRU_W = 512
LRU_BLOCKS = 8
LRU_BW = LRU_W // LRU_BLOCKS
LRU_CONV = 4
LRU_C = 8.0

GQA_HQ = 8
GQA_HKV = 2
GQA_DH = 64
WINDOW = 128
ATT_BLOCK = 128

MLA_H = 8
MLA_RQ = 384
MLA_RKV = 256
MLA_DN = 64
MLA_DR = 32
MLA_DV = 64

RET_H = 4
RET_DK = 128
RET_DV = 128
RET_CHUNK = 128

FF = 2816

ROPE_BASE = 10000.0
LN_EPS = 1e-5
MASK_VALUE = -1e30
DN_ALPHA = (2 * DEPTH) ** 0.25
DN_BETA = (8 * DEPTH) ** -0.25

IN_SIZES = (LRU_W, LRU_W, GQA_HQ * GQA_DH, GQA_HKV * GQA_DH, GQA_HKV * GQA_DH, MLA_RQ, MLA_RKV, MLA_DR, RET_H * RET_DK, RET_H * RET_DK, RET_H * RET_DV, RET_H * RET_DV, N_BRANCH * D_MODEL)
IN_COLS = sum(IN_SIZES)

kernel_name = 'hybrid_gated_parallel_diffusion_block'


def _split_cols(p):
    cuts = np.cumsum(np.array(IN_SIZES))[:-1].tolist()
    return jnp.split(p, cuts, axis=-1)


def _layer_norm(x, g, b):
    xf = x.astype(jnp.float32)
    mu = jnp.mean(xf, -1, keepdims=True)
    var = jnp.mean(jnp.square(xf - mu), -1, keepdims=True)
    y = (xf - mu) * lax.rsqrt(var + LN_EPS)
    return (y * g.astype(jnp.float32) + b.astype(jnp.float32)).astype(x.dtype)


def _rms_norm(x, g):
    xf = x.astype(jnp.float32)
    y = xf * lax.rsqrt(jnp.mean(jnp.square(xf), -1, keepdims=True) + LN_EPS)
    return (y * g.astype(jnp.float32)).astype(x.dtype)


def _axial_rope(x, rows, cols):
    dim = x.shape[-1]
    da = dim // 2
    nf = da // 2
    inv = ROPE_BASE ** (-jnp.arange(nf, dtype=jnp.float32) / nf)
    parts = []
    for pos, seg in ((rows, x[..., :da]), (cols, x[..., da:])):
        ang = pos.astype(jnp.float32)[:, None] * inv[None, :]
        cos = jnp.cos(ang)[:, None, :]
        sin = jnp.sin(ang)[:, None, :]
        sf = seg.astype(jnp.float32)
        s1, s2 = sf[..., :nf], sf[..., nf:]
        parts += [s1 * cos - s2 * sin, s2 * cos + s1 * sin]
    return jnp.concatenate(parts, -1).astype(x.dtype)


def _swiglu(h, w_in, w_out):
    a, b = jnp.split(h @ w_in, 2, axis=-1)
    return (jax.nn.silu(a) * b) @ w_out


def _half_ffn(h, shift, scale, gate, w1, w2, g, b):
    y = _swiglu(h * (1 + scale) + shift, w1, w2)
    return _layer_norm(DN_ALPHA * h + 0.5 * gate * y, g, b)


def _centred_dwconv(x, w, b):
    T = x.shape[1]
    left = LRU_CONV // 2
    xp = jnp.pad(x, ((0, 0), (left, LRU_CONV - 1 - left), (0, 0)))
    y = b
    for j in range(LRU_CONV):
        y = y + xp[:, j:j + T] * w[j]
    return y


def _rglru_coeffs(x, w_a, b_a, w_x, b_x, lam):
    B, T, _ = x.shape
    xb = x.reshape(B, T, LRU_BLOCKS, LRU_BW)
    r = jax.nn.sigmoid(jnp.einsum('btnd,nde->btne', xb, w_a).reshape(B, T, LRU_W) + b_a).astype(jnp.float32)
    i = jax.nn.sigmoid(jnp.einsum('btnd,nde->btne', xb, w_x).reshape(B, T, LRU_W) + b_x).astype(jnp.float32)
    log_a = -LRU_C * r * jax.nn.softplus(-lam.astype(jnp.float32))
    a = jnp.exp(log_a)
    bt = jnp.sqrt(1.0 - jnp.exp(2.0 * log_a)) * (i * x.astype(jnp.float32))
    return a, bt


def _linear_scan(a, b, h0):
    def comb(left, right):
        return (left[0] * right[0], right[0] * left[1] + right[1])
    acum, h = lax.associative_scan(comb, (a, b), axis=1)
    return h + acum * h0[:, None, :]


def _gqa_window(q, k, v, kc, vc, sink):
    B, S = q.shape[:2]
    G = GQA_HQ // GQA_HKV
    blk = ATT_BLOCK
    nb = S // blk
    Lc = kc.shape[1]
    scale = GQA_DH ** -0.5
    qb = q.reshape(B, nb, blk, GQA_HKV, G, GQA_DH)

    def windows(t):
        tp = jnp.pad(t, ((0, 0), (blk, blk), (0, 0), (0, 0))).reshape(B, nb + 2, blk, GQA_HKV, GQA_DH)
        return jnp.concatenate([tp[:, :nb], tp[:, 1:nb + 1], tp[:, 2:]], axis=2)

    kw, vw = windows(k), windows(v)
    s_win = jnp.einsum('bnqhgd,bnkhd->bnhgqk', qb, kw).astype(jnp.float32) * scale
    blocks = jnp.arange(nb)
    qpos = blocks[:, None] * blk + jnp.arange(blk)[None, :]
    kpos = (blocks[:, None] - 1) * blk + jnp.arange(3 * blk)[None, :]
    dist = qpos[:, :, None] - kpos[:, None, :]
    valid = (jnp.abs(dist) <= WINDOW) & (kpos[:, None, :] >= 0) & (kpos[:, None, :] < S)
    s_win = jnp.where(valid[None, :, None, None], s_win, MASK_VALUE)
    s_ctx = jnp.einsum('bnqhgd,bchd->bnhgqc', qb, kc).astype(jnp.float32) * scale
    s_sink = jnp.broadcast_to(sink.astype(jnp.float32).reshape(1, 1, GQA_HKV, G, 1, 1), s_ctx.shape[:-1] + (1,))
    p = jax.nn.softmax(jnp.concatenate([s_win, s_ctx, s_sink], -1), axis=-1).astype(v.dtype)
    nw = 3 * blk
    o = jnp.einsum('bnhgqk,bnkhd->bnqhgd', p[..., :nw], vw) + jnp.einsum('bnhgqc,bchd->bnqhgd', p[..., nw:nw + Lc], vc)
    return o.reshape(B, S, GQA_HQ * GQA_DH)


def _gqa_ctx(q, k, v, sink):
    B, L = q.shape[:2]
    G = GQA_HQ // GQA_HKV
    qg = q.reshape(B, L, GQA_HKV, G, GQA_DH)
    s = jnp.einsum('bqhgd,bkhd->bhgqk', qg, k).astype(jnp.float32) * (GQA_DH ** -0.5)
    s_sink = jnp.broadcast_to(sink.astype(jnp.float32).reshape(1, GQA_HKV, G, 1, 1), s.shape[:-1] + (1,))
    p = jax.nn.softmax(jnp.concatenate([s, s_sink], -1), axis=-1).astype(v.dtype)
    o = jnp.einsum('bhgqk,bkhd->bqhgd', p[..., :L], v)
    return o.reshape(B, L, GQA_HQ * GQA_DH)


def _mla_q(cq, g, w_uq, rows, cols):
    B, T, _ = cq.shape
    q = (_rms_norm(cq, g) @ w_uq).reshape(B, T, MLA_H, MLA_DN + MLA_DR)
    qn, qr = q[..., :MLA_DN], q[..., MLA_DN:]
    if rows is not None:
        qr = _axial_rope(qr, rows, cols)
    return jnp.concatenate([qn, qr], -1)


def _mla_kv(ckv, kr, g, w_ukv, rows, cols):
    B, T, _ = ckv.shape
    kv = (_rms_norm(ckv, g) @ w_ukv).reshape(B, T, MLA_H, MLA_DN + MLA_DV)
    kn, v = kv[..., :MLA_DN], kv[..., MLA_DN:]
    kr = kr[:, :, None, :]
    if rows is not None:
        kr = _axial_rope(kr, rows, cols)
    k = jnp.concatenate([kn, jnp.broadcast_to(kr, (B, T, MLA_H, MLA_DR))], -1)
    return k, v


def _dense_attend(q, k, v):
    s = jnp.einsum('bqhd,bkhd->bhqk', q, k).astype(jnp.float32) * ((MLA_DN + MLA_DR) ** -0.5)
    p = jax.nn.softmax(s, axis=-1).astype(v.dtype)
    return jnp.einsum('bhqk,bkhd->bqhd', p, v)


def _ret_heads(t, n_dim, rows, cols):
    B, T, _ = t.shape
    t = t.reshape(B, T, RET_H, n_dim)
    if rows is not None:
        t = _axial_rope(t, rows, cols)
    return t.astype(jnp.float32)


def _retention_chunkwise(q, k, v, gamma, state0, strict):
    B, T, H, DK = q.shape
    DV = v.shape[-1]
    C = RET_CHUNK
    nc = T // C
    qc = q.reshape(B, nc, C, H, DK)
    kc = k.reshape(B, nc, C, H, DK)
    vc = v.reshape(B, nc, C, H, DV)
    lg = jnp.log(gamma)
    pos = jnp.arange(C, dtype=jnp.float32)
    diff = pos[:, None] - pos[None, :]
    keep = (diff > 0) if strict else (diff >= 0)
    dmask = jnp.where(keep[None], jnp.exp(jnp.where(keep, diff, 0.0)[None] * lg[:, None, None]), 0.0)
    s = jnp.einsum('bnihd,bnjhd->bnhij', qc, kc) * dmask
    intra = jnp.einsum('bnhij,bnjhe->bnihe', s, vc)
    zeta = jnp.exp((C - 1 - pos)[:, None] * lg[None, :])
    kv = jnp.einsum('bnjhd,bnjhe->nbhde', kc * zeta[:, :, None], vc)
    decay = jnp.exp(C * lg)[None, :, None, None]

    def step(R, kv_n):
        return decay * R + kv_n, R

    final, before = lax.scan(step, state0, kv)
    xi = jnp.exp((pos + 1)[:, None] * lg[None, :])
    cross = jnp.einsum('bnihd,nbhde->bnihe', qc * xi[:, :, None], before)
    return (intra + cross).reshape(B, T, H, DV), final


def _retention_state(k, v, gamma):
    T = k.shape[1]
    w = jnp.exp((T - 1 - jnp.arange(T, dtype=jnp.float32))[:, None] * jnp.log(gamma)[None, :])
    return jnp.einsum('bthd,th,bthe->bhde', k, w, v)


def _head_group_norm(y, g):
    B, T, H, DV = y.shape
    mu = jnp.mean(y, -1, keepdims=True)
    var = jnp.mean(jnp.square(y - mu), -1, keepdims=True)
    yn = (y - mu) * lax.rsqrt(var + LN_EPS)
    return yn.reshape(B, T, H * DV) * g.astype(jnp.float32)


def _gated_merge(branches, gates, w_branch, w_out):
    B, T = gates.shape[:2]
    br = jnp.stack([b.astype(gates.dtype) for b in branches], axis=2)
    g = jax.nn.sigmoid(gates.reshape(B, T, N_BRANCH, D_MODEL))
    proj = jnp.einsum('btkw,kwd->btkd', br, w_branch)
    return jnp.einsum('btkd,btkd->btd', g, proj) @ w_out


def _mixer(hl, hc, rows, cols, need_ctx, w_in, conv_w, conv_b, lru_w_a, lru_b_a, lru_w_x, lru_b_x, lru_lambda, sink, q_norm, kv_norm, w_uq, w_ukv, ret_decay, gn_g, w_branch, w_out):
    B, S, _ = hl.shape
    Lc = hc.shape[1]
    dt = hl.dtype
    f32 = jnp.float32
    (ax_l, ay_l, bq_l, bk_l, bv_l, cq_l, ckv_l, ckr_l, dq_l, dk_l, dv_l, dg_l, gt_l) = _split_cols(hl @ w_in)
    (ax_c, ay_c, bq_c, bk_c, bv_c, cq_c, ckv_c, ckr_c, dq_c, dk_c, dv_c, dg_c, gt_c) = _split_cols(hc @ w_in)

    def flip(t):
        return t[:, ::-1]

    xa_l = _centred_dwconv(ax_l, conv_w, conv_b)
    xa_c = _centred_dwconv(ax_c, conv_w, conv_b)
    hs_l, hs_c = [], []
    for d in range(2):
        prm = (lru_w_a[d], lru_b_a[d], lru_w_x[d], lru_b_x[d], lru_lambda[d])
        src_c = flip(xa_c) if d == 1 else xa_c
        src_l = flip(xa_l) if d == 1 else xa_l
        st_c = _linear_scan(*_rglru_coeffs(src_c, *prm), jnp.zeros((B, LRU_W), f32))
        st_l = _linear_scan(*_rglru_coeffs(src_l, *prm), st_c[:, -1])
        hs_l.append(flip(st_l) if d == 1 else st_l)
        if need_ctx:
            hs_c.append(flip(st_c) if d == 1 else st_c)
    out_a_l = (hs_l[0] + hs_l[1]).astype(dt) * jax.nn.gelu(ay_l)

    qb_l = _axial_rope(bq_l.reshape(B, S, GQA_HQ, GQA_DH), rows, cols)
    kb_l = _axial_rope(bk_l.reshape(B, S, GQA_HKV, GQA_DH), rows, cols)
    vb_l = bv_l.reshape(B, S, GQA_HKV, GQA_DH)
    kb_c = bk_c.reshape(B, Lc, GQA_HKV, GQA_DH)
    vb_c = bv_c.reshape(B, Lc, GQA_HKV, GQA_DH)
    out_b_l = _gqa_window(qb_l, kb_l, vb_l, kb_c, vb_c, sink)

    k_c, v_c = _mla_kv(ckv_c, ckr_c, kv_norm, w_ukv, None, None)
    k_l, v_l = _mla_kv(ckv_l, ckr_l, kv_norm, w_ukv, rows, cols)
    q_l = _mla_q(cq_l, q_norm, w_uq, rows, cols)
    k_all = jnp.concatenate([k_c, k_l], axis=1)
    v_all = jnp.concatenate([v_c, v_l], axis=1)
    nb = S // ATT_BLOCK
    qblk = q_l.reshape(B, nb, ATT_BLOCK, MLA_H, MLA_DN + MLA_DR).transpose(1, 0, 2, 3, 4)
    o = lax.map(lambda qi: _dense_attend(qi, k_all, v_all), qblk)
    out_c_l = o.transpose(1, 0, 2, 3, 4).reshape(B, S, MLA_H * MLA_DV)

    gam = jax.nn.sigmoid(ret_decay.astype(f32))
    kd_c = _ret_heads(dk_c, RET_DK, None, None) * (RET_DK ** -0.5)
    vd_c = _ret_heads(dv_c, RET_DV, None, None)
    zero = jnp.zeros((B, RET_H, RET_DK, RET_DV), f32)
    if need_ctx:
        qd_c = _ret_heads(dq_c, RET_DK, None, None)
        oc_f, st_f = _retention_chunkwise(qd_c, kd_c, vd_c, gam[0], zero, False)
        oc_b, st_b = _retention_chunkwise(flip(qd_c), flip(kd_c), flip(vd_c), gam[1], zero, True)
    else:
        st_f = _retention_state(kd_c, vd_c, gam[0])
        st_b = _retention_state(flip(kd_c), flip(vd_c), gam[1])
    qd_l = _ret_heads(dq_l, RET_DK, rows, cols)
    kd_l = _ret_heads(dk_l, RET_DK, rows, cols) * (RET_DK ** -0.5)
    vd_l = _ret_heads(dv_l, RET_DV, None, None)
    ol_f, _ = _retention_chunkwise(qd_l, kd_l, vd_l, gam[0], st_f, False)
    ol_b, _ = _retention_chunkwise(flip(qd_l), flip(kd_l), flip(vd_l), gam[1], st_b, True)
    out_d_l = jax.nn.silu(dg_l) * _head_group_norm(ol_f + flip(ol_b), gn_g).astype(dt)

    y_l = _gated_merge([out_a_l, out_b_l, out_c_l, out_d_l], gt_l, w_branch, w_out)
    if not need_ctx:
        return y_l, None

    out_a_c = (hs_c[0] + hs_c[1]).astype(dt) * jax.nn.gelu(ay_c)
    out_b_c = _gqa_ctx(bq_c.reshape(B, Lc, GQA_HQ, GQA_DH), kb_c, vb_c, sink)
    q_c = _mla_q(cq_c, q_norm, w_uq, None, None)
    out_c_c = _dense_attend(q_c, k_c, v_c).reshape(B, Lc, MLA_H * MLA_DV)
    out_d_c = jax.nn.silu(dg_c) * _head_group_norm(oc_f + flip(oc_b), gn_g).astype(dt)
    y_c = _gated_merge([out_a_c, out_b_c, out_c_c, out_d_c], gt_c, w_branch, w_out)
    return y_l, y_c


def setup_inputs(seed: int = 0) -> dict:
    key = jax.random.key(seed)
    ks = jax.random.split(key, 32)
    f32 = jnp.float32

    def nrm(k, shape, scale):
        return jax.random.normal(k, shape, f32) * scale

    x = nrm(ks[0], (BATCH, SEQ, D_MODEL), 1.0)
    c = nrm(ks[1], (BATCH, D_MODEL), 1.0)
    ctx = nrm(ks[2], (BATCH, CTX_LEN, D_MODEL), 1.0)
    c_ctx = nrm(ks[3], (D_MODEL,), 1.0)
    w_mod = nrm(ks[4], (DEPTH, D_MODEL, 9 * D_MODEL), 0.5 * D_MODEL ** -0.5)
    b_mod = nrm(ks[5], (DEPTH, 9 * D_MODEL), 0.02)
    ln_g = 1.0 + nrm(ks[6], (DEPTH, 3, D_MODEL), 0.02)
    ln_b = nrm(ks[7], (DEPTH, 3, D_MODEL), 0.02)
    ffn_w_in = nrm(ks[8], (DEPTH, 2, D_MODEL, 2 * FF), D_MODEL ** -0.5)
    ffn_w_out = nrm(ks[9], (DEPTH, 2, FF, D_MODEL), DN_BETA * FF ** -0.5)
    w_in = nrm(ks[10], (DEPTH, D_MODEL, IN_COLS), D_MODEL ** -0.5)
    lru_conv_w = nrm(ks[11], (DEPTH, LRU_CONV, LRU_W), LRU_CONV ** -0.5)
    lru_conv_b = nrm(ks[12], (DEPTH, LRU_W), 0.02)
    lru_w_a = nrm(ks[13], (DEPTH, 2, LRU_BLOCKS, LRU_BW, LRU_BW), LRU_BW ** -0.5)
    lru_b_a = nrm(ks[14], (DEPTH, 2, LRU_W), 0.02)
    lru_w_x = nrm(ks[15], (DEPTH, 2, LRU_BLOCKS, LRU_BW, LRU_BW), LRU_BW ** -0.5)
    lru_b_x = nrm(ks[16], (DEPTH, 2, LRU_W), 0.02)
    u = jax.random.uniform(ks[17], (DEPTH, 2, LRU_W), f32, 0.9, 0.999)
    s = u ** (1.0 / LRU_C)
    lru_lambda = jnp.log(s) - jnp.log1p(-s)
    gqa_sink = nrm(ks[18], (DEPTH, GQA_HQ), 0.5)
    mla_q_norm = 1.0 + nrm(ks[19], (DEPTH, MLA_RQ), 0.02)
    mla_kv_norm = 1.0 + nrm(ks[20], (DEPTH, MLA_RKV), 0.02)
    mla_w_uq = nrm(ks[21], (DEPTH, MLA_RQ, MLA_H * (MLA_DN + MLA_DR)), MLA_RQ ** -0.5)
    mla_w_ukv = nrm(ks[22], (DEPTH, MLA_RKV, MLA_H * (MLA_DN + MLA_DV)), MLA_RKV ** -0.5)
    g0 = 1.0 - 2.0 ** (-5.0 - jnp.arange(RET_H, dtype=f32))
    ret_decay = (jnp.log(g0) - jnp.log1p(-g0))[None, None, :] + nrm(ks[23], (DEPTH, 2, RET_H), 0.1)
    ret_gn_g = 1.0 + nrm(ks[24], (DEPTH, RET_H * RET_DV), 0.02)
    w_branch = nrm(ks[25], (DEPTH, N_BRANCH, BRANCH_W, D_MODEL), BRANCH_W ** -0.5)
    w_out = nrm(ks[26], (DEPTH, D_MODEL, D_MODEL), DN_BETA * D_MODEL ** -0.5)
    return {'x': x, 'c': c, 'ctx': ctx, 'c_ctx': c_ctx, 'w_mod': w_mod, 'b_mod': b_mod, 'ln_g': ln_g, 'ln_b': ln_b, 'ffn_w_in': ffn_w_in, 'ffn_w_out': ffn_w_out, 'w_in': w_in, 'lru_conv_w': lru_conv_w, 'lru_conv_b': lru_conv_b, 'lru_w_a': lru_w_a, 'lru_b_a': lru_b_a, 'lru_w_x': lru_w_x, 'lru_b_x': lru_b_x, 'lru_lambda': lru_lambda, 'gqa_sink': gqa_sink, 'mla_q_norm': mla_q_norm, 'mla_kv_norm': mla_kv_norm, 'mla_w_uq': mla_w_uq, 'mla_w_ukv': mla_w_ukv, 'ret_decay': ret_decay, 'ret_gn_g': ret_gn_g, 'w_branch': w_branch, 'w_out': w_out}


def reference(x, c, ctx, c_ctx, w_mod, b_mod, ln_g, ln_b, ffn_w_in, ffn_w_out, w_in, lru_conv_w, lru_conv_b, lru_w_a, lru_b_a, lru_w_x, lru_b_x, lru_lambda, gqa_sink, mla_q_norm, mla_kv_norm, mla_w_uq, mla_w_ukv, ret_decay, ret_gn_g, w_branch, w_out):
    B, S, D = x.shape
    n_rows = S // GRID_W
    rows = jnp.repeat(jnp.arange(n_rows), GRID_W)
    cols = jnp.tile(jnp.arange(GRID_W), n_rows)
    s_lat = jax.nn.silu(c)
    s_ctx = jax.nn.silu(c_ctx)
    xl, xc = x, ctx
    for l in range(DEPTH):
        last = l == DEPTH - 1
        mod_l = (s_lat @ w_mod[l] + b_mod[l]).reshape(B, 9, 1, D)
        mod_c = (s_ctx @ w_mod[l] + b_mod[l]).reshape(9, 1, 1, D)
        ml = [mod_l[:, k] for k in range(9)]
        mc = [mod_c[k] for k in range(9)]
        xl = _half_ffn(xl, ml[0], ml[1], ml[2], ffn_w_in[l, 0], ffn_w_out[l, 0], ln_g[l, 0], ln_b[l, 0])
        xc = _half_ffn(xc, mc[0], mc[1], mc[2], ffn_w_in[l, 0], ffn_w_out[l, 0], ln_g[l, 0], ln_b[l, 0])
        hl = xl * (1 + ml[4]) + ml[3]
        hc = xc * (1 + mc[4]) + mc[3]
        yl, yc = _mixer(hl, hc, rows, cols, not last, w_in[l], lru_conv_w[l], lru_conv_b[l], lru_w_a[l], lru_b_a[l], lru_w_x[l], lru_b_x[l], lru_lambda[l], gqa_sink[l], mla_q_norm[l], mla_kv_norm[l], mla_w_uq[l], mla_w_ukv[l], ret_decay[l], ret_gn_g[l], w_branch[l], w_out[l])
        xl = _layer_norm(DN_ALPHA * xl + ml[5] * yl, ln_g[l, 1], ln_b[l, 1])
        xl = _half_ffn(xl, ml[6], ml[7], ml[8], ffn_w_in[l, 1], ffn_w_out[l, 1], ln_g[l, 2], ln_b[l, 2])
        if not last:
            xc = _layer_norm(DN_ALPHA * xc + mc[5] * yc, ln_g[l, 1], ln_b[l, 1])
            xc = _half_ffn(xc, mc[6], mc[7], mc[8], ffn_w_in[l, 1], ffn_w_out[l, 1], ln_g[l, 2], ln_b[l, 2])
    return xl
```

```python
import numpy as np
import concourse.bass as bass
import concourse.mybir as mybir
from concourse.bass_utils import run_bass_kernel_spmd
from contextlib import ExitStack

F32 = mybir.dt.float32
BF16 = mybir.dt.bfloat16
AF = mybir.ActivationFunctionType
ALU = mybir.AluOpType

LV = 9
ENGS = ("pe", "act", "dve", "pool", "sp")
NDS = 12

D = 1024
S = 2048
LC = 256
T = S + LC
NT = T // 128
FF = 2816
NFC = FF // 128
LN_EPS = 1e-5
ALPHA = 4.0 ** 0.25
O_AX, O_AY, O_BQ, O_BK, O_BV, O_CQ, O_CKV, O_CKR, O_DQ, O_DK, O_DV, O_DG, O_GT = (
    0, 512, 1024, 1536, 1664, 1792, 2176, 2432, 2464, 2976, 3488, 4000, 4512)
W_BQ, W_BK, W_CKR, W_DQ, W_DK = 0, 512, 640, 672, 1184
NSW = 1696
R_CW, R_CB, R_BA, R_BX, R_LAM, R_QN, R_KVN, R_GN = 0, 16, 20, 28, 36, 44, 47, 49


class KB:
    def __init__(self, nc, same_engine_sync=True):
        self.nc = nc
        self.es = ExitStack()
        self.eng = {"pe": nc.tensor, "act": nc.scalar, "dve": nc.vector,
                    "pool": nc.gpsimd, "sp": nc.sync}
        self.sems = {}
        self.cnt = {}
        for e in ENGS:
            self.sems[e] = self.es.enter_context(nc.semaphore("s_" + e))
            self.cnt[e] = 0
        self.dq = {}
        for q in ("sp", "pool"):
            for i in range(NDS):
                k = ("d", q, i)
                self.sems[k] = self.es.enter_context(nc.semaphore("d_%s_%d" % (q, i)))
                self.cnt[k] = 0
            self.dq[q] = 0
        self.lastw = {}
        self.basew = {}
        self.readers = {}
        self.waited = {e: {} for e in ENGS}
        self.same = same_engine_sync
        self.n_ins = {e: 0 for e in ENGS}

    def _deps(self, e, reads, writes, more=()):
        ev = []
        for r in reads:
            ev.extend(self.lastw.get(r, ()))
        for w_ in writes:
            ev.extend(self.lastw.get(w_, ()))
            ev.extend(self.readers.get(w_, {}).values())
        for w_ in more:
            bw = self.basew.get(w_)
            if bw is not None:
                ev.append(bw)
            ev.extend(self.readers.get(w_, {}).values())
        need = {}
        wd = self.waited[e]
        for (sk, v) in ev:
            if sk == e and (e == "pe" or not self.same):
                continue
            if wd.get(sk, 0) >= v:
                continue
            if need.get(sk, 0) < v:
                need[sk] = v
        for sk, v in need.items():
            self.eng[e].wait_ge(self.sems[sk], v)
            wd[sk] = v

    def _commit(self, event, reads, writes, more, rkey):
        for w_ in writes:
            self.lastw[w_] = [event]
            self.basew[w_] = event
            self.readers[w_] = {}
        for w_ in more:
            self.lastw.setdefault(w_, []).append(event)
        for r in reads:
            self.readers.setdefault(r, {})[rkey] = event

    def op(self, e, fn, reads=(), writes=(), more=()):
        self._deps(e, reads, writes, more)
        ins = fn(self.eng[e])
        self.cnt[e] += 1
        ins.then_inc(self.sems[e], 1)
        self.n_ins[e] += 1
        self._commit((e, self.cnt[e]), reads, writes, more, e)
        return ins

    def dma(self, q, out, in_, reads=(), writes=(), more=(), **kw):
        slot = self.dq[q] % NDS
        self.dq[q] += 1
        sk = ("d", q, slot)
        if self.cnt[sk] > 0 and self.waited[q].get(sk, 0) < self.cnt[sk]:
            self.eng[q].wait_ge(self.sems[sk], self.cnt[sk])
            self.waited[q][sk] = self.cnt[sk]
        self._deps(q, reads, writes, more)
        ins = self.eng[q].dma_start(out=out, in_=in_, **kw)
        self.cnt[sk] += 16
        ins.then_inc(self.sems[sk], 16)
        self.n_ins[q] += 1
        self._commit((sk, self.cnt[sk]), reads, writes, more, sk)
        return ins

    def barrier(self):
        for e in ENGS:
            for sk, v in self.cnt.items():
                if v == 0 or sk == e:
                    continue
                if self.waited[e].get(sk, 0) >= v:
                    continue
                self.eng[e].wait_ge(self.sems[sk], v)
                self.waited[e][sk] = v
        self.lastw.clear()
        self.basew.clear()
        self.readers.clear()

    def finish(self):
        for sk, v in self.cnt.items():
            if v == 0 or sk == "sp":
                continue
            if self.waited["sp"].get(sk, 0) >= v:
                continue
            self.eng["sp"].wait_ge(self.sems[sk], v)
            self.waited["sp"][sk] = v
        self.es.close()


class StopBuild(Exception):
    pass


class Rot:
    def __init__(self, items):
        self.items = list(items)
        self.i = 0

    def next(self):
        x = self.items[self.i % len(self.items)]
        self.i += 1
        return x


class Builder:
    def __init__(self, debug=False, n_layers=2, stages=None):
        self.debug = debug
        self.n_layers = n_layers
        self.stages = stages
        nc = bass.Bass("TRN2", target_bir_lowering=False)
        self.nc = nc
        self.k = KB(nc)
        self.top = ExitStack()
        self.uid = 0
        I = {}

        def inp(name, shape, dt=F32):
            I[name] = nc.dram_tensor(name, list(shape), dt, kind="ExternalInput").ap()

        inp("x", [S, D]); inp("ctx", [LC, D]); inp("cc", [16, 128])
        inp("w_mod", [2, D, 9 * D]); inp("b_mod", [2, 1, 9 * D])
        inp("ln_g", [2, 3, D]); inp("ln_b", [2, 3, D])
        inp("ffn_w_in", [2, 2, D, 2 * FF]); inp("ffn_w_out", [2, 2, FF, D])
        inp("w_in", [2, D, 8608]); inp("w_sw", [2, D, NSW])
        inp("prow", [2, 64, 128])
        inp("lru_w_a", [2, 2, 8, 64, 64]); inp("lru_w_x", [2, 2, 8, 64, 64])
        inp("gqa_sink", [2, 8]); inp("ret_decay", [2, 8])
        inp("mla_w_uq", [2, 384, 768]); inp("mla_w_uq_sw", [2, 384, 768]); inp("mla_w_ukv", [2, 256, 1024])
        inp("w_branch", [2, 4, 512, D]); inp("w_out", [2, D, D])
        inp("c_ident", [128, 128]); inp("c_tri", [128, 1024], BF16)
        inp("c_dist", [128, 512]); inp("c_mults", [128, 18]); inp("c_dpn", [8, 128, 512])
        inp("rope_b", [2, 64, T]); inp("rope_c", [2, 96, T]); inp("rope_d", [2, 128, T])
        self.I = I
        self.out = nc.dram_tensor("out", [S, D], F32, kind="ExternalOutput").ap()
        self.X = nc.dram_tensor("xs", [T, D], F32, kind="Internal").ap()
        self.modrow = nc.dram_tensor("modrow", [2, 2, 9 * D], F32, kind="Internal").ap()
        self.brs = nc.dram_tensor("brs", [16, 128, T], BF16, kind="Internal").ap()
        self.dbg = {}
        if debug:
            for l in range(n_layers):
                for s in ("f0", "mx", "f1"):
                    self.dbg["x_%d_%s" % (l, s)] = nc.dram_tensor(
                        "dbg_x_%d_%s" % (l, s), [T, D], F32, kind="ExternalOutput").ap()
                self.dbg["br_%d" % l] = nc.dram_tensor(
                    "dbg_br_%d" % l, [16, 128, T], BF16, kind="ExternalOutput").ap()

    def run_pipe(self, steps, sk, sk2=None):
        n = len(steps)
        postq = []
        for i in range(n + sk):
            if i < n:
                steps[i][0]()
            if i >= sk:
                st = steps[i - sk]
                st[1]()
                if len(st) > 2 and st[2]:
                    for (dl, fn) in st[2]:
                        postq.append((i + dl, fn))
            if postq and postq[0][0] <= i:
                postq.pop(0)[1]()
        for (_, fn) in postq:
            fn()

    def chk(self, name):
        if getattr(self, "stop", None) == name:
            self.stopped = True
        return getattr(self, "stopped", False)

    def sb(self, es, name, shape, dt=F32):
        self.uid += 1
        return es.enter_context(self.nc.sbuf_tensor("%s_%d" % (name, self.uid), list(shape), dt))

    def load_w(self, dst, key, src2d, nchunks, ncols, colblk=2048, q="pool", dcol0=0):
        k = self.k
        for c in range(nchunks):
            first = True
            for c0 in range(0, ncols, colblk):
                w = min(colblk, ncols - c0)
                kw = dict(writes=[(key, c)]) if first else dict(more=[(key, c)])
                k.dma(q, dst[:, c, dcol0 + c0:dcol0 + c0 + w], src2d[c * 128:(c + 1) * 128, c0:c0 + w], **kw)
                first = False

    def layer_norm(self, rr, rkey, lng, lnb, st, mv, tag):
        k = self.k
        for h in range(2):
            k.op("dve", lambda e, h=h: e.bn_stats(st[:, h * 6:(h + 1) * 6], rr[:, h * 512:(h + 1) * 512]),
                 reads=[rkey], writes=[(tag, "st", h)])
        k.op("dve", lambda e: e.bn_aggr(mv[:, 0:2], st[:, 0:12]), reads=[(tag, "st", 0), (tag, "st", 1)],
             writes=[(tag, "mv")])
        k.op("act", lambda e: e.activation(mv[:, 2:3], mv[:, 1:2], AF.Sqrt, bias=self.eps_col[:, 0:1], scale=1.0),
             reads=[(tag, "mv")], writes=[(tag, "sd")])
        k.op("dve", lambda e: e.reciprocal(mv[:, 3:4], mv[:, 2:3]), reads=[(tag, "sd")], writes=[(tag, "rs")])
        k.op("dve", lambda e: e.tensor_scalar(rr[:], rr[:], mv[:, 0:1], mv[:, 3:4], ALU.subtract, ALU.mult),
             reads=[rkey, (tag, "mv"), (tag, "rs")], writes=[rkey])
        k.op("pool", lambda e: e.tensor_tensor(rr[:], rr[:], lng[:], ALU.mult), reads=[rkey, "lng"], writes=[rkey])
        k.op("pool", lambda e: e.tensor_tensor(rr[:], rr[:], lnb[:], ALU.add), reads=[rkey, "lnb"], writes=[rkey])

    def prologue(self):
        k, nc, I = self.k, self.nc, self.I
        es = self.top
        self.PSALL = es.enter_context(nc.psum_tensor("psall", [128, 4096], F32))
        self.PS = [self.PSALL[:, i * 512:(i + 1) * 512] for i in range(8)]
        self.ident = self.sb(es, "ident", [128, 128])
        self.ones_bf = self.sb(es, "ones_bf", [128, 64], BF16)
        self.ones32 = self.sb(es, "ones32", [128, 128])
        self.onesm = self.sb(es, "onesm", [128, 128])
        self.eps_col = self.sb(es, "eps_col", [128, 1])
        self.modT = [[self.sb(es, "modT", [128, 72]) for r in range(2)] for l in range(2)]
        self.onep = [[self.sb(es, "onep", [128, 72]) for r in range(2)] for l in range(2)]
        self.pcol = [self.sb(es, "pcol", [128, 64]) for l in range(2)]
        self.s2 = self.sb(es, "s2", [128, 8, 2])
        k.dma("sp", self.ident[:], I["c_ident"], writes=["ident"])
        k.op("dve", lambda e: e.memset(self.ones_bf[:], 1.0), writes=["ones_bf"])
        k.op("dve", lambda e: e.memset(self.ones32[:], 1.0), writes=["ones32"])
        k.op("dve", lambda e: e.memset(self.onesm[:], 1.0 / 128.0), writes=["onesm"])
        k.op("dve", lambda e: e.memset(self.eps_col[:], LN_EPS), writes=["eps"])
        with ExitStack() as ls:
            cc = self.sb(ls, "cc", [16, 128])
            sT = self.sb(ls, "sT", [128, 16])
            s2 = self.s2
            brow = self.sb(ls, "brow", [2, 9 * D])
            mrow = self.sb(ls, "mrow", [2, 9 * D])
            wm = [self.sb(ls, "wm", [128, 8, 512]) for i in range(2)]
            m72 = self.sb(ls, "m72", [72, 128])
            pr = self.sb(ls, "pr", [64, 128])
            k.dma("sp", cc[:], I["cc"], writes=["cc"])
            k.op("pe", lambda e: e.transpose(self.PS[0][:, 0:16], cc[:], self.ident[0:16, 0:16]),
                 reads=["cc", "ident"], writes=[("ps", 0)])
            k.op("act", lambda e: e.activation(sT[:], self.PS[0][:, 0:16], AF.Silu), reads=[("ps", 0)], writes=["sT"])
            for r in range(2):
                k.op("dve", lambda e, r=r: e.tensor_copy(s2[:, :, r], sT[:, r * 8:(r + 1) * 8]), reads=["sT"],
                     writes=[("s2", r)])
            psr = Rot([1, 2])
            for l in range(self.n_layers):
                if l == 1:
                    k.dma("sp", pr[:], I["prow"][l], writes=["pr"])
                    k.op("pe", lambda e: e.transpose(self.PS[4][:, 0:64], pr[:], self.ident[0:64, 0:64]),
                         reads=["pr", "ident"], writes=[("ps", 4)])
                    k.op("dve", lambda e, l=l: e.tensor_copy(self.pcol[l][:], self.PS[4][:, 0:64]), reads=[("ps", 4)],
                         writes=[("pcol", l)])
                    continue
                for r in range(2):
                    k.dma("sp", brow[r:r + 1, :], I["b_mod"][l], writes=[("brow", r)])
                for blk in range(18):
                    w = wm[blk % 2]
                    wk = ("wm", blk % 2)
                    for c in range(8):
                        kw = dict(writes=[wk]) if c == 0 else dict(more=[wk])
                        k.dma("sp", w[:, c, :], I["w_mod"][l, c * 128:(c + 1) * 128, blk * 512:(blk + 1) * 512], **kw)
                    pi = psr.next()
                    for c in range(8):
                        k.op("pe", lambda e, c=c, w=w, pi=pi: e.matmul(self.PS[pi][0:2, :], s2[:, c, :], w[:, c, :],
                                                                        start=(c == 0), stop=(c == 7)),
                             reads=[wk, ("s2", 0), ("s2", 1)], writes=[("ps", pi)])
                    k.op("dve", lambda e, pi=pi, blk=blk: e.tensor_tensor(
                        mrow[:, blk * 512:(blk + 1) * 512], self.PS[pi][0:2, :], brow[:, blk * 512:(blk + 1) * 512],
                        ALU.add), reads=[("ps", pi), ("brow", 0), ("brow", 1)], writes=[("mrow", blk)])
                k.dma("sp", self.modrow[l], mrow[:], reads=[("mrow", b) for b in range(18)], writes=[("modrow", l)])
                for r in range(2):
                    k.dma("sp", m72[:], self.modrow[l, r].rearrange("(j p) -> j p", p=128), reads=[("modrow", l)],
                          writes=["m72"])
                    k.op("pe", lambda e: e.transpose(self.PS[3][:, 0:72], m72[:], self.ident[0:72, 0:72]),
                         reads=["m72", "ident"], writes=[("ps", 3)])
                    k.op("dve", lambda e, l=l, r=r: e.tensor_copy(self.modT[l][r][:], self.PS[3][:, 0:72]),
                         reads=[("ps", 3)], writes=[("modT", l, r)])
                    k.op("dve", lambda e, l=l, r=r: e.tensor_scalar_add(self.onep[l][r][:], self.modT[l][r][:], 1.0),
                         reads=[("modT", l, r)], writes=[("onep", l, r)])
                k.dma("sp", pr[:], I["prow"][l], writes=["pr"])
                k.op("pe", lambda e: e.transpose(self.PS[4][:, 0:64], pr[:], self.ident[0:64, 0:64]),
                     reads=["pr", "ident"], writes=[("ps", 4)])
                k.op("dve", lambda e, l=l: e.tensor_copy(self.pcol[l][:], self.PS[4][:, 0:64]), reads=[("ps", 4)],
                     writes=[("pcol", l)])
            k.barrier()

    def mod_tasks(self, es, l):
        k, I, PS = self.k, self.I, self.PS
        wm = [self.sb(es, "wm2", [128, 8, 256]) for i in range(2)]
        bb = [self.sb(es, "bb2", [2, 256]) for i in range(2)]
        mb = [self.sb(es, "mb2", [2, 256]) for i in range(2)]
        m72 = self.sb(es, "m72b", [72, 128])
        psr = Rot([4, 5])
        tasks = []
        for blk in range(36):
            def task(blk=blk):
                w, wk = wm[blk % 2], ("wm2", blk % 2)
                for c in range(8):
                    kw = dict(writes=[wk]) if c == 0 else dict(more=[wk])
                    k.dma("sp", w[:, c, :], I["w_mod"][l, c * 128:(c + 1) * 128, blk * 256:(blk + 1) * 256], **kw)
                for r in range(2):
                    kw = dict(writes=[("bb2", blk % 2)]) if r == 0 else dict(more=[("bb2", blk % 2)])
                    k.dma("sp", bb[blk % 2][r:r + 1, :], I["b_mod"][l][:, blk * 256:(blk + 1) * 256], **kw)
                pi = psr.next()
                for c in range(8):
                    k.op("pe", lambda e, c=c: e.matmul(PS[pi][0:2, 0:256], self.s2[:, c, :], w[:, c, :],
                                                       start=(c == 0), stop=(c == 7)), reads=[wk], writes=[("ps", pi)])
                k.op("dve", lambda e: e.tensor_tensor(mb[blk % 2][:], PS[pi][0:2, 0:256], bb[blk % 2][:], ALU.add),
                     reads=[("ps", pi), ("bb2", blk % 2)], writes=[("mb2", blk % 2)])
                k.dma("pool", self.modrow[l][:, blk * 256:(blk + 1) * 256], mb[blk % 2][:], reads=[("mb2", blk % 2)],
                      writes=[("modrowb", blk)])
            tasks.append(task)

        def finish():
            for r in range(2):
                k.dma("sp", m72[:], self.modrow[l, r].rearrange("(j p) -> j p", p=128),
                      reads=[("modrowb", b) for b in range(36)], writes=["m72b"])
                k.op("pe", lambda e: e.transpose(PS[6][:, 0:72], m72[:], self.ident[0:72, 0:72]), reads=["m72b"],
                     writes=[("ps", 6)])
                k.op("dve", lambda e: e.tensor_copy(self.modT[l][r][:], PS[6][:, 0:72]), reads=[("ps", 6)],
                     writes=[("modT", l, r)])
                k.op("dve", lambda e: e.tensor_scalar_add(self.onep[l][r][:], self.modT[l][r][:], 1.0),
                     reads=[("modT", l, r)], writes=[("onep", l, r)])
        return tasks, finish

    def ffn(self, l, f, src, dst, ksh, ksc, kg, lni, tiles, dbgname=None):
        k, nc, I = self.k, self.nc, self.I
        PS = self.PS
        with ExitStack() as es:
            w1b = self.sb(es, "w1b", [128, 8, 2 * FF], BF16)
            w2b = self.sb(es, "w2b", [128, NFC, D], BF16)
            xin = self.sb(es, "xin", [128, 3, D])
            hmT = self.sb(es, "hmT", [128, 8, 384], BF16)
            gT = self.sb(es, "gT", [128, NFC, 384], BF16)
            sg = [self.sb(es, "sg", [128, 384]) for i in range(2)]
            rr = [self.sb(es, "rr", [128, D]) for i in range(2)]
            grow = [self.sb(es, "grow", [128, D]) for r in range(2)]
            lng = self.sb(es, "lng", [128, D])
            lnb = self.sb(es, "lnb", [128, D])
            st = [self.sb(es, "st", [128, 12]) for i in range(2)]
            mv = [self.sb(es, "mv", [128, 4]) for i in range(2)]
            for blk in (0, 2, 1, 3):
                for c in range(8):
                    k.dma("pool", w1b[:, c, blk * 1408:(blk + 1) * 1408],
                          I["ffn_w_in"][l, f][c * 128:(c + 1) * 128, blk * 1408:(blk + 1) * 1408],
                          writes=[("w1b", c, blk)])
                if blk == 2:
                    self.load_w(w2b, "w2b", I["ffn_w_out"][l, f], NFC, D, colblk=1024)
            for r in range(2):
                k.dma("sp", grow[r][:], self.modrow[l, r, kg * D:(kg + 1) * D].partition_broadcast(128),
                      writes=[("grow", r)])
                k.op("pool", lambda e, r=r: e.tensor_scalar_mul(grow[r][:], grow[r][:], 0.5), reads=[("grow", r)],
                     writes=[("grow", r)])
            k.dma("sp", lng[:], I["ln_g"][l, lni].partition_broadcast(128), writes=["lng"])
            k.dma("sp", lnb[:], I["ln_b"][l, lni].partition_broadcast(128), writes=["lnb"])
            pst = Rot([0, 1])
            pab = Rot([2, 3, 4, 5])
            pyr = Rot([6, 7])
            groups = [tiles[i:i + 3] for i in range(0, len(tiles), 3)]
            ti = 0
            for grp in groups:
                n = 128 * len(grp)
                for tt, t in enumerate(grp):
                    r = 0 if t < 16 else 1
                    k.dma("sp", xin[:, tt, :], src(t), writes=[("xin", tt)])
                    for half in range(2):
                        pi = pst.next()
                        for cc_ in range(4):
                            c = half * 4 + cc_
                            k.op("pe", lambda e, c=c, cc_=cc_, tt=tt, pi=pi: e.transpose(
                                PS[pi][:, cc_ * 128:(cc_ + 1) * 128], xin[:, tt, c * 128:(c + 1) * 128], self.ident[:]),
                                reads=[("xin", tt), "ident"], writes=[("ps", pi)])
                        for cc_ in range(4):
                            c = half * 4 + cc_
                            k.op("act", lambda e, c=c, cc_=cc_, tt=tt, pi=pi, r=r: e.activation(
                                hmT[:, c, tt * 128:(tt + 1) * 128], PS[pi][:, cc_ * 128:(cc_ + 1) * 128], AF.Identity,
                                bias=self.modT[l][r][:, ksh * 8 + c:ksh * 8 + c + 1],
                                scale=self.onep[l][r][:, ksc * 8 + c:ksc * 8 + c + 1]),
                                reads=[("ps", pi), ("modT", l, r), ("onep", l, r)], writes=[("hmT", c, tt)])
                hk = [[("hmT", c, tt) for tt in range(len(grp))] for c in range(8)]
                for fc in range(NFC):
                    pa = pab.next()
                    pb = pab.next()
                    for c in range(8):
                        k.op("pe", lambda e, c=c, fc=fc, pa=pa: e.matmul(
                            PS[pa][:, 0:n], w1b[:, c, fc * 128:(fc + 1) * 128], hmT[:, c, 0:n],
                            start=(c == 0), stop=(c == 7)), reads=[("w1b", c, fc // 11)] + hk[c], writes=[("ps", pa)])
                    for c in range(8):
                        k.op("pe", lambda e, c=c, fc=fc, pb=pb: e.matmul(
                            PS[pb][:, 0:n], w1b[:, c, FF + fc * 128:FF + (fc + 1) * 128], hmT[:, c, 0:n],
                            start=(c == 0), stop=(c == 7)), reads=[("w1b", c, 2 + fc // 11)] + hk[c], writes=[("ps", pb)])
                    s = sg[fc % 2]
                    k.op("act", lambda e, s=s, pa=pa: e.activation(s[:, 0:n], PS[pa][:, 0:n], AF.Silu),
                         reads=[("ps", pa)], writes=[("sg", fc % 2)])
                    k.op("dve", lambda e, s=s, pb=pb, fc=fc: e.tensor_tensor(gT[:, fc, 0:n], s[:, 0:n], PS[pb][:, 0:n],
                                                                             ALU.mult),
                         reads=[("sg", fc % 2), ("ps", pb)], writes=[("gT", fc)])
                for tt, t in enumerate(grp):
                    r = 0 if t < 16 else 1
                    ri = ti % 2
                    ti += 1
                    R = rr[ri]
                    rk = ("rr", ri)
                    for dh in range(2):
                        py = pyr.next()
                        for fc in range(NFC):
                            k.op("pe", lambda e, fc=fc, tt=tt, dh=dh, py=py: e.matmul(
                                PS[py][:, :], gT[:, fc, tt * 128:(tt + 1) * 128], w2b[:, fc, dh * 512:(dh + 1) * 512],
                                start=(fc == 0), stop=(fc == NFC - 1)), reads=[("gT", fc), ("w2b", fc)],
                                writes=[("ps", py)])
                        kw = dict(writes=[rk]) if dh == 0 else dict(more=[rk])
                        k.op("dve", lambda e, R=R, dh=dh, py=py, r=r: e.tensor_tensor(
                            R[:, dh * 512:(dh + 1) * 512], PS[py][:, :], grow[r][:, dh * 512:(dh + 1) * 512], ALU.mult),
                            reads=[("ps", py), ("grow", r)], **kw)
                    k.op("dve", lambda e, R=R, tt=tt: e.scalar_tensor_tensor(
                        R[:], xin[:, tt, :], ALPHA, R[:], ALU.mult, ALU.add), reads=[("xin", tt), rk], writes=[rk])
                    self.layer_norm(R, rk, lng, lnb, st[ri], mv[ri], ("ln", ri))
                    k.dma("pool", dst(t), R[:], reads=[rk], writes=[("X", t)])
                    if self.debug and dbgname:
                        k.dma("sp", self.dbg[dbgname][t * 128:(t + 1) * 128, :], R[:], reads=[rk])
            k.barrier()

    def mixer(self, l, last):
        k, nc, I = self.k, self.nc, self.I
        PS = self.PS
        Xt = lambda t: self.X[t * 128:(t + 1) * 128, :]
        pc = self.pcol[l]
        qgroups = [(g * 512, 512) for g in range(4)] + ([] if last else [(2048, 256)])
        allgroups = [(g * 512, 512) for g in range(4)] + [(2048, 256)]
        with ExitStack() as ms:
            hT = self.sb(ms, "hT", [128, 8, T], BF16)
            brT = None
            with ExitStack() as es:
                xin = [self.sb(es, "mxin", [128, D]) for i in range(3)]
                pst = Rot([0, 1, 2, 3])
                for t in range(NT):
                    r = 0 if t < 16 else 1
                    xi = xin[t % 3]
                    xk = ("mxin", t % 3)
                    k.dma("sp", xi[:], Xt(t), reads=[("X", t)], writes=[xk])
                    for half in range(2):
                        pi = pst.next()
                        for cc_ in range(4):
                            c = half * 4 + cc_
                            k.op("pe", lambda e, c=c, cc_=cc_, xi=xi, pi=pi: e.transpose(
                                PS[pi][:, cc_ * 128:(cc_ + 1) * 128], xi[:, c * 128:(c + 1) * 128], self.ident[:]),
                                reads=[xk, "ident"], writes=[("ps", pi)])
                        for cc_ in range(4):
                            c = half * 4 + cc_
                            k.op("act", lambda e, c=c, cc_=cc_, t=t, pi=pi, r=r: e.activation(
                                hT[:, c, t * 128:(t + 1) * 128], PS[pi][:, cc_ * 128:(cc_ + 1) * 128], AF.Identity,
                                bias=self.modT[l][r][:, 3 * 8 + c:3 * 8 + c + 1],
                                scale=self.onep[l][r][:, 4 * 8 + c:4 * 8 + c + 1]),
                                reads=[("ps", pi), ("modT", l, r), ("onep", l, r)], writes=[("hT", c, t)])
                k.barrier()
            hTk = [[("hT", c, t) for t in range(NT)] for c in range(8)]

            def proj(pi, wt, wkey, col0, ncols, t0, n, prow0=0):
                for c in range(8):
                    k.op("pe", lambda e, c=c: e.matmul(PS[pi][prow0:prow0 + ncols, 0:n], wt[:, c, col0:col0 + ncols],
                                                       hT[:, c, t0:t0 + n], start=(c == 0), stop=(c == 7)),
                         reads=[(wkey, c)] + hTk[c][t0 // 128:(t0 + n) // 128], writes=[("ps", pi)])

            if self.chk("hT"):
                return
            self.mix_lru(l, ms, hT, brT, proj, pc, allgroups)
            if self.chk("lru"):
                for j in range(16):
                    k.dma("sp", self.dbg["br_%d" % l][j], self.brs[j], reads=[("brs", j)])
                return
            self.mix_gqa(l, ms, hT, brT, proj, hTk, qgroups, last)
            if self.chk("gqa"):
                for j in range(16):
                    k.dma("sp", self.dbg["br_%d" % l][j], self.brs[j], reads=[("brs", j)])
                return
            self.mix_mla(l, ms, hT, brT, proj, hTk, pc, qgroups, allgroups, last)
            if getattr(self, "stopped", False):
                return
            if self.chk("mla"):
                for j in range(16):
                    k.dma("sp", self.dbg["br_%d" % l][j], self.brs[j], reads=[("brs", j)])
                return
            self.mix_ret(l, ms, hT, brT, proj, hTk, pc, qgroups, allgroups, last)
            if self.chk("ret"):
                for j in range(16):
                    k.dma("sp", self.dbg["br_%d" % l][j], self.brs[j], reads=[("brs", j)])
                return
            if self.debug:
                for j in range(16):
                    k.dma("sp", self.dbg["br_%d" % l][j], self.brs[j], reads=[("brs", j)])
            self.mix_merge(l, ms, hT, hTk, last)

    def mix_lru(self, l, ms, hT, brT, proj, pc, allgroups):
        k, I, PS = self.k, self.I, self.PS
        with ExitStack() as es:
            brT = self.sb(es, "brP", [128, 4, T], BF16)
            wA = self.sb(es, "wA", [128, 8, 1024], BF16)
            wbd = self.sb(es, "wbd", [128, 2, 2, 4, 128], BF16)
            spc = self.sb(es, "spc", [128, 8, 3])
            axp = [self.sb(es, "axp", [128, 2310])] * 2
            xa = [self.sb(es, "xa", [128, T]) for i in range(2)]
            xab = [self.sb(es, "xab", [128, T], BF16)] * 2
            rb = [self.sb(es, "rb", [128, T]) for i in range(2)]
            ib = [self.sb(es, "ib", [128, T]) for i in range(2)]
            tb = [self.sb(es, "tb", [128, T]) for i in range(2)]
            hf = self.sb(es, "hf", [128, T])
            hb = self.sb(es, "hb", [128, T])
            gl = [self.sb(es, "gl", [128, 512]) for i in range(2)]
            mtasks, mfinish = ([], None)
            if l == 0 and self.n_layers == 2:
                mtasks, mfinish = self.mod_tasks(es, 1)
            self.load_w(wA, "wA", I["w_in"][l][:, O_AX:O_AX + 1024], 8, 1024, colblk=1024)
            k.op("pool", lambda e: e.memset(wbd[:], 0.0), writes=["wbd"])
            for g, nm in enumerate(("lru_w_a", "lru_w_x")):
                for d in range(2):
                    for nb in range(8):
                        p0 = (nb % 2) * 64
                        k.dma("pool", wbd[p0:p0 + 64, g, d, nb // 2, p0:p0 + 64], I[nm][l, d, nb], more=["wbd"])
            lam = pc[:, R_LAM:R_LAM + 8]
            k.op("act", lambda e: e.activation(spc[:, :, 0], lam, AF.Exp, scale=-1.0), reads=[("pcol", l)],
                 writes=["spc0"])
            k.op("act", lambda e: e.activation(spc[:, :, 0], spc[:, :, 0], AF.Ln, bias=1.0, scale=1.0), reads=["spc0"],
                 writes=["spc0"])
            k.op("dve", lambda e: e.tensor_scalar_mul(spc[:, :, 1], spc[:, :, 0], -8.0), reads=["spc0"],
                 writes=["spc1"])
            k.op("dve", lambda e: e.tensor_scalar_mul(spc[:, :, 2], spc[:, :, 0], -16.0), reads=["spc0"],
                 writes=["spc2"])
            k.op("dve", lambda e: e.memset(axp[0][:], 0.0), writes=[("axp", 0)])
            pr_ = Rot([0, 1, 2, 3])
            mt = iter(mtasks)

            def some_mod(n):
                for _ in range(n):
                    t = next(mt, None)
                    if t is not None:
                        t()

            def s1(c):
                X, XA, XB = axp[c % 2], xa[c % 2], xab[c % 2]
                kx, ka, kb = ("axp", 0), ("xa", c % 2), ("xab", 0)
                for (t0, n) in allgroups:
                    pi = pr_.next()
                    proj(pi, wA, "wA", c * 128, 128, t0, n)
                    o0 = 2 + t0 if t0 < S else 2053 + (t0 - S)
                    k.op("act", lambda e: e.copy(X[:, o0:o0 + n], PS[pi][:, 0:n]), reads=[("ps", pi)], more=[kx])
                for (o_in, o_out, n) in ((0, 0, S), (2051, S, LC)):
                    k.op("dve", lambda e: e.tensor_scalar(
                        XA[:, o_out:o_out + n], X[:, o_in:o_in + n], pc[:, R_CW + c:R_CW + c + 1],
                        pc[:, R_CB + c:R_CB + c + 1], ALU.mult, ALU.add), reads=[kx, ("pcol", l)],
                        **(dict(writes=[ka]) if o_out == 0 else dict(more=[ka])))
                for j in range(1, 4):
                    for (o_in, o_out, n) in ((0, 0, S), (2051, S, LC)):
                        k.op("dve", lambda e: e.scalar_tensor_tensor(
                            XA[:, o_out:o_out + n], X[:, o_in + j:o_in + j + n],
                            pc[:, R_CW + j * 4 + c:R_CW + j * 4 + c + 1], XA[:, o_out:o_out + n], ALU.mult, ALU.add),
                            reads=[kx, ka, ("pcol", l)], writes=[ka])
                k.op("pool", lambda e: e.tensor_copy(XB[:], XA[:]), reads=[ka], writes=[kb])

            def s2(c):
                XA, XB = xa[c % 2], xab[c % 2]
                ka, kb = ("xa", c % 2), ("xab", 0)
                for d in range(2):
                    for g, (buf, bkey, brow_) in enumerate(((rb[d], ("rb", d), R_BA), (ib[d], ("ib", d), R_BX))):
                        first = True
                        for (t0, n) in allgroups:
                            pi = pr_.next()
                            k.op("pe", lambda e: e.matmul(PS[pi][:, 0:n], wbd[:, g, d, c, :], XB[:, t0:t0 + n],
                                                          start=True, stop=True), reads=["wbd", kb], writes=[("ps", pi)])
                            k.op("act", lambda e: e.activation(
                                buf[:, t0:t0 + n], PS[pi][:, 0:n], AF.Sigmoid,
                                bias=pc[:, brow_ + d * 4 + c:brow_ + d * 4 + c + 1], scale=1.0),
                                reads=[("ps", pi), ("pcol", l)], **(dict(writes=[bkey]) if first else dict(more=[bkey])))
                            first = False
                for d in range(2):
                    dc = d * 4 + c
                    k.op("act", lambda e: e.activation(tb[d][:], rb[d][:], AF.Exp, scale=spc[:, dc, 2:3]),
                         reads=[("rb", d), "spc2"], writes=[("tb", d)])
                    k.op("act", lambda e: e.activation(rb[d][:], rb[d][:], AF.Exp, scale=spc[:, dc, 1:2]),
                         reads=[("rb", d), "spc1"], writes=[("rb", d)])
                for d in range(2):
                    k.op("act", lambda e: e.activation(tb[d][:], tb[d][:], AF.Sqrt, bias=1.0, scale=-1.0),
                         reads=[("tb", d)], writes=[("tb", d)])
                for d in range(2):
                    k.op("pool", lambda e: e.tensor_tensor(ib[d][:], ib[d][:], XA[:], ALU.mult), reads=[("ib", d), ka],
                         writes=[("ib", d)])
                    k.op("pool", lambda e: e.tensor_tensor(tb[d][:], tb[d][:], ib[d][:], ALU.mult),
                         reads=[("ib", d), ("tb", d)], writes=[("tb", d)])

            def s3(c):
                k.op("dve", lambda e: e.tensor_tensor_scan(hf[:, S:T], rb[0][:, S:T], tb[0][:, S:T], 0.0, ALU.mult,
                                                           ALU.add), reads=[("rb", 0), ("tb", 0)], writes=["hf"])
                k.op("dve", lambda e: e.tensor_tensor_scan(hf[:, 0:S], rb[0][:, 0:S], tb[0][:, 0:S], hf[:, T - 1:T],
                                                           ALU.mult, ALU.add), reads=[("rb", 0), ("tb", 0), "hf"],
                     writes=["hf"])
                k.op("dve", lambda e: e.tensor_tensor_scan(hb[:, ::-1], rb[1][:, ::-1], tb[1][:, ::-1], 0.0, ALU.mult,
                                                           ALU.add), reads=[("rb", 1), ("tb", 1)], writes=["hb"])
                k.op("pool", lambda e: e.tensor_tensor(hf[:], hf[:], hb[:], ALU.add), reads=["hf", "hb"], writes=["hf"])
                for gi, (t0, n) in enumerate(allgroups):
                    pi = pr_.next()
                    proj(pi, wA, "wA", 512 + c * 128, 128, t0, n)
                    G_, gk = gl[gi % 2], ("gl", gi % 2)
                    k.op("act", lambda e: e.activation(G_[:, 0:n], PS[pi][:, 0:n], AF.Gelu_apprx_tanh),
                         reads=[("ps", pi)], writes=[gk])
                    k.op("dve", lambda e: e.tensor_tensor(brT[:, c, t0:t0 + n], G_[:, 0:n], hf[:, t0:t0 + n], ALU.mult),
                         reads=[gk, "hf"], **(dict(writes=[("brT", c)]) if t0 == 0 else dict(more=[("brT", c)])))

            s1(0)
            for c in range(4):
                some_mod(3)
                s2(c)
                some_mod(3)
                if c + 1 < 4:
                    s1(c + 1)
                some_mod(3)
                s3(c)
            some_mod(99)
            if mfinish is not None:
                mfinish()
            for c in range(4):
                k.dma("sp", self.brs[c], brT[:, c, :], reads=[("brT", c)], writes=[("brs", c)])
            k.barrier()

    def rope_evac(self, out_ap, pa, pb, r0, r1, n, C, Sg, t0, t1b, t2b, okw, rs=None):
        k, PS = self.k, self.PS
        k.op("dve", lambda e: e.tensor_tensor(t1b[r0:r1, 0:n], PS[pa][r0:r1, 0:n], C[r0:r1, t0:t0 + n], ALU.mult),
             reads=[("ps", pa), "ropeC"], writes=["rt1"])
        k.op("dve", lambda e: e.tensor_tensor(t2b[r0:r1, 0:n], PS[pb][r0:r1, 0:n], Sg[r0:r1, t0:t0 + n], ALU.mult),
             reads=[("ps", pb), "ropeS"], writes=["rt2"])
        if rs is None:
            k.op("pool", lambda e: e.tensor_tensor(out_ap, t1b[r0:r1, 0:n], t2b[r0:r1, 0:n], ALU.add),
                 reads=["rt1", "rt2"], **okw)
        else:
            k.op("pool", lambda e: e.tensor_tensor(t1b[r0:r1, 0:n], t1b[r0:r1, 0:n], t2b[r0:r1, 0:n], ALU.add),
                 reads=["rt1", "rt2"], writes=["rt1"])
            k.op("pool", lambda e: e.tensor_tensor(out_ap, t1b[r0:r1, 0:n], rs[r0:r1, t0:t0 + n], ALU.mult),
                 reads=["rt1", "rsq"], **okw)

    def mix_gqa(self, l, ms, hT, brT, proj, hTk, qgroups, last):
        k, I, PS = self.k, self.I, self.PS
        with ExitStack() as es:
            brT = self.sb(es, "brP", [128, 4, T], BF16)
            wB = self.sb(es, "wB", [128, 8, 1408], BF16)
            qT = self.sb(es, "qT", [128, 8, T], BF16)
            kT = self.sb(es, "kT", [128, 2, T], BF16)
            V = self.sb(es, "V", [128, NT, 2, 128], BF16)
            sinkl = self.sb(es, "sinkl", [1, 128])
            C = self.sb(es, "ropeC", [128, T])
            Sg = self.sb(es, "ropeS", [128, T])
            t1b = [self.sb(es, "t1b", [128, 512]) for i in range(2)]
            t2b = [self.sb(es, "t2b", [128, 512]) for i in range(2)]
            tri = self.sb(es, "tri", [128, 1024], BF16)
            E = [self.sb(es, "E", [128, 512], BF16) for i in range(4)]
            sink = self.sb(es, "sink", [1, 8])
            sinkrow = self.sb(es, "sinkrow", [1, 8, 128])
            rc = [self.sb(es, "rc", [64, 512]) for i in range(2)]
            self.load_w(wB, "wB", I["w_in"][l][:, O_BQ:O_BQ + 512], 8, 512, colblk=512, dcol0=0)
            self.load_w(wB, "wB1", I["w_sw"][l][:, W_BQ:W_BQ + 512], 8, 512, colblk=512, dcol0=512)
            self.load_w(wB, "wB2", I["w_in"][l][:, O_BK:O_BK + 128], 8, 128, colblk=512, dcol0=1024)
            self.load_w(wB, "wB3", I["w_sw"][l][:, W_BK:W_BK + 128], 8, 128, colblk=512, dcol0=1152)
            self.load_w(wB, "wB4", I["w_in"][l][:, O_BV:O_BV + 128], 8, 128, colblk=512, dcol0=1280)
            k.dma("sp", C[0:64, :], I["rope_b"][0], writes=["ropeC"])
            k.dma("sp", Sg[0:64, :], I["rope_b"][1], writes=["ropeS"])
            k.dma("sp", C[64:128, :], I["rope_b"][0], more=["ropeC"])
            k.dma("sp", Sg[64:128, :], I["rope_b"][1], more=["ropeS"])
            k.dma("sp", tri[:], I["c_tri"], writes=["tri"])
            k.dma("sp", sink[:], I["gqa_sink"][l:l + 1, :], writes=["sink"])
            k.op("act", lambda e: e.activation(sink[:], sink[:], AF.Exp), reads=["sink"], writes=["sink"])
            k.op("dve", lambda e: e.memset(sinkrow[:], 1.0), writes=["sinkrow"])
            k.op("dve", lambda e: e.memset(sinkl[:], 0.0), writes=["sinkl"])
            k.op("dve", lambda e: e.memset(sinkl[0:1, 64:128], 1.0), writes=["sinkl"])
            k.op("pool", lambda e: e.memset(qT[64:128, :, :], 0.0), writes=["qTz"])
            k.op("pool", lambda e: e.memset(kT[64:128, :, :], 0.0), writes=["kTz"])
            k.op("pool", lambda e: e.memset(V[:], 1.0), writes=["Vones"])
            for h in range(8):
                k.op("dve", lambda e, h=h: e.tensor_scalar_mul(sinkrow[:, h, :], sinkrow[:, h, :], sink[0:1, h:h + 1]),
                     reads=["sink", "sinkrow"], writes=["sinkrow"])
            pr_ = Rot([0, 1, 2, 3])
            allg = [(g * 512, 512) for g in range(4)] + [(2048, 256)]
            ri = 0
            for (dst, dk, wk0, c0, wk1, c1, npair) in ((qT, "qT", "wB", 0, "wB1", 512, 4), (kT, "kT", "wB2", 1024, "wB3", 1152, 1)):
                for hp in range(npair):
                    for (t0, n) in allg:
                        pa, pb = pr_.next(), pr_.next()
                        proj(pa, wB, wk0, c0 + hp * 128, 128, t0, n)
                        proj(pb, wB, wk1, c1 + hp * 128, 128, t0, n)
                        t1, t2 = t1b[ri % 2], t2b[ri % 2]
                        k1, k2 = ("rt1", ri % 2), ("rt2", ri % 2)
                        ri += 1
                        k.op("dve", lambda e: e.tensor_tensor(t1[:, 0:n], PS[pa][:, 0:n], C[:, t0:t0 + n], ALU.mult),
                             reads=[("ps", pa), "ropeC"], writes=[k1])
                        k.op("dve", lambda e: e.tensor_tensor(t2[:, 0:n], PS[pb][:, 0:n], Sg[:, t0:t0 + n], ALU.mult),
                             reads=[("ps", pb), "ropeS"], writes=[k2])
                        h0, h1 = 2 * hp, 2 * hp + 1
                        k.op("pool", lambda e: e.tensor_tensor(dst[0:64, h0, t0:t0 + n], t1[0:64, 0:n], t2[0:64, 0:n],
                                                               ALU.add), reads=[k1, k2],
                             **(dict(writes=[(dk, h0)]) if t0 == 0 else dict(more=[(dk, h0)])))
                        k.op("dve", lambda e: e.tensor_tensor(dst[0:64, h1, t0:t0 + n], t1[64:128, 0:n], t2[64:128, 0:n],
                                                              ALU.add), reads=[k1, k2],
                             **(dict(writes=[(dk, h1)]) if t0 == 0 else dict(more=[(dk, h1)])))
            for t in range(NT):
                pi = pr_.next()
                for c in range(8):
                    k.op("pe", lambda e, c=c, t=t, pi=pi: e.matmul(PS[pi][:, 0:128], hT[:, c, t * 128:(t + 1) * 128],
                                                                   wB[:, c, 1280:1408], start=(c == 0), stop=(c == 7)),
                         reads=[("wB4", c), hTk[c][t]], writes=[("ps", pi)])
                k.op("act", lambda e, t=t, pi=pi: e.copy(V[:, t, :, 0:64],
                                                         PS[pi][:, 0:128].rearrange("p (a b) -> p a b", a=2)),
                     reads=[("ps", pi), "Vones"], writes=[("V", t)])
            psS = Rot([0, 1, 2, 3])
            psN = Rot([4, 5, 6, 7])
            eR = Rot([0, 1, 2, 3])
            qtiles = list(range(16)) + ([] if last else [16, 17])
            steps = []

            def mkA(j, m, g, nq, st):
                def A():
                    pS = psS.next()
                    ei = eR.next()
                    Eb, ek = E[ei], ("E", ei)
                    st["Eb"], st["ek"] = Eb, ek
                    k.op("pe", lambda e: e.matmul(
                        PS[pS][:, :].rearrange("p (a b) -> p a b", a=4), kT[:, g, j * 128:(j + 1) * 128],
                        qT[:, 4 * g:4 * g + 4, nq * 128:(nq + 1) * 128], start=True, stop=True),
                        reads=[("kT", g), "qTz", "kTz"] + [("qT", 4 * g + i) for i in range(4)], writes=[("ps", pS)])
                    k.op("act", lambda e: e.activation(Eb[:], PS[pS][:, :], AF.Exp, scale=0.125),
                         reads=[("ps", pS)], writes=[ek])
                    if m:
                        k.op("pool", lambda e: e.tensor_tensor(Eb[:], Eb[:], tri[:, (m - 1) * 512:m * 512], ALU.mult),
                             reads=[ek, "tri"], writes=[ek])
                return A

            def mkB(j, g, nq, st, idx, nk, pn):
                def B():
                    Eb, ek = st["Eb"], st["ek"]
                    k.op("pe", lambda e: e.matmul(PS[pn][:, :], V[:, j, g, :], Eb[:], start=(idx == 0), stop=False),
                         reads=[("V", j), ek], writes=[("ps", pn)])
                    if idx != nk - 1:
                        return
                    k.op("pe", lambda e: e.matmul(
                        PS[pn][:, :].rearrange("p (a b) -> p a b", a=4), sinkl[0:1, :],
                        sinkrow[0:1, 4 * g:4 * g + 4, :], start=False, stop=True),
                        reads=["sinkl", "sinkrow"], writes=[("ps", pn)])
                    R = rc[g]
                    k.op("dve", lambda e: e.reciprocal(R[:], PS[pn][64:128, :]), reads=[("ps", pn)], writes=[("rc", g)])
                    for i in range(4):
                        h = 4 * g + i
                        p0 = (h % 2) * 64
                        k.op("dve", lambda e, i=i, h=h, p0=p0: e.tensor_tensor(
                            brT[p0:p0 + 64, h // 2, nq * 128:(nq + 1) * 128], PS[pn][0:64, i * 128:(i + 1) * 128],
                            R[:, i * 128:(i + 1) * 128], ALU.mult), reads=[("ps", pn), ("rc", g)],
                            more=[("brT", h // 2)])
                return B

            for nq in qtiles:
                if nq < 16:
                    keys = [(j, m) for (j, m) in ((nq - 1, 1), (nq, 0), (nq + 1, 2)) if 0 <= j < 16] + [(16, 0), (17, 0)]
                else:
                    keys = [(16, 0), (17, 0)]
                for g in range(2):
                    pn = psN.next()
                    for idx, (j, m) in enumerate(keys):
                        st = {}
                        steps.append((mkA(j, m, g, nq, st), mkB(j, g, nq, st, idx, len(keys), pn)))
            self.run_pipe(steps, 3)
            for c in range(4):
                k.dma("sp", self.brs[4 + c], brT[:, c, :], reads=[("brT", c)], writes=[("brs", 4 + c)])
            k.barrier()

    def mix_mla(self, l, ms, hT, brT, proj, hTk, pc, qgroups, allgroups, last):
        k, I, PS = self.k, self.I, self.PS
        SC = 96.0 ** -0.5
        with ExitStack() as es:
            brT = self.sb(es, "brP", [128, 4, T], BF16)
            wuq = self.sb(es, "wuq", [128, 3, 768], BF16)
            wuqs = self.sb(es, "wuqs", [128, 3, 768], BF16)
            wukv = self.sb(es, "wukv", [128, 2, 1024], BF16)
            cqT = self.sb(es, "cqT", [128, 3, T], BF16)
            ckvT = self.sb(es, "ckvT", [128, 2, T], BF16)
            krT = self.sb(es, "krT", [96, T], BF16)
            V = self.sb(es, "Vc", [128, NT, 8, 128], BF16)
            C = self.sb(es, "ropeCc", [96, T])
            Sg = self.sb(es, "ropeSc", [96, T])
            t1b = self.sb(es, "t1c", [96, 512])
            t2b = self.sb(es, "t2c", [96, 512])
            p1 = ExitStack()
            wC = self.sb(p1, "wC", [128, 8, 800], BF16)
            self.load_w(wC, "wC", I["w_in"][l][:, O_CQ:O_CQ + 672], 8, 672, colblk=672, dcol0=0)
            self.load_w(wC, "wCs", I["w_sw"][l][:, W_CKR - 96:W_CKR + 32], 8, 128, colblk=128, dcol0=672)
            k.dma("sp", C[64:96, :], I["rope_c"][0, 64:96], writes=["ropeC"])
            k.dma("sp", Sg[64:96, :], I["rope_c"][1, 64:96], writes=["ropeS"])
            k.op("pool", lambda e: e.memset(V[:], 1.0), writes=["Vones"])
            self.load_w(wuq, "wuq", I["mla_w_uq"][l], 3, 768, colblk=768)
            self.load_w(wuqs, "wuqs", I["mla_w_uq_sw"][l], 3, 768, colblk=768)
            self.load_w(wukv, "wukv", I["mla_w_ukv"][l], 2, 1024, colblk=1024)
            sq = [self.sb(p1, "sq", [128, 512]) for i in range(2)]
            raw = self.sb(p1, "raw", [128, 3, 512])
            rsg = self.sb(p1, "rsg", [128, 512])
            pr_ = Rot([0, 1, 2, 3])
            for (dstT, dkey, col0, nch, dim, rg) in ((cqT, "cqT", 0, 3, 384.0, R_QN), (ckvT, "ckvT", 384, 2, 256.0, R_KVN)):
                for (t0, n) in allgroups:
                    pss = 4 + (0 if dkey == "cqT" else 1)
                    for c in range(nch):
                        pi = pr_.next()
                        proj(pi, wC, "wC", col0 + c * 128, 128, t0, n)
                        k.op("dve", lambda e, pi=pi, c=c: e.tensor_copy(raw[:, c, 0:n], PS[pi][:, 0:n]),
                             reads=[("ps", pi)], writes=[("raw", c)])
                        sqb = sq[c % 2]
                        k.op("pool", lambda e, c=c, sqb=sqb: e.tensor_tensor(sqb[:, 0:n], raw[:, c, 0:n], raw[:, c, 0:n],
                                                                             ALU.mult),
                             reads=[("raw", c)], writes=[("sq", c % 2)])
                        k.op("pe", lambda e, sqb=sqb, c=c, pss=pss: e.matmul(PS[pss][:, 0:n], self.ones32[:], sqb[:, 0:n],
                                                                             start=(c == 0), stop=(c == nch - 1)),
                             reads=["ones32", ("sq", c % 2)], writes=[("ps", pss)])
                    k.op("act", lambda e, pss=pss: e.activation(rsg[:, 0:n], PS[pss][:, 0:n], AF.Sqrt,
                                                                bias=self.eps_col[:, 0:1], scale=1.0 / dim),
                         reads=[("ps", pss), "eps"], writes=["rsg"])
                    k.op("dve", lambda e: e.reciprocal(rsg[:, 0:n], rsg[:, 0:n]), reads=["rsg"], writes=["rsg"])
                    for c in range(nch):
                        k.op("dve", lambda e, c=c: e.scalar_tensor_tensor(
                            dstT[:, c, t0:t0 + n], raw[:, c, 0:n], pc[:, rg + c:rg + c + 1], rsg[:, 0:n], ALU.mult, ALU.mult),
                             reads=[("raw", c), "rsg", ("pcol", l)],
                             **(dict(writes=[(dkey, c)]) if t0 == 0 else dict(more=[(dkey, c)])))
            for (t0, n) in allgroups:
                pa, pb = pr_.next(), pr_.next()
                proj(pa, wC, "wC", 576, 96, t0, n)
                for c in range(8):
                    k.op("pe", lambda e, c=c: e.matmul(PS[pb][0:96, 0:n], wC[:, c, 704:800], hT[:, c, t0:t0 + n],
                                                       start=(c == 0), stop=(c == 7)),
                         reads=[("wCs", c)] + hTk[c][t0 // 128:(t0 + n) // 128], writes=[("ps", pb)])
                self.rope_evac(krT[64:96, t0:t0 + n], pa, pb, 64, 96, n, C, Sg, t0, t1b, t2b,
                               dict(writes=["krT"]) if t0 == 0 else dict(more=["krT"]))
            for t in range(NT):
                pi = pr_.next()
                for c in range(2):
                    k.op("pe", lambda e, c=c, t=t, pi=pi: e.matmul(
                        PS[pi][:, :].rearrange("p (a b) -> p a b", a=8), ckvT[:, c, t * 128:(t + 1) * 128],
                        wukv[:, c, :].rearrange("p (h x) -> p h x", x=128)[:, :, 64:128], start=(c == 0), stop=(c == 1)),
                        reads=[("ckvT", c), ("wukv", c)], writes=[("ps", pi)])
                k.op("act", lambda e, t=t, pi=pi: e.copy(V[:, t, :, 0:64],
                                                         PS[pi][:, :].rearrange("p (a b) -> p a b", a=8)),
                     reads=[("ps", pi), "Vones"], writes=[("Vc", t)])
            k.barrier()
            p1.close()
            if self.chk("mla_v"):
                return
            qTh = [self.sb(es, "qTh", [96, T], BF16) for i in range(2)]
            kTh = [self.sb(es, "kTh", [96, T], BF16) for i in range(2)]
            E = [self.sb(es, "Ec", [128, 1024], BF16) for i in range(4)]
            rc = [self.sb(es, "rcc", [64, 512]) for i in range(2)]
            psS = Rot([0, 2, 4])
            psN = Rot([6, 7])
            eR = Rot([0, 1, 2, 3])
            steps = []

            def mkA(jp, t0, n, qh, kh, qk_, kk_, st):
                def A():
                    p0 = psS.next()
                    ei = eR.next()
                    Eb, ek = E[ei], ("Ec", ei)
                    st["Eb"], st["ek"] = Eb, ek
                    for u, j in enumerate(jp):
                        k.op("pe", lambda e: e.matmul(PS[p0 + u][:, 0:n], kh[0:96, j * 128:(j + 1) * 128],
                                                      qh[0:96, t0:t0 + n], start=True, stop=True), reads=[qk_, kk_],
                             writes=[("ps", p0 + u)])
                    src = self.PSALL[:, p0 * 512:(p0 + 2) * 512].rearrange("p (a b) -> p a b", a=2)[:, :, 0:n]
                    dst = Eb[:].rearrange("p (a b) -> p a b", a=2)[:, :, 0:n]
                    k.op("act", lambda e: e.activation(dst, src, AF.Exp, scale=SC),
                         reads=[("ps", p0), ("ps", p0 + 1)], writes=[ek])
                return A

            def mkB(jp, t0, n, h, st, first, lastp, pn):
                def B():
                    Eb, ek = st["Eb"], st["ek"]
                    for u, j in enumerate(jp):
                        k.op("pe", lambda e: e.matmul(PS[pn][:, 0:n], V[:, j, h, :], Eb[:, u * 512:u * 512 + n],
                                                      start=(first and u == 0), stop=(lastp and u == 1)),
                             reads=[("Vc", j), ek], writes=[("ps", pn)])
                    if not lastp:
                        return
                    R = rc[h % 2]
                    k.op("dve", lambda e: e.reciprocal(R[:, 0:n], PS[pn][64:128, 0:n]), reads=[("ps", pn)],
                         writes=[("rcc", h % 2)])
                    p0 = (h % 2) * 64
                    k.op("dve", lambda e: e.tensor_tensor(brT[p0:p0 + 64, h // 2, t0:t0 + n], PS[pn][0:64, 0:n], R[:, 0:n],
                                                          ALU.mult), reads=[("ps", pn), ("rcc", h % 2)],
                         more=[("brT", h // 2)])
                return B

            def prep(h):
                qh, kh = qTh[h % 2], kTh[h % 2]
                qk_, kk_ = ("qTh", h % 2), ("kTh", h % 2)
                for gi, (t0, n) in enumerate(allgroups):
                    kwq = dict(writes=[qk_]) if gi == 0 else dict(more=[qk_])
                    kwk = dict(writes=[kk_]) if gi == 0 else dict(more=[kk_])
                    pi = pr_.next()
                    for c in range(2):
                        k.op("pe", lambda e, c=c: e.matmul(PS[pi][:, 0:n], wukv[:, c, h * 128:(h + 1) * 128],
                                                           ckvT[:, c, t0:t0 + n], start=(c == 0), stop=(c == 1)),
                             reads=[("wukv", c), ("ckvT", c)], writes=[("ps", pi)])
                    k.op("dve", lambda e: e.tensor_copy(kh[0:64, t0:t0 + n], PS[pi][0:64, 0:n]),
                         reads=[("ps", pi)], **kwk)
                    k.op("pool", lambda e: e.tensor_copy(kh[64:96, t0:t0 + n], krT[64:96, t0:t0 + n]), reads=["krT"],
                         more=[kk_])
                    pa, pb = pr_.next(), pr_.next()
                    for c in range(3):
                        k.op("pe", lambda e, c=c: e.matmul(PS[pa][0:96, 0:n], wuq[:, c, h * 96:(h + 1) * 96],
                                                           cqT[:, c, t0:t0 + n], start=(c == 0), stop=(c == 2)),
                             reads=[("wuq", c), ("cqT", c)], writes=[("ps", pa)])
                    for c in range(3):
                        k.op("pe", lambda e, c=c: e.matmul(PS[pb][0:96, 0:n], wuqs[:, c, h * 96:(h + 1) * 96],
                                                           cqT[:, c, t0:t0 + n], start=(c == 0), stop=(c == 2)),
                             reads=[("wuqs", c), ("cqT", c)], writes=[("ps", pb)])
                    k.op("act", lambda e: e.copy(qh[0:64, t0:t0 + n], PS[pa][0:64, 0:n]), reads=[("ps", pa)], **kwq)
                    self.rope_evac(qh[64:96, t0:t0 + n], pa, pb, 64, 96, n, C, Sg, t0, t1b, t2b, dict(more=[qk_]))

            def withpre(pre, A):
                def f():
                    pre()
                    A()
                return f

            prep(0)
            for h in range(8):
                qh, kh = qTh[h % 2], kTh[h % 2]
                qk_, kk_ = ("qTh", h % 2), ("kTh", h % 2)
                firstA = True
                for (t0, n) in qgroups:
                    keys = list(range(NT)) if t0 < S else [16, 17]
                    pn = psN.next()
                    pairs = [keys[i:i + 2] for i in range(0, len(keys), 2)]
                    for idx, jp in enumerate(pairs):
                        st = {}
                        A = mkA(jp, t0, n, qh, kh, qk_, kk_, st)
                        if firstA and h + 1 < 8:
                            A = withpre(lambda h=h: prep(h + 1), A)
                        firstA = False
                        steps.append((A, mkB(jp, t0, n, h, st, idx == 0, idx == len(pairs) - 1, pn)))
            self.run_pipe(steps, 2)
            for c in range(4):
                k.dma("sp", self.brs[8 + c], brT[:, c, :], reads=[("brT", c)], writes=[("brs", 8 + c)])
            k.barrier()

    def mix_ret(self, l, ms, hT, brT, proj, hTk, pc, qgroups, allgroups, last):
        k, I, PS = self.k, self.I, self.PS
        LNS = float(np.log(128.0 ** -0.5))
        with ExitStack() as es:
            brT = self.sb(es, "brP", [128, 4, T], BF16)
            wt = [self.sb(es, "wD", [128, 8, 512], BF16) for i in range(3)]
            qT = self.sb(es, "qTd", [128, 4, T], BF16)
            kT = self.sb(es, "kTd", [128, 4, T], BF16)
            V = self.sb(es, "Vd", [128, NT, 512], BF16)
            dist = self.sb(es, "dist", [128, 512])
            mults = self.sb(es, "mults", [128, 18])
            dec = self.sb(es, "dec", [128, 8])
            lng_ = self.sb(es, "lngm", [128, 8])
            nlng = self.sb(es, "nlng", [128, 8])
            tab = self.sb(es, "tab", [128, 8, 18])
            p1 = ExitStack()
            C = self.sb(p1, "ropeCd", [128, T])
            Sg = self.sb(p1, "ropeSd", [128, T])
            t1b = self.sb(p1, "t1d", [128, 512])
            t2b = self.sb(p1, "t2d", [128, 512])
            k.dma("sp", C[:], I["rope_d"][0], writes=["ropeC"])
            k.dma("sp", Sg[:], I["rope_d"][1], writes=["ropeS"])
            k.dma("sp", dist[:], I["c_dist"], writes=["dist"])
            k.dma("sp", mults[:], I["c_mults"], writes=["mults"])
            k.dma("sp", dec[:], I["ret_decay"][l].partition_broadcast(128), writes=["dec"])
            k.op("act", lambda e: e.activation(nlng[:], dec[:], AF.Exp, scale=-1.0), reads=["dec"], writes=["nlng"])
            k.op("act", lambda e: e.activation(nlng[:], nlng[:], AF.Ln, bias=1.0, scale=1.0), reads=["nlng"],
                 writes=["nlng"])
            k.op("dve", lambda e: e.tensor_scalar_mul(lng_[:], nlng[:], -1.0), reads=["nlng"], writes=["lngm"])
            for j in range(8):
                k.op("dve", lambda e, j=j: e.tensor_scalar(tab[:, j, :], mults[:], lng_[:, j:j + 1], LNS, ALU.mult,
                                                           ALU.add), reads=["mults", "lngm"],
                     **(dict(writes=["tab"]) if j == 0 else dict(more=["tab"])))
            pr_ = Rot([0, 1, 2, 3])
            wi = 0

            def getw(src, ncols=512):
                nonlocal wi
                w = wt[wi % 3]
                key = "wD%d" % (wi % 3)
                wi += 1
                self.load_w(w, key, src, 8, ncols, colblk=512)
                return w, key

            for (dstT, dkey, o_w, o_s) in ((qT, "qTd", O_DQ, W_DQ), (kT, "kTd", O_DK, W_DK)):
                w0, k0 = getw(I["w_in"][l][:, o_w:o_w + 512])
                w1, k1 = getw(I["w_sw"][l][:, o_s:o_s + 512])
                for h in range(4):
                    for (t0, n) in allgroups:
                        pa, pb = pr_.next(), pr_.next()
                        proj(pa, w0, k0, h * 128, 128, t0, n)
                        proj(pb, w1, k1, h * 128, 128, t0, n)
                        self.rope_evac(dstT[:, h, t0:t0 + n], pa, pb, 0, 128, n, C, Sg, t0, t1b, t2b,
                                       dict(writes=[(dkey, h)]) if t0 == 0 else dict(more=[(dkey, h)]))
            wv, kv_ = getw(I["w_in"][l][:, O_DV:O_DV + 512])
            for t in range(NT):
                pi = pr_.next()
                for c in range(8):
                    k.op("pe", lambda e, c=c, t=t, pi=pi: e.matmul(PS[pi][:, :], hT[:, c, t * 128:(t + 1) * 128],
                                                                   wv[:, c, :], start=(c == 0), stop=(c == 7)),
                         reads=[(kv_, c), hTk[c][t]], writes=[("ps", pi)])
                k.op("act", lambda e, t=t, pi=pi: e.copy(V[:, t, :], PS[pi][:, :]), reads=[("ps", pi)],
                     writes=[("Vd", t)])
            wg, kg_ = getw(I["w_in"][l][:, O_DG:O_DG + 512])
            k.barrier()
            p1.close()
            Wm = [self.sb(es, "Wm", [128, 512]) for i in range(4)]
            W2 = [self.sb(es, "W2", [128, 512]) for i in range(2)]
            W3 = self.sb(es, "W3", [128, 512])
            Wmix = [self.sb(es, "Wmix", [128, 4, 512]) for i in range(2)]
            dpn = self.sb(es, "dpn", [128, 8, 512])
            for i in range(8):
                k.dma("sp", dpn[:, i, :], I["c_dpn"][i], writes=[("dpn", i)])
            Pb = [self.sb(es, "Pb", [128, 512], BF16) for i in range(6)]
            oh = self.sb(es, "oh", [128, T])
            xc = [self.sb(es, "xc", [128, 512]) for i in range(1)]
            sqd = [self.sb(es, "sqd", [128, 512]) for i in range(1)]
            sl = [self.sb(es, "sl", [128, 512]) for i in range(1)]
            psS = Rot([0, 1, 2, 3])
            psN = Rot([4, 5])
            psG = Rot([6, 7])
            wR = Rot([0, 1, 2, 3])
            pR = Rot([0, 1, 2, 3, 4, 5])
            steps = []

            def mkmix(h, i):
                def f():
                    A_, B_ = W2[0], W2[1]
                    k.op("dve", lambda e: e.tensor_scalar_mul(B_[:], dpn[:, 2 * i + 1, :], lng_[:, 4 + h:5 + h]),
                         reads=[("dpn", 2 * i + 1), "lngm"], writes=["W2b"])
                    k.op("dve", lambda e: e.scalar_tensor_tensor(A_[:], dpn[:, 2 * i, :], lng_[:, h:h + 1], B_[:], ALU.mult,
                                                                 ALU.add), reads=[("dpn", 2 * i), "W2b", "lngm"],
                         writes=["W2a"])
                    k.op("act", lambda e: e.activation(Wmix[h % 2][:, i, :], A_[:], AF.Exp, bias=tab[:, h, 0:1],
                                                       scale=1.0), reads=["W2a", "tab"],
                         **(dict(writes=[("Wmix", h % 2)]) if i == 0 else dict(more=[("Wmix", h % 2)])))
                return f

            def mkA(h, t0, n, j, st):
                def A():
                    f_sc, b_sc = lng_[:, h:h + 1], nlng[:, 4 + h:5 + h]
                    Wsrc = None
                    if (t0 < S) == (j < 16):
                        o = (4 * (t0 // 512) - j) if t0 < S else -(j - 16)
                        if -4 < o < 1:
                            Wsrc, wk = Wmix[h % 2][:, -o, 0:n], ("Wmix", h % 2)
                    if Wsrc is None:
                        wi_ = wR.next()
                        W, wk = Wm[wi_], ("Wm", wi_)
                        Wsrc = W[:, 0:n]
                        if (t0 < S) == (j < 16):
                            if o >= 1:
                                k.op("act", lambda e: e.activation(W[:, 0:n], dist[:, 0:n], AF.Exp, scale=f_sc,
                                                                   bias=tab[:, h, o:o + 1]),
                                     reads=["dist", "lngm", "tab"], writes=[wk])
                            else:
                                k.op("act", lambda e: e.activation(W[:, 0:n], dist[:, 0:n], AF.Exp, scale=b_sc,
                                                                   bias=tab[:, 4 + h, -o:-o + 1]),
                                     reads=["dist", "nlng", "tab"], writes=[wk])
                        else:
                            G = t0 // 512
                            jc = j - 16
                            mf = 4 * G - jc + 2
                            mb = 16 - 4 * G + jc
                            B_ = W3
                            k.op("act", lambda e: e.activation(W[:, 0:n], dist[:, 0:n], AF.Exp, scale=f_sc,
                                                               bias=tab[:, h, mf:mf + 1]),
                                 reads=["dist", "lngm", "tab"], writes=[wk])
                            k.op("act", lambda e: e.activation(B_[:, 0:n], dist[:, 0:n], AF.Exp, scale=b_sc,
                                                               bias=tab[:, 4 + h, mb:mb + 1]),
                                 reads=["dist", "nlng", "tab"], writes=["W3"])
                            k.op("pool", lambda e: e.tensor_tensor(W[:, 0:n], W[:, 0:n], B_[:, 0:n], ALU.add),
                                 reads=[wk, "W3"], writes=[wk])
                    pS = psS.next()
                    pi_ = pR.next()
                    P, pk = Pb[pi_], ("Pb", pi_)
                    st["P"], st["pk"] = P, pk
                    k.op("pe", lambda e: e.matmul(PS[pS][:, 0:n], kT[:, h, j * 128:(j + 1) * 128], qT[:, h, t0:t0 + n],
                                                  start=True, stop=True), reads=[("kTd", h), ("qTd", h)],
                         writes=[("ps", pS)])
                    k.op("dve", lambda e: e.tensor_tensor(P[:, 0:n], PS[pS][:, 0:n], Wsrc, ALU.mult),
                         reads=[("ps", pS), wk], writes=[pk])
                return A

            def mkB(h, t0, n, j, st, idx, nk, pn):
                def B():
                    P, pk = st["P"], st["pk"]
                    k.op("pe", lambda e: e.matmul(PS[pn][:, 0:n], V[:, j, h * 128:(h + 1) * 128], P[:, 0:n],
                                                  start=(idx == 0), stop=(idx == nk - 1)), reads=[("Vd", j), pk],
                         writes=[("ps", pn)])
                return B

            def mkC(h, t0, n, pn):
                X_, Q_, L_ = xc[0], sqd[0], sl[0]
                xk_, qk2, lk_ = ("xc", 0), ("sqd", 0), ("sl", 0)
                stt = {}

                def P1():
                    k.op("act", lambda e: e.copy(oh[:, t0:t0 + n], PS[pn][:, 0:n]), reads=[("ps", pn)],
                         writes=[("oh", t0)])
                    pm = psG.next()
                    stt["pm"] = pm
                    k.op("pe", lambda e: e.matmul(PS[pm][:, 0:n], self.onesm[:], oh[:, t0:t0 + n], start=True, stop=True),
                         reads=["onesm", ("oh", t0)], writes=[("ps", pm)])

                def P2():
                    pm = stt["pm"]
                    k.op("dve", lambda e: e.tensor_tensor(X_[:, 0:n], oh[:, t0:t0 + n], PS[pm][:, 0:n], ALU.subtract),
                         reads=[("oh", t0), ("ps", pm)], writes=[xk_])
                    k.op("pool", lambda e: e.tensor_tensor(Q_[:, 0:n], X_[:, 0:n], X_[:, 0:n], ALU.mult),
                         reads=[xk_], writes=[qk2])

                def P3():
                    pv = psG.next()
                    k.op("pe", lambda e: e.matmul(PS[pv][:, 0:n], self.onesm[:], Q_[:, 0:n], start=True, stop=True),
                         reads=["onesm", qk2], writes=[("ps", pv)])
                    k.op("act", lambda e: e.activation(Q_[:, 0:n], PS[pv][:, 0:n], AF.Ln, bias=self.eps_col[:, 0:1],
                                                       scale=1.0), reads=[("ps", pv), "eps"], writes=[qk2])
                    k.op("act", lambda e: e.activation(Q_[:, 0:n], Q_[:, 0:n], AF.Exp, scale=-0.5), reads=[qk2],
                         writes=[qk2])
                    k.op("dve", lambda e: e.scalar_tensor_tensor(X_[:, 0:n], X_[:, 0:n], pc[:, R_GN + h:R_GN + h + 1],
                                                                 Q_[:, 0:n], ALU.mult, ALU.mult),
                         reads=[xk_, qk2, ("pcol", l)], writes=[xk_])

                def P4():
                    pg = psS.next()
                    proj(pg, wg, kg_, h * 128, 128, t0, n)
                    k.op("act", lambda e: e.activation(L_[:, 0:n], PS[pg][:, 0:n], AF.Exp, scale=-1.0), reads=[("ps", pg)],
                         writes=[lk_])
                    k.op("act", lambda e: e.copy(Q_[:, 0:n], PS[pg][:, 0:n]), reads=[("ps", pg), xk_], writes=[qk2])
                    k.op("act", lambda e: e.activation(L_[:, 0:n], L_[:, 0:n], AF.Ln, bias=1.0, scale=1.0), reads=[lk_],
                         writes=[lk_])
                    k.op("act", lambda e: e.activation(L_[:, 0:n], L_[:, 0:n], AF.Exp, scale=-1.0), reads=[lk_],
                         writes=[lk_])
                    k.op("pool", lambda e: e.tensor_tensor(L_[:, 0:n], Q_[:, 0:n], L_[:, 0:n], ALU.mult),
                         reads=[lk_, qk2], writes=[lk_])
                    k.op("pool", lambda e: e.tensor_tensor(brT[:, h, t0:t0 + n], X_[:, 0:n], L_[:, 0:n], ALU.mult),
                         reads=[xk_, lk_], more=[("brT", h)])
                return [(1, P1), (4, P2), (7, P3), (10, P4)]

            def withpre(pre, A):
                def f():
                    pre()
                    A()
                return f

            for i in range(4):
                mkmix(0, i)()
            for h in range(4):
                nstep = 0
                for (t0, n) in qgroups:
                    keys = list(range(NT)) if t0 < S else [16, 17]
                    pn = psN.next()
                    for idx, j in enumerate(keys):
                        st = {}
                        A = mkA(h, t0, n, j, st)
                        if h + 1 < 4 and nstep in (2, 8, 14, 20):
                            A = withpre(mkmix(h + 1, (nstep - 2) // 6), A)
                        nstep += 1
                        steps.append((A, mkB(h, t0, n, j, st, idx, len(keys), pn),
                                      mkC(h, t0, n, pn) if idx == len(keys) - 1 else None))
            self.run_pipe(steps, 4)
            for c in range(4):
                k.dma("sp", self.brs[12 + c], brT[:, c, :], reads=[("brT", c)], writes=[("brs", 12 + c)])
            k.barrier()

    def mix_merge(self, l, ms, hT, hTk, last):
        k, I, PS = self.k, self.I, self.PS
        Xt = lambda t: self.X[t * 128:(t + 1) * 128, :]
        groups = [(g * 512, 512) for g in range(4)] + ([] if last else [(2048, 256)])
        tiles = list(range(16)) + ([] if last else [16, 17])
        with ExitStack() as es0:
            mT = self.sb(es0, "mT", [128, 8, T], BF16)
            with ExitStack() as es:
                brT = self.sb(es, "brT", [128, 16, T], BF16)
                wbr = [self.sb(es, "wbr", [128, 16, 128], BF16) for i in range(2)]
                wg = [self.sb(es, "wgm", [128, 8, 512], BF16) for i in range(2)]
                acc = [self.sb(es, "acc", [128, 512]) for i in range(2)]
                sgm = [self.sb(es, "sgm", [128, 512]) for i in range(2)]
                tmp = [self.sb(es, "tmpm", [128, 512]) for i in range(2)]
                for gi_, (t0_, n_) in enumerate([(g * 512, 512) for g in range(4)] + [(2048, 256)]):
                    for j in range(16):
                        k.dma("sp", brT[:, j, t0_:t0_ + n_], self.brs[j][:, t0_:t0_ + n_], reads=[("brs", j)],
                              writes=[("brT", j, gi_)])
                psP = Rot([0, 1, 2, 3])
                psGt = Rot([4, 5, 6, 7])
                ai = 0
                for e_ in range(8):
                    w = wg[e_ % 2]
                    wk = "wgm%d" % (e_ % 2)
                    wb_ = wbr[e_ % 2]
                    wbk = "wbr%d" % (e_ % 2)
                    for c in range(8):
                        k.dma("pool", w[:, c, :].rearrange("p (a x) -> p a x", a=4),
                              I["w_in"][l][c * 128:(c + 1) * 128, O_GT:O_GT + 4 * D].rearrange(
                                  "p (a x) -> p a x", a=4)[:, :, e_ * 128:(e_ + 1) * 128], writes=[(wk, c)])
                    for kk in range(4):
                        k.dma("pool", wb_[:, kk * 4:(kk + 1) * 4, :],
                              I["w_branch"][l, kk].rearrange("(c p) d -> p c d", p=128)[:, :, e_ * 128:(e_ + 1) * 128],
                              writes=[(wbk, kk * 4 + c) for c in range(4)])
                    for (t0, n) in groups:
                        A = acc[ai % 2]
                        ak = ("acc", ai % 2)
                        ai += 1
                        for kk in range(4):
                            pp = psP.next()
                            pg = psGt.next()
                            for c in range(4):
                                k.op("pe", lambda e, c=c: e.matmul(
                                    PS[pp][:, 0:n], wb_[:, kk * 4 + c, :], brT[:, kk * 4 + c, t0:t0 + n],
                                    start=(c == 0), stop=(c == 3)), reads=[(wbk, kk * 4 + c), ("brT", kk * 4 + c, t0 // 512)],
                                    writes=[("ps", pp)])
                            for c in range(8):
                                k.op("pe", lambda e, c=c: e.matmul(
                                    PS[pg][:, 0:n], w[:, c, kk * 128:(kk + 1) * 128], hT[:, c, t0:t0 + n],
                                    start=(c == 0), stop=(c == 7)), reads=[(wk, c)] + hTk[c][t0 // 128:(t0 + n) // 128],
                                    writes=[("ps", pg)])
                            sg_ = sgm[kk % 2]
                            sk_ = ("sgm", kk % 2)
                            k.op("act", lambda e: e.activation(sg_[:, 0:n], PS[pg][:, 0:n], AF.Sigmoid),
                                 reads=[("ps", pg)], writes=[sk_])
                            if kk == 0:
                                k.op("dve", lambda e: e.tensor_tensor(A[:, 0:n], sg_[:, 0:n], PS[pp][:, 0:n], ALU.mult),
                                     reads=[sk_, ("ps", pp)], writes=[ak])
                            else:
                                tm = tmp[kk % 2]
                                tk = ("tmpm", kk % 2)
                                k.op("dve", lambda e: e.tensor_tensor(tm[:, 0:n], sg_[:, 0:n], PS[pp][:, 0:n], ALU.mult),
                                     reads=[sk_, ("ps", pp)], writes=[tk])
                                if kk < 3:
                                    k.op("dve", lambda e: e.tensor_tensor(A[:, 0:n], A[:, 0:n], tm[:, 0:n], ALU.add),
                                         reads=[tk, ak], writes=[ak])
                                else:
                                    k.op("dve", lambda e: e.tensor_tensor(mT[:, e_, t0:t0 + n], A[:, 0:n], tm[:, 0:n],
                                                                           ALU.add),
                                         reads=[tk, ak], **(dict(writes=[("mT", e_)]) if t0 == 0 else dict(more=[("mT", e_)])))
                k.barrier()
            with ExitStack() as es:
                wout = self.sb(es, "wout", [128, 8, D], BF16)
                xin = [self.sb(es, "mxin2", [128, D]) for i in range(2)]
                rr = [self.sb(es, "mrr", [128, D]) for i in range(2)]
                grow = [self.sb(es, "mgrow", [128, D]) for r in range(2)]
                lng = self.sb(es, "mlng", [128, D])
                lnb = self.sb(es, "mlnb", [128, D])
                st = [self.sb(es, "mst", [128, 12]) for i in range(2)]
                mv = [self.sb(es, "mmv", [128, 4]) for i in range(2)]
                self.load_w(wout, "wout", I["w_out"][l], 8, D, colblk=1024)
                for r in range(2):
                    k.dma("sp", grow[r][:], self.modrow[l, r, 5 * D:6 * D].partition_broadcast(128), writes=[("grow", r)])
                k.dma("sp", lng[:], I["ln_g"][l, 1].partition_broadcast(128), writes=["lng"])
                k.dma("sp", lnb[:], I["ln_b"][l, 1].partition_broadcast(128), writes=["lnb"])
                pyr = Rot([0, 1, 2, 3])
                for ti, t in enumerate(tiles):
                    r = 0 if t < 16 else 1
                    xi = xin[ti % 2]
                    xk = ("mxin2", ti % 2)
                    R = rr[ti % 2]
                    rk = ("rr", ti % 2)
                    k.dma("sp", xi[:], Xt(t), reads=[("X", t)], writes=[xk])
                    for dh in range(2):
                        py = pyr.next()
                        for e_ in range(8):
                            k.op("pe", lambda e, e_=e_: e.matmul(
                                PS[py][:, :], mT[:, e_, t * 128:(t + 1) * 128], wout[:, e_, dh * 512:(dh + 1) * 512],
                                start=(e_ == 0), stop=(e_ == 7)), reads=[("mT", e_), ("wout", e_)], writes=[("ps", py)])
                        kw = dict(writes=[rk]) if dh == 0 else dict(more=[rk])
                        k.op("dve", lambda e: e.tensor_tensor(R[:, dh * 512:(dh + 1) * 512], PS[py][:, :],
                                                              grow[r][:, dh * 512:(dh + 1) * 512], ALU.mult),
                             reads=[("ps", py), ("grow", r)], **kw)
                    k.op("dve", lambda e: e.scalar_tensor_tensor(R[:], xi[:], ALPHA, R[:], ALU.mult, ALU.add),
                         reads=[xk, rk], writes=[rk])
                    self.layer_norm(R, rk, lng, lnb, st[ti % 2], mv[ti % 2], ("ln", ti % 2))
                    k.dma("pool", Xt(t), R[:], reads=[rk], writes=[("X", t)])
                    if self.debug:
                        k.dma("sp", self.dbg["x_%d_mx" % l][t * 128:(t + 1) * 128, :], R[:], reads=[rk])
                k.barrier()


def build(debug=False, n_layers=2, stop=None):
    b = Builder(debug=debug, n_layers=n_layers)
    b.stop = stop
    I = b.I
    try:
        _build_body(b, n_layers)
    except StopBuild:
        pass
    b.k.finish()
    b.top.close()
    return b


def _build_body(b, n_layers):
    I = b.I
    b.prologue()
    if b.chk("pro"):
        return
    Xt = lambda t: b.X[t * 128:(t + 1) * 128, :]
    for l in range(n_layers):
        last = (l == 1)
        if l == 0:
            src0 = lambda t: (I["x"][t * 128:(t + 1) * 128, :] if t < 16 else I["ctx"][(t - 16) * 128:(t - 15) * 128, :])
        else:
            src0 = Xt
        b.ffn(l, 0, src0, Xt, 0, 1, 2, 0, list(range(NT)), dbgname="x_%d_f0" % l)
        if b.chk("f0"):
            return
        b.mixer(l, last)
        if b.chk("mixer"):
            return
        if last:
            b.ffn(l, 1, Xt, lambda t: b.out[t * 128:(t + 1) * 128, :], 6, 7, 8, 2, list(range(16)), dbgname="x_%d_f1" % l)
        else:
            b.ffn(l, 1, Xt, Xt, 6, 7, 8, 2, list(range(NT)), dbgname="x_%d_f1" % l)


def _host_constants():
    c = {}
    c["c_ident"] = np.eye(128, dtype=np.float32)
    import ml_dtypes
    jj = np.arange(128)[:, None]
    ii = np.arange(128)[None, :]
    t1 = (jj >= ii).astype(np.float32)
    t2 = (jj <= ii).astype(np.float32)
    c["c_tri"] = np.concatenate([np.tile(t1, (1, 4)), np.tile(t2, (1, 4))], 1).astype(ml_dtypes.bfloat16)
    c["c_dist"] = (np.arange(512)[None, :] - np.arange(128)[:, None]).astype(np.float32)
    c["c_mults"] = np.tile((128.0 * np.arange(18))[None, :], (128, 1)).astype(np.float32)
    dpn = np.zeros((8, 128, 512), np.float32)
    for i in range(4):
        dd = c["c_dist"] - 128.0 * i
        dpn[2 * i] = np.maximum(dd, 0.0)
        dpn[2 * i + 1] = np.maximum(-dd, 0.0)
    c["c_dpn"] = dpn

    def rope(dim):
        da = dim // 2
        nf = da // 2
        inv = 10000.0 ** (-np.arange(nf, dtype=np.float64) / nf)
        tpos = np.arange(S)
        C = np.ones((dim, T), np.float64)
        Sg = np.zeros((dim, T), np.float64)
        perm = np.zeros(dim, np.int64)
        for f in range(dim):
            seg, u = f // da, f % da
            pos = (tpos // 64) if seg == 0 else (tpos % 64)
            ang = (pos.astype(np.float32) * np.float32(inv[u % nf])).astype(np.float64)
            C[f, :S] = np.cos(ang)
            if u < nf:
                Sg[f, :S] = -np.sin(ang)
                perm[f] = f + nf
            else:
                Sg[f, :S] = np.sin(ang)
                perm[f] = f - nf
        return np.stack([C, Sg]).astype(np.float32), perm

    rb, pb = rope(64)
    rcm, pcm = rope(32)
    rd, pd = rope(128)
    c["rope_b"] = rb
    rc96 = np.zeros((2, 96, T), np.float32)
    rc96[0] = 1.0
    rc96[:, 64:96] = rcm
    c["rope_c"] = rc96
    c["rope_d"] = rd
    return c, pb, pcm, pd


def host_inputs(inputs):
    consts, pb, pcm, pd = _host_constants()
    f = lambda a: np.ascontiguousarray(np.asarray(a, dtype=np.float32))
    w_in = f(inputs["w_in"])
    cols = []
    for h in range(8):
        cols += list(O_BQ + h * 64 + pb)
    for h in range(2):
        cols += list(O_BK + h * 64 + pb)
    cols += list(O_CKR + pcm)
    for h in range(4):
        cols += list(O_DQ + h * 128 + pd)
    for h in range(4):
        cols += list(O_DK + h * 128 + pd)
    cols = np.asarray(cols)
    assert cols.shape[0] == NSW
    w_sw = np.ascontiguousarray(w_in[:, :, cols])
    w_uq = f(inputs["mla_w_uq"])
    w_uq_sw = np.zeros_like(w_uq)
    for h in range(8):
        w_uq_sw[:, :, h * 96 + 64:h * 96 + 96] = w_uq[:, :, h * 96 + 64 + pcm]
    prow = np.zeros((2, 64, 128), np.float32)
    for l in range(2):
        prow[l, R_CW:R_CW + 16] = f(inputs["lru_conv_w"])[l].reshape(16, 128)
        prow[l, R_CB:R_CB + 4] = f(inputs["lru_conv_b"])[l].reshape(4, 128)
        prow[l, R_BA:R_BA + 8] = f(inputs["lru_b_a"])[l].reshape(8, 128)
        prow[l, R_BX:R_BX + 8] = f(inputs["lru_b_x"])[l].reshape(8, 128)
        prow[l, R_LAM:R_LAM + 8] = f(inputs["lru_lambda"])[l].reshape(8, 128)
        prow[l, R_QN:R_QN + 3] = f(inputs["mla_q_norm"])[l].reshape(3, 128)
        prow[l, R_KVN:R_KVN + 2] = f(inputs["mla_kv_norm"])[l].reshape(2, 128)
        prow[l, R_GN:R_GN + 4] = f(inputs["ret_gn_g"])[l].reshape(4, 128)
    shared = dict(consts)
    shared.update({
        "w_mod": f(inputs["w_mod"]), "b_mod": f(inputs["b_mod"]).reshape(2, 1, 9 * D),
        "ln_g": f(inputs["ln_g"]), "ln_b": f(inputs["ln_b"]),
        "ffn_w_in": f(inputs["ffn_w_in"]), "ffn_w_out": f(inputs["ffn_w_out"]),
        "w_in": w_in, "w_sw": w_sw, "prow": prow,
        "lru_w_a": f(inputs["lru_w_a"]), "lru_w_x": f(inputs["lru_w_x"]),
        "gqa_sink": f(inputs["gqa_sink"]), "ret_decay": f(inputs["ret_decay"]).reshape(2, 8),
        "mla_w_uq": w_uq, "mla_w_uq_sw": w_uq_sw, "mla_w_ukv": f(inputs["mla_w_ukv"]),
        "w_branch": f(inputs["w_branch"]), "w_out": f(inputs["w_out"]),
    })
    x = f(inputs["x"]); ctx = f(inputs["ctx"]); c = f(inputs["c"]); c_ctx = f(inputs["c_ctx"])
    maps = []
    for b in range(x.shape[0]):
        m = dict(shared)
        m["x"] = x[b]
        m["ctx"] = ctx[b]
        m["cc"] = np.concatenate([c[b].reshape(8, 128), c_ctx.reshape(8, 128)], 0)
        maps.append(m)
    return maps


def kernel(**inputs):
    maps = host_inputs(inputs)
    b = build()
    res = run_bass_kernel_spmd(b.nc, maps, core_ids=list(range(8)))
    return np.stack([np.asarray(r["out"], dtype=np.float32) for r in res.results], 0)
```

```python
import numpy as np
import concourse.bass as bass
import concourse.mybir as mybir
from concourse.bass_utils import run_bass_kernel_spmd
from contextlib import ExitStack

F32 = mybir.dt.float32
BF16 = mybir.dt.bfloat16
AF = mybir.ActivationFunctionType
ALU = mybir.AluOpType

LV = 9
ENGS = ("pe", "act", "dve", "pool", "sp")
NDS = 12

D = 1024
S = 2048
LC = 256
T = S + LC
NT = T // 128
FF = 2816
NFC = FF // 128
LN_EPS = 1e-5
ALPHA = 4.0 ** 0.25
O_AX, O_AY, O_BQ, O_BK, O_BV, O_CQ, O_CKV, O_CKR, O_DQ, O_DK, O_DV, O_DG, O_GT = (
    0, 512, 1024, 1536, 1664, 1792, 2176, 2432, 2464, 2976, 3488, 4000, 4512)
W_BQ, W_BK, W_CKR, W_DQ, W_DK = 0, 512, 640, 672, 1184
NSW = 1696
R_CW, R_CB, R_BA, R_BX, R_LAM, R_QN, R_KVN, R_GN = 0, 16, 20, 28, 36, 44, 47, 49


class KB:
    def __init__(self, nc, same_engine_sync=True):
        self.nc = nc
        self.es = ExitStack()
        self.eng = {"pe": nc.tensor, "act": nc.scalar, "dve": nc.vector,
                    "pool": nc.gpsimd, "sp": nc.sync}
        self.sems = {}
        self.cnt = {}
        for e in ENGS:
            self.sems[e] = self.es.enter_context(nc.semaphore("s_" + e))
            self.cnt[e] = 0
        self.dq = {}
        for q in ("sp", "pool"):
            for i in range(NDS):
                k = ("d", q, i)
                self.sems[k] = self.es.enter_context(nc.semaphore("d_%s_%d" % (q, i)))
                self.cnt[k] = 0
            self.dq[q] = 0
        self.lastw = {}
        self.basew = {}
        self.readers = {}
        self.waited = {e: {} for e in ENGS}
        self.same = same_engine_sync
        self.n_ins = {e: 0 for e in ENGS}

    def _deps(self, e, reads, writes, more=()):
        ev = []
        for r in reads:
            ev.extend(self.lastw.get(r, ()))
        for w_ in writes:
            ev.extend(self.lastw.get(w_, ()))
            ev.extend(self.readers.get(w_, {}).values())
        for w_ in more:
            bw = self.basew.get(w_)
            if bw is not None:
                ev.append(bw)
            ev.extend(self.readers.get(w_, {}).values())
        need = {}
        wd = self.waited[e]
        for (sk, v) in ev:
            if sk == e and (e == "pe" or not self.same):
                continue
            if wd.get(sk, 0) >= v:
                continue
            if need.get(sk, 0) < v:
                need[sk] = v
        for sk, v in need.items():
            self.eng[e].wait_ge(self.sems[sk], v)
            wd[sk] = v

    def _commit(self, event, reads, writes, more, rkey):
        for w_ in writes:
            self.lastw[w_] = [event]
            self.basew[w_] = event
            self.readers[w_] = {}
        for w_ in more:
            self.lastw.setdefault(w_, []).append(event)
        for r in reads:
            self.readers.setdefault(r, {})[rkey] = event

    def op(self, e, fn, reads=(), writes=(), more=()):
        self._deps(e, reads, writes, more)
        ins = fn(self.eng[e])
        self.cnt[e] += 1
        ins.then_inc(self.sems[e], 1)
        self.n_ins[e] += 1
        self._commit((e, self.cnt[e]), reads, writes, more, e)
        return ins

    def dma(self, q, out, in_, reads=(), writes=(), more=(), **kw):
        slot = self.dq[q] % NDS
        self.dq[q] += 1
        sk = ("d", q, slot)
        if self.cnt[sk] > 0 and self.waited[q].get(sk, 0) < self.cnt[sk]:
            self.eng[q].wait_ge(self.sems[sk], self.cnt[sk])
            self.waited[q][sk] = self.cnt[sk]
        self._deps(q, reads, writes, more)
        ins = self.eng[q].dma_start(out=out, in_=in_, **kw)
        self.cnt[sk] += 16
        ins.then_inc(self.sems[sk], 16)
        self.n_ins[q] += 1
        self._commit((sk, self.cnt[sk]), reads, writes, more, sk)
        return ins

    def barrier(self):
        for e in ENGS:
            for sk, v in self.cnt.items():
                if v == 0 or sk == e:
                    continue
                if self.waited[e].get(sk, 0) >= v:
                    continue
                self.eng[e].wait_ge(self.sems[sk], v)
                self.waited[e][sk] = v
        self.lastw.clear()
        self.basew.clear()
        self.readers.clear()

    def finish(self):
        for sk, v in self.cnt.items():
            if v == 0 or sk == "sp":
                continue
            if self.waited["sp"].get(sk, 0) >= v:
                continue
            self.eng["sp"].wait_ge(self.sems[sk], v)
            self.waited["sp"][sk] = v
        self.es.close()


class StopBuild(Exception):
    pass


class Rot:
    def __init__(self, items):
        self.items = list(items)
        self.i = 0

    def next(self):
        x = self.items[self.i % len(self.items)]
        self.i += 1
        return x


class Builder:
    def __init__(self, debug=False, n_layers=2, stages=None):
        self.debug = debug
        self.n_layers = n_layers
        self.stages = stages
        nc = bass.Bass("TRN2", target_bir_lowering=False)
        self.nc = nc
        self.k = KB(nc)
        self.top = ExitStack()
        self.uid = 0
        I = {}

        def inp(name, shape, dt=F32):
            I[name] = nc.dram_tensor(name, list(shape), dt, kind="ExternalInput").ap()

        inp("x", [S, D]); inp("ctx", [LC, D]); inp("cc", [16, 128])
        inp("w_mod", [2, D, 9 * D]); inp("b_mod", [2, 1, 9 * D])
        inp("ln_g", [2, 3, D]); inp("ln_b", [2, 3, D])
        inp("ffn_w_in", [2, 2, D, 2 * FF]); inp("ffn_w_out", [2, 2, FF, D])
        inp("w_in", [2, D, 8608]); inp("w_sw", [2, D, NSW])
        inp("prow", [2, 64, 128])
        inp("lru_w_a", [2, 2, 8, 64, 64]); inp("lru_w_x", [2, 2, 8, 64, 64])
        inp("gqa_sink", [2, 8]); inp("ret_decay", [2, 8])
        inp("mla_w_uq", [2, 384, 768]); inp("mla_w_uq_sw", [2, 384, 768]); inp("mla_w_ukv", [2, 256, 1024])
        inp("w_branch", [2, 4, 512, D]); inp("w_out", [2, D, D])
        inp("c_ident", [128, 128]); inp("c_tri", [128, 1024], BF16)
        inp("c_dist", [128, 512]); inp("c_mults", [128, 18]); inp("c_dpn", [8, 128, 512])
        inp("rope_b", [2, 64, T]); inp("rope_c", [2, 96, T]); inp("rope_d", [2, 128, T])
        self.I = I
        self.out = nc.dram_tensor("out", [S, D], F32, kind="ExternalOutput").ap()
        self.X = nc.dram_tensor("xs", [T, D], F32, kind="Internal").ap()
        self.modrow = nc.dram_tensor("modrow", [2, 2, 9 * D], F32, kind="Internal").ap()
        self.brs = nc.dram_tensor("brs", [16, 128, T], BF16, kind="Internal").ap()
        self.dbg = {}
        if debug:
            for l in range(n_layers):
                for s in ("f0", "mx", "f1"):
                    self.dbg["x_%d_%s" % (l, s)] = nc.dram_tensor(
                        "dbg_x_%d_%s" % (l, s), [T, D], F32, kind="ExternalOutput").ap()
                self.dbg["br_%d" % l] = nc.dram_tensor(
                    "dbg_br_%d" % l, [16, 128, T], BF16, kind="ExternalOutput").ap()

    def run_pipe(self, steps, sk, sk2=None):
        n = len(steps)
        postq = []
        for i in range(n + sk):
            if i < n:
                steps[i][0]()
            if i >= sk:
                st = steps[i - sk]
                st[1]()
                if len(st) > 2 and st[2]:
                    for (dl, fn) in st[2]:
                        postq.append((i + dl, fn))
            if postq and postq[0][0] <= i:
                postq.pop(0)[1]()
        for (_, fn) in postq:
            fn()

    def chk(self, name):
        if getattr(self, "stop", None) == name:
            self.stopped = True
        return getattr(self, "stopped", False)

    def sb(self, es, name, shape, dt=F32):
        self.uid += 1
        return es.enter_context(self.nc.sbuf_tensor("%s_%d" % (name, self.uid), list(shape), dt))

    def load_w(self, dst, key, src2d, nchunks, ncols, colblk=2048, q="pool", dcol0=0):
        k = self.k
        for c in range(nchunks):
            first = True
            for c0 in range(0, ncols, colblk):
                w = min(colblk, ncols - c0)
                kw = dict(writes=[(key, c)]) if first else dict(more=[(key, c)])
                k.dma(q, dst[:, c, dcol0 + c0:dcol0 + c0 + w], src2d[c * 128:(c + 1) * 128, c0:c0 + w], **kw)
                first = False

    def layer_norm(self, rr, rkey, lng, lnb, st, mv, tag):
        k = self.k
        for h in range(2):
            k.op("dve", lambda e, h=h: e.bn_stats(st[:, h * 6:(h + 1) * 6], rr[:, h * 512:(h + 1) * 512]),
                 reads=[rkey], writes=[(tag, "st", h)])
        k.op("dve", lambda e: e.bn_aggr(mv[:, 0:2], st[:, 0:12]), reads=[(tag, "st", 0), (tag, "st", 1)],
             writes=[(tag, "mv")])
        k.op("act", lambda e: e.activation(mv[:, 2:3], mv[:, 1:2], AF.Sqrt, bias=self.eps_col[:, 0:1], scale=1.0),
             reads=[(tag, "mv")], writes=[(tag, "sd")])
        k.op("dve", lambda e: e.reciprocal(mv[:, 3:4], mv[:, 2:3]), reads=[(tag, "sd")], writes=[(tag, "rs")])
        k.op("dve", lambda e: e.tensor_scalar(rr[:], rr[:], mv[:, 0:1], mv[:, 3:4], ALU.subtract, ALU.mult),
             reads=[rkey, (tag, "mv"), (tag, "rs")], writes=[rkey])
        k.op("pool", lambda e: e.tensor_tensor(rr[:], rr[:], lng[:], ALU.mult), reads=[rkey, "lng"], writes=[rkey])
        k.op("pool", lambda e: e.tensor_tensor(rr[:], rr[:], lnb[:], ALU.add), reads=[rkey, "lnb"], writes=[rkey])

    def prologue(self):
        k, nc, I = self.k, self.nc, self.I
        es = self.top
        self.PSALL = es.enter_context(nc.psum_tensor("psall", [128, 4096], F32))
        self.PS = [self.PSALL[:, i * 512:(i + 1) * 512] for i in range(8)]
        self.ident = self.sb(es, "ident", [128, 128])
        self.ones_bf = self.sb(es, "ones_bf", [128, 64], BF16)
        self.ones32 = self.sb(es, "ones32", [128, 128])
        self.onesm = self.sb(es, "onesm", [128, 128])
        self.eps_col = self.sb(es, "eps_col", [128, 1])
        self.modT = [[self.sb(es, "modT", [128, 72]) for r in range(2)] for l in range(2)]
        self.onep = [[self.sb(es, "onep", [128, 72]) for r in range(2)] for l in range(2)]
        self.pcol = [self.sb(es, "pcol", [128, 64]) for l in range(2)]
        self.s2 = self.sb(es, "s2", [128, 8, 2])
        k.dma("sp", self.ident[:], I["c_ident"], writes=["ident"])
        k.op("dve", lambda e: e.memset(self.ones_bf[:], 1.0), writes=["ones_bf"])
        k.op("dve", lambda e: e.memset(self.ones32[:], 1.0), writes=["ones32"])
        k.op("dve", lambda e: e.memset(self.onesm[:], 1.0 / 128.0), writes=["onesm"])
        k.op("dve", lambda e: e.memset(self.eps_col[:], LN_EPS), writes=["eps"])
        with ExitStack() as ls:
            cc = self.sb(ls, "cc", [16, 128])
            sT = self.sb(ls, "sT", [128, 16])
            s2 = self.s2
            brow = self.sb(ls, "brow", [2, 9 * D])
            mrow = self.sb(ls, "mrow", [2, 9 * D])
            wm = [self.sb(ls, "wm", [128, 8, 512]) for i in range(2)]
            m72 = self.sb(ls, "m72", [72, 128])
            pr = self.sb(ls, "pr", [64, 128])
            k.dma("sp", cc[:], I["cc"], writes=["cc"])
            k.op("pe", lambda e: e.transpose(self.PS[0][:, 0:16], cc[:], self.ident[0:16, 0:16]),
                 reads=["cc", "ident"], writes=[("ps", 0)])
            k.op("act", lambda e: e.activation(sT[:], self.PS[0][:, 0:16], AF.Silu), reads=[("ps", 0)], writes=["sT"])
            for r in range(2):
                k.op("dve", lambda e, r=r: e.tensor_copy(s2[:, :, r], sT[:, r * 8:(r + 1) * 8]), reads=["sT"],
                     writes=[("s2", r)])
            psr = Rot([1, 2])
            for l in range(self.n_layers):
                if l == 1:
                    k.dma("sp", pr[:], I["prow"][l], writes=["pr"])
                    k.op("pe", lambda e: e.transpose(self.PS[4][:, 0:64], pr[:], self.ident[0:64, 0:64]),
                         reads=["pr", "ident"], writes=[("ps", 4)])
                    k.op("dve", lambda e, l=l: e.tensor_copy(self.pcol[l][:], self.PS[4][:, 0:64]), reads=[("ps", 4)],
                         writes=[("pcol", l)])
                    continue
                for r in range(2):
                    k.dma("sp", brow[r:r + 1, :], I["b_mod"][l], writes=[("brow", r)])
                for blk in range(18):
                    w = wm[blk % 2]
                    wk = ("wm", blk % 2)
                    for c in range(8):
                        kw = dict(writes=[wk]) if c == 0 else dict(more=[wk])
                        k.dma("sp", w[:, c, :], I["w_mod"][l, c * 128:(c + 1) * 128, blk * 512:(blk + 1) * 512], **kw)
                    pi = psr.next()
                    for c in range(8):
                        k.op("pe", lambda e, c=c, w=w, pi=pi: e.matmul(self.PS[pi][0:2, :], s2[:, c, :], w[:, c, :],
                                                                        start=(c == 0), stop=(c == 7)),
                             reads=[wk, ("s2", 0), ("s2", 1)], writes=[("ps", pi)])
                    k.op("dve", lambda e, pi=pi, blk=blk: e.tensor_tensor(
                        mrow[:, blk * 512:(blk + 1) * 512], self.PS[pi][0:2, :], brow[:, blk * 512:(blk + 1) * 512],
                        ALU.add), reads=[("ps", pi), ("brow", 0), ("brow", 1)], writes=[("mrow", blk)])
                k.dma("sp", self.modrow[l], mrow[:], reads=[("mrow", b) for b in range(18)], writes=[("modrow", l)])
                for r in range(2):
                    k.dma("sp", m72[:], self.modrow[l, r].rearrange("(j p) -> j p", p=128), reads=[("modrow", l)],
                          writes=["m72"])
                    k.op("pe", lambda e: e.transpose(self.PS[3][:, 0:72], m72[:], self.ident[0:72, 0:72]),
                         reads=["m72", "ident"], writes=[("ps", 3)])
                    k.op("dve", lambda e, l=l, r=r: e.tensor_copy(self.modT[l][r][:], self.PS[3][:, 0:72]),
                         reads=[("ps", 3)], writes=[("modT", l, r)])
                    k.op("dve", lambda e, l=l, r=r: e.tensor_scalar_add(self.onep[l][r][:], self.modT[l][r][:], 1.0),
                         reads=[("modT", l, r)], writes=[("onep", l, r)])
                k.dma("sp", pr[:], I["prow"][l], writes=["pr"])
                k.op("pe", lambda e: e.transpose(self.PS[4][:, 0:64], pr[:], self.ident[0:64, 0:64]),
                     reads=["pr", "ident"], writes=[("ps", 4)])
                k.op("dve", lambda e, l=l: e.tensor_copy(self.pcol[l][:], self.PS[4][:, 0:64]), reads=[("ps", 4)],
                     writes=[("pcol", l)])
            k.barrier()

    def mod_tasks(self, es, l):
        k, I, PS = self.k, self.I, self.PS
        wm = [self.sb(es, "wm2", [128, 8, 256]) for i in range(2)]
        bb = [self.sb(es, "bb2", [2, 256]) for i in range(2)]
        mb = [self.sb(es, "mb2", [2, 256]) for i in range(2)]
        m72 = self.sb(es, "m72b", [72, 128])
        psr = Rot([4, 5])
        tasks = []
        for blk in range(36):
            def task(blk=blk):
                w, wk = wm[blk % 2], ("wm2", blk % 2)
                for c in range(8):
                    kw = dict(writes=[wk]) if c == 0 else dict(more=[wk])
                    k.dma("sp", w[:, c, :], I["w_mod"][l, c * 128:(c + 1) * 128, blk * 256:(blk + 1) * 256], **kw)
                for r in range(2):
                    kw = dict(writes=[("bb2", blk % 2)]) if r == 0 else dict(more=[("bb2", blk % 2)])
                    k.dma("sp", bb[blk % 2][r:r + 1, :], I["b_mod"][l][:, blk * 256:(blk + 1) * 256], **kw)
                pi = psr.next()
                for c in range(8):
                    k.op("pe", lambda e, c=c: e.matmul(PS[pi][0:2, 0:256], self.s2[:, c, :], w[:, c, :],
                                                       start=(c == 0), stop=(c == 7)), reads=[wk], writes=[("ps", pi)])
                k.op("dve", lambda e: e.tensor_tensor(mb[blk % 2][:], PS[pi][0:2, 0:256], bb[blk % 2][:], ALU.add),
                     reads=[("ps", pi), ("bb2", blk % 2)], writes=[("mb2", blk % 2)])
                k.dma("pool", self.modrow[l][:, blk * 256:(blk + 1) * 256], mb[blk % 2][:], reads=[("mb2", blk % 2)],
                      writes=[("modrowb", blk)])
            tasks.append(task)

        def finish():
            for r in range(2):
                k.dma("sp", m72[:], self.modrow[l, r].rearrange("(j p) -> j p", p=128),
                      reads=[("modrowb", b) for b in range(36)], writes=["m72b"])
                k.op("pe", lambda e: e.transpose(PS[6][:, 0:72], m72[:], self.ident[0:72, 0:72]), reads=["m72b"],
                     writes=[("ps", 6)])
                k.op("dve", lambda e: e.tensor_copy(self.modT[l][r][:], PS[6][:, 0:72]), reads=[("ps", 6)],
                     writes=[("modT", l, r)])
                k.op("dve", lambda e: e.tensor_scalar_add(self.onep[l][r][:], self.modT[l][r][:], 1.0),
                     reads=[("modT", l, r)], writes=[("onep", l, r)])
        return tasks, finish

    def ffn(self, l, f, src, dst, ksh, ksc, kg, lni, tiles, dbgname=None):
        k, nc, I = self.k, self.nc, self.I
        PS = self.PS
        with ExitStack() as es:
            w1b = self.sb(es, "w1b", [128, 8, 2 * FF], BF16)
            w2b = self.sb(es, "w2b", [128, NFC, D], BF16)
            xin = self.sb(es, "xin", [128, 3, D])
            hmT = self.sb(es, "hmT", [128, 8, 384], BF16)
            gT = self.sb(es, "gT", [128, NFC, 384], BF16)
            sg = [self.sb(es, "sg", [128, 384]) for i in range(2)]
            rr = [self.sb(es, "rr", [128, D]) for i in range(2)]
            grow = [self.sb(es, "grow", [128, D]) for r in range(2)]
            lng = self.sb(es, "lng", [128, D])
            lnb = self.sb(es, "lnb", [128, D])
            st = [self.sb(es, "st", [128, 12]) for i in range(2)]
            mv = [self.sb(es, "mv", [128, 4]) for i in range(2)]
            for blk in (0, 2, 1, 3):
                for c in range(8):
                    k.dma("pool", w1b[:, c, blk * 1408:(blk + 1) * 1408],
                          I["ffn_w_in"][l, f][c * 128:(c + 1) * 128, blk * 1408:(blk + 1) * 1408],
                          writes=[("w1b", c, blk)])
                if blk == 2:
                    self.load_w(w2b, "w2b", I["ffn_w_out"][l, f], NFC, D, colblk=1024)
            for r in range(2):
                k.dma("sp", grow[r][:], self.modrow[l, r, kg * D:(kg + 1) * D].partition_broadcast(128),
                      writes=[("grow", r)])
                k.op("pool", lambda e, r=r: e.tensor_scalar_mul(grow[r][:], grow[r][:], 0.5), reads=[("grow", r)],
                     writes=[("grow", r)])
            k.dma("sp", lng[:], I["ln_g"][l, lni].partition_broadcast(128), writes=["lng"])
            k.dma("sp", lnb[:], I["ln_b"][l, lni].partition_broadcast(128), writes=["lnb"])
            pst = Rot([0, 1])
            pab = Rot([2, 3, 4, 5])
            pyr = Rot([6, 7])
            groups = [tiles[i:i + 3] for i in range(0, len(tiles), 3)]
            ti = 0
            for grp in groups:
                n = 128 * len(grp)
                for tt, t in enumerate(grp):
                    r = 0 if t < 16 else 1
                    k.dma("sp", xin[:, tt, :], src(t), writes=[("xin", tt)])
                    for half in range(2):
                        pi = pst.next()
                        for cc_ in range(4):
                            c = half * 4 + cc_
                            k.op("pe", lambda e, c=c, cc_=cc_, tt=tt, pi=pi: e.transpose(
                                PS[pi][:, cc_ * 128:(cc_ + 1) * 128], xin[:, tt, c * 128:(c + 1) * 128], self.ident[:]),
                                reads=[("xin", tt), "ident"], writes=[("ps", pi)])
                        for cc_ in range(4):
                            c = half * 4 + cc_
                            k.op("act", lambda e, c=c, cc_=cc_, tt=tt, pi=pi, r=r: e.activation(
                                hmT[:, c, tt * 128:(tt + 1) * 128], PS[pi][:, cc_ * 128:(cc_ + 1) * 128], AF.Identity,
                                bias=self.modT[l][r][:, ksh * 8 + c:ksh * 8 + c + 1],
                                scale=self.onep[l][r][:, ksc * 8 + c:ksc * 8 + c + 1]),
                                reads=[("ps", pi), ("modT", l, r), ("onep", l, r)], writes=[("hmT", c, tt)])
                hk = [[("hmT", c, tt) for tt in range(len(grp))] for c in range(8)]
                for fc in range(NFC):
                    pa = pab.next()
                    pb = pab.next()
                    for c in range(8):
                        k.op("pe", lambda e, c=c, fc=fc, pa=pa: e.matmul(
                            PS[pa][:, 0:n], w1b[:, c, fc * 128:(fc + 1) * 128], hmT[:, c, 0:n],
                            start=(c == 0), stop=(c == 7)), reads=[("w1b", c, fc // 11)] + hk[c], writes=[("ps", pa)])
                    for c in range(8):
                        k.op("pe", lambda e, c=c, fc=fc, pb=pb: e.matmul(
                            PS[pb][:, 0:n], w1b[:, c, FF + fc * 128:FF + (fc + 1) * 128], hmT[:, c, 0:n],
                            start=(c == 0), stop=(c == 7)), reads=[("w1b", c, 2 + fc // 11)] + hk[c], writes=[("ps", pb)])
                    s = sg[fc % 2]
                    k.op("act", lambda e, s=s, pa=pa: e.activation(s[:, 0:n], PS[pa][:, 0:n], AF.Silu),
                         reads=[("ps", pa)], writes=[("sg", fc % 2)])
                    k.op("dve", lambda e, s=s, pb=pb, fc=fc: e.tensor_tensor(gT[:, fc, 0:n], s[:, 0:n], PS[pb][:, 0:n],
                                                                             ALU.mult),
                         reads=[("sg", fc % 2), ("ps", pb)], writes=[("gT", fc)])
                for tt, t in enumerate(grp):
                    r = 0 if t < 16 else 1
                    ri = ti % 2
                    ti += 1
                    R = rr[ri]
                    rk = ("rr", ri)
                    for dh in range(2):
                        py = pyr.next()
                        for fc in range(NFC):
                            k.op("pe", lambda e, fc=fc, tt=tt, dh=dh, py=py: e.matmul(
                                PS[py][:, :], gT[:, fc, tt * 128:(tt + 1) * 128], w2b[:, fc, dh * 512:(dh + 1) * 512],
                                start=(fc == 0), stop=(fc == NFC - 1)), reads=[("gT", fc), ("w2b", fc)],
                                writes=[("ps", py)])
                        kw = dict(writes=[rk]) if dh == 0 else dict(more=[rk])
                        k.op("dve", lambda e, R=R, dh=dh, py=py, r=r: e.tensor_tensor(
                            R[:, dh * 512:(dh + 1) * 512], PS[py][:, :], grow[r][:, dh * 512:(dh + 1) * 512], ALU.mult),
                            reads=[("ps", py), ("grow", r)], **kw)
                    k.op("dve", lambda e, R=R, tt=tt: e.scalar_tensor_tensor(
                        R[:], xin[:, tt, :], ALPHA, R[:], ALU.mult, ALU.add), reads=[("xin", tt), rk], writes=[rk])
                    self.layer_norm(R, rk, lng, lnb, st[ri], mv[ri], ("ln", ri))
                    k.dma("pool", dst(t), R[:], reads=[rk], writes=[("X", t)])
                    if self.debug and dbgname:
                        k.dma("sp", self.dbg[dbgname][t * 128:(t + 1) * 128, :], R[:], reads=[rk])
            k.barrier()

    def mixer(self, l, last):
        k, nc, I = self.k, self.nc, self.I
        PS = self.PS
        Xt = lambda t: self.X[t * 128:(t + 1) * 128, :]
        pc = self.pcol[l]
        qgroups = [(g * 512, 512) for g in range(4)] + ([] if last else [(2048, 256)])
        allgroups = [(g * 512, 512) for g in range(4)] + [(2048, 256)]
        with ExitStack() as ms:
            hT = self.sb(ms, "hT", [128, 8, T], BF16)
            brT = None
            with ExitStack() as es:
                xin = [self.sb(es, "mxin", [128, D]) for i in range(3)]
                pst = Rot([0, 1, 2, 3])
                for t in range(NT):
                    r = 0 if t < 16 else 1
                    xi = xin[t % 3]
                    xk = ("mxin", t % 3)
                    k.dma("sp", xi[:], Xt(t), reads=[("X", t)], writes=[xk])
                    for half in range(2):
                        pi = pst.next()
                        for cc_ in range(4):
                            c = half * 4 + cc_
                            k.op("pe", lambda e, c=c, cc_=cc_, xi=xi, pi=pi: e.transpose(
                                PS[pi][:, cc_ * 128:(cc_ + 1) * 128], xi[:, c * 128:(c + 1) * 128], self.ident[:]),
                                reads=[xk, "ident"], writes=[("ps", pi)])
                        for cc_ in range(4):
                            c = half * 4 + cc_
                            k.op("act", lambda e, c=c, cc_=cc_, t=t, pi=pi, r=r: e.activation(
                                hT[:, c, t * 128:(t + 1) * 128], PS[pi][:, cc_ * 128:(cc_ + 1) * 128], AF.Identity,
                                bias=self.modT[l][r][:, 3 * 8 + c:3 * 8 + c + 1],
                                scale=self.onep[l][r][:, 4 * 8 + c:4 * 8 + c + 1]),
                                reads=[("ps", pi), ("modT", l, r), ("onep", l, r)], writes=[("hT", c, t)])
                k.barrier()
            hTk = [[("hT", c, t) for t in range(NT)] for c in range(8)]

            def proj(pi, wt, wkey, col0, ncols, t0, n, prow0=0):
                for c in range(8):
                    k.op("pe", lambda e, c=c: e.matmul(PS[pi][prow0:prow0 + ncols, 0:n], wt[:, c, col0:col0 + ncols],
                                                       hT[:, c, t0:t0 + n], start=(c == 0), stop=(c == 7)),
                         reads=[(wkey, c)] + hTk[c][t0 // 128:(t0 + n) // 128], writes=[("ps", pi)])

            if self.chk("hT"):
                return
            self.mix_lru(l, ms, hT, brT, proj, pc, allgroups)
            if self.chk("lru"):
                for j in range(16):
                    k.dma("sp", self.dbg["br_%d" % l][j], self.brs[j], reads=[("brs", j)])
                return
            self.mix_gqa(l, ms, hT, brT, proj, hTk, qgroups, last)
            if self.chk("gqa"):
                for j in range(16):
                    k.dma("sp", self.dbg["br_%d" % l][j], self.brs[j], reads=[("brs", j)])
                return
            self.mix_mla(l, ms, hT, brT, proj, hTk, pc, qgroups, allgroups, last)
            if getattr(self, "stopped", False):
                return
            if self.chk("mla"):
                for j in range(16):
                    k.dma("sp", self.dbg["br_%d" % l][j], self.brs[j], reads=[("brs", j)])
                return
            self.mix_ret(l, ms, hT, brT, proj, hTk, pc, qgroups, allgroups, last)
            if self.chk("ret"):
                for j in range(16):
                    k.dma("sp", self.dbg["br_%d" % l][j], self.brs[j], reads=[("brs", j)])
                return
            if self.debug:
                for j in range(16):
                    k.dma("sp", self.dbg["br_%d" % l][j], self.brs[j], reads=[("brs", j)])
            self.mix_merge(l, ms, hT, hTk, last)

    def mix_lru(self, l, ms, hT, brT, proj, pc, allgroups):
        k, I, PS = self.k, self.I, self.PS
        with ExitStack() as es:
            brT = self.sb(es, "brP", [128, 4, T], BF16)
            wA = self.sb(es, "wA", [128, 8, 1024], BF16)
            wbd = self.sb(es, "wbd", [128, 2, 2, 4, 128], BF16)
            spc = self.sb(es, "spc", [128, 8, 3])
            axp = [self.sb(es, "axp", [128, 2310])] * 2
            xa = [self.sb(es, "xa", [128, T]) for i in range(2)]
            xab = [self.sb(es, "xab", [128, T], BF16)] * 2
            rb = [self.sb(es, "rb", [128, T]) for i in range(2)]
            ib = [self.sb(es, "ib", [128, T]) for i in range(2)]
            tb = [self.sb(es, "tb", [128, T]) for i in range(2)]
            hf = self.sb(es, "hf", [128, T])
            hb = self.sb(es, "hb", [128, T])
            gl = [self.sb(es, "gl", [128, 512]) for i in range(2)]
            mtasks, mfinish = ([], None)
            if l == 0 and self.n_layers == 2:
                mtasks, mfinish = self.mod_tasks(es, 1)
            self.load_w(wA, "wA", I["w_in"][l][:, O_AX:O_AX + 1024], 8, 1024, colblk=1024)
            k.op("pool", lambda e: e.memset(wbd[:], 0.0), writes=["wbd"])
            for g, nm in enumerate(("lru_w_a", "lru_w_x")):
                for d in range(2):
                    for nb in range(8):
                        p0 = (nb % 2) * 64
                        k.dma("pool", wbd[p0:p0 + 64, g, d, nb // 2, p0:p0 + 64], I[nm][l, d, nb], more=["wbd"])
            lam = pc[:, R_LAM:R_LAM + 8]
            k.op("act", lambda e: e.activation(spc[:, :, 0], lam, AF.Exp, scale=-1.0), reads=[("pcol", l)],
                 writes=["spc0"])
            k.op("act", lambda e: e.activation(spc[:, :, 0], spc[:, :, 0], AF.Ln, bias=1.0, scale=1.0), reads=["spc0"],
                 writes=["spc0"])
            k.op("dve", lambda e: e.tensor_scalar_mul(spc[:, :, 1], spc[:, :, 0], -8.0), reads=["spc0"],
                 writes=["spc1"])
            k.op("dve", lambda e: e.tensor_scalar_mul(spc[:, :, 2], spc[:, :, 0], -16.0), reads=["spc0"],
                 writes=["spc2"])
            k.op("dve", lambda e: e.memset(axp[0][:], 0.0), writes=[("axp", 0)])
            pr_ = Rot([0, 1, 2, 3])
            mt = iter(mtasks)

            def some_mod(n):
                for _ in range(n):
                    t = next(mt, None)
                    if t is not None:
                        t()

            def s1(c):
                X, XA, XB = axp[c % 2], xa[c % 2], xab[c % 2]
                kx, ka, kb = ("axp", 0), ("xa", c % 2), ("xab", 0)
                for (t0, n) in allgroups:
                    pi = pr_.next()
                    proj(pi, wA, "wA", c * 128, 128, t0, n)
                    o0 = 2 + t0 if t0 < S else 2053 + (t0 - S)
                    k.op("act", lambda e: e.copy(X[:, o0:o0 + n], PS[pi][:, 0:n]), reads=[("ps", pi)], more=[kx])
                for (o_in, o_out, n) in ((0, 0, S), (2051, S, LC)):
                    k.op("dve", lambda e: e.tensor_scalar(
                        XA[:, o_out:o_out + n], X[:, o_in:o_in + n], pc[:, R_CW + c:R_CW + c + 1],
                        pc[:, R_CB + c:R_CB + c + 1], ALU.mult, ALU.add), reads=[kx, ("pcol", l)],
                        **(dict(writes=[ka]) if o_out == 0 else dict(more=[ka])))
                for j in range(1, 4):
                    for (o_in, o_out, n) in ((0, 0, S), (2051, S, LC)):
                        k.op("dve", lambda e: e.scalar_tensor_tensor(
                            XA[:, o_out:o_out + n], X[:, o_in + j:o_in + j + n],
                            pc[:, R_CW + j * 4 + c:R_CW + j * 4 + c + 1], XA[:, o_out:o_out + n], ALU.mult, ALU.add),
                            reads=[kx, ka, ("pcol", l)], writes=[ka])
                k.op("pool", lambda e: e.tensor_copy(XB[:], XA[:]), reads=[ka], writes=[kb])

            def s2(c):
                XA, XB = xa[c % 2], xab[c % 2]
                ka, kb = ("xa", c % 2), ("xab", 0)
                for d in range(2):
                    for g, (buf, bkey, brow_) in enumerate(((rb[d], ("rb", d), R_BA), (ib[d], ("ib", d), R_BX))):
                        first = True
                        for (t0, n) in allgroups:
                            pi = pr_.next()
                            k.op("pe", lambda e: e.matmul(PS[pi][:, 0:n], wbd[:, g, d, c, :], XB[:, t0:t0 + n],
                                                          start=True, stop=True), reads=["wbd", kb], writes=[("ps", pi)])
                            k.op("act", lambda e: e.activation(
                                buf[:, t0:t0 + n], PS[pi][:, 0:n], AF.Sigmoid,
                                bias=pc[:, brow_ + d * 4 + c:brow_ + d * 4 + c + 1], scale=1.0),
                                reads=[("ps", pi), ("pcol", l)], **(dict(writes=[bkey]) if first else dict(more=[bkey])))
                            first = False
                for d in range(2):
                    dc = d * 4 + c
                    k.op("act", lambda e: e.activation(tb[d][:], rb[d][:], AF.Exp, scale=spc[:, dc, 2:3]),
                         reads=[("rb", d), "spc2"], writes=[("tb", d)])
                    k.op("act", lambda e: e.activation(rb[d][:], rb[d][:], AF.Exp, scale=spc[:, dc, 1:2]),
                         reads=[("rb", d), "spc1"], writes=[("rb", d)])
                for d in range(2):
                    k.op("act", lambda e: e.activation(tb[d][:], tb[d][:], AF.Sqrt, bias=1.0, scale=-1.0),
                         reads=[("tb", d)], writes=[("tb", d)])
                for d in range(2):
                    k.op("pool", lambda e: e.tensor_tensor(ib[d][:], ib[d][:], XA[:], ALU.mult), reads=[("ib", d), ka],
                         writes=[("ib", d)])
                    k.op("pool", lambda e: e.tensor_tensor(tb[d][:], tb[d][:], ib[d][:], ALU.mult),
                         reads=[("ib", d), ("tb", d)], writes=[("tb", d)])

            def s3(c):
                k.op("dve", lambda e: e.tensor_tensor_scan(hf[:, S:T], rb[0][:, S:T], tb[0][:, S:T], 0.0, ALU.mult,
                                                           ALU.add), reads=[("rb", 0), ("tb", 0)], writes=["hf"])
                k.op("dve", lambda e: e.tensor_tensor_scan(hf[:, 0:S], rb[0][:, 0:S], tb[0][:, 0:S], hf[:, T - 1:T],
                                                           ALU.mult, ALU.add), reads=[("rb", 0), ("tb", 0), "hf"],
                     writes=["hf"])
                k.op("dve", lambda e: e.tensor_tensor_scan(hb[:, ::-1], rb[1][:, ::-1], tb[1][:, ::-1], 0.0, ALU.mult,
                                                           ALU.add), reads=[("rb", 1), ("tb", 1)], writes=["hb"])
                k.op("pool", lambda e: e.tensor_tensor(hf[:], hf[:], hb[:], ALU.add), reads=["hf", "hb"], writes=["hf"])
                for gi, (t0, n) in enumerate(allgroups):
                    pi = pr_.next()
                    proj(pi, wA, "wA", 512 + c * 128, 128, t0, n)
                    G_, gk = gl[gi % 2], ("gl", gi % 2)
                    k.op("act", lambda e: e.activation(G_[:, 0:n], PS[pi][:, 0:n], AF.Gelu_apprx_tanh),
                         reads=[("ps", pi)], writes=[gk])
                    k.op("dve", lambda e: e.tensor_tensor(brT[:, c, t0:t0 + n], G_[:, 0:n], hf[:, t0:t0 + n], ALU.mult),
                         reads=[gk, "hf"], **(dict(writes=[("brT", c)]) if t0 == 0 else dict(more=[("brT", c)])))

            s1(0)
            for c in range(4):
                some_mod(3)
                s2(c)
                some_mod(3)
                if c + 1 < 4:
                    s1(c + 1)
                some_mod(3)
                s3(c)
            some_mod(99)
            if mfinish is not None:
                mfinish()
            for c in range(4):
                k.dma("sp", self.brs[c], brT[:, c, :], reads=[("brT", c)], writes=[("brs", c)])
            k.barrier()

    def rope_evac(self, out_ap, pa, pb, r0, r1, n, C, Sg, t0, t1b, t2b, okw, rs=None):
        k, PS = self.k, self.PS
        k.op("dve", lambda e: e.tensor_tensor(t1b[r0:r1, 0:n], PS[pa][r0:r1, 0:n], C[r0:r1, t0:t0 + n], ALU.mult),
             reads=[("ps", pa), "ropeC"], writes=["rt1"])
        k.op("dve", lambda e: e.tensor_tensor(t2b[r0:r1, 0:n], PS[pb][r0:r1, 0:n], Sg[r0:r1, t0:t0 + n], ALU.mult),
             reads=[("ps", pb), "ropeS"], writes=["rt2"])
        if rs is None:
            k.op("pool", lambda e: e.tensor_tensor(out_ap, t1b[r0:r1, 0:n], t2b[r0:r1, 0:n], ALU.add),
                 reads=["rt1", "rt2"], **okw)
        else:
            k.op("pool", lambda e: e.tensor_tensor(t1b[r0:r1, 0:n], t1b[r0:r1, 0:n], t2b[r0:r1, 0:n], ALU.add),
                 reads=["rt1", "rt2"], writes=["rt1"])
            k.op("pool", lambda e: e.tensor_tensor(out_ap, t1b[r0:r1, 0:n], rs[r0:r1, t0:t0 + n], ALU.mult),
                 reads=["rt1", "rsq"], **okw)

    def mix_gqa(self, l, ms, hT, brT, proj, hTk, qgroups, last):
        k, I, PS = self.k, self.I, self.PS
        with ExitStack() as es:
            brT = self.sb(es, "brP", [128, 4, T], BF16)
            wB = self.sb(es, "wB", [128, 8, 1408], BF16)
            qT = self.sb(es, "qT", [128, 8, T], BF16)
            kT = self.sb(es, "kT", [128, 2, T], BF16)
            V = self.sb(es, "V", [128, NT, 2, 128], BF16)
            sinkl = self.sb(es, "sinkl", [1, 128])
            C = self.sb(es, "ropeC", [128, T])
            Sg = self.sb(es, "ropeS", [128, T])
            t1b = [self.sb(es, "t1b", [128, 512]) for i in range(2)]
            t2b = [self.sb(es, "t2b", [128, 512]) for i in range(2)]
            tri = self.sb(es, "tri", [128, 1024], BF16)
            E = [self.sb(es, "E", [128, 512], BF16) for i in range(4)]
            sink = self.sb(es, "sink", [1, 8])
            sinkrow = self.sb(es, "sinkrow", [1, 8, 128])
            rc = [self.sb(es, "rc", [64, 512]) for i in range(2)]
            self.load_w(wB, "wB", I["w_in"][l][:, O_BQ:O_BQ + 512], 8, 512, colblk=512, dcol0=0)
            self.load_w(wB, "wB1", I["w_sw"][l][:, W_BQ:W_BQ + 512], 8, 512, colblk=512, dcol0=512)
            self.load_w(wB, "wB2", I["w_in"][l][:, O_BK:O_BK + 128], 8, 128, colblk=512, dcol0=1024)
            self.load_w(wB, "wB3", I["w_sw"][l][:, W_BK:W_BK + 128], 8, 128, colblk=512, dcol0=1152)
            self.load_w(wB, "wB4", I["w_in"][l][:, O_BV:O_BV + 128], 8, 128, colblk=512, dcol0=1280)
            k.dma("sp", C[0:64, :], I["rope_b"][0], writes=["ropeC"])
            k.dma("sp", Sg[0:64, :], I["rope_b"][1], writes=["ropeS"])
            k.dma("sp", C[64:128, :], I["rope_b"][0], more=["ropeC"])
            k.dma("sp", Sg[64:128, :], I["rope_b"][1], more=["ropeS"])
            k.dma("sp", tri[:], I["c_tri"], writes=["tri"])
            k.dma("sp", sink[:], I["gqa_sink"][l:l + 1, :], writes=["sink"])
            k.op("act", lambda e: e.activation(sink[:], sink[:], AF.Exp), reads=["sink"], writes=["sink"])
            k.op("dve", lambda e: e.memset(sinkrow[:], 1.0), writes=["sinkrow"])
            k.op("dve", lambda e: e.memset(sinkl[:], 0.0), writes=["sinkl"])
            k.op("dve", lambda e: e.memset(sinkl[0:1, 64:128], 1.0), writes=["sinkl"])
            k.op("pool", lambda e: e.memset(qT[64:128, :, :], 0.0), writes=["qTz"])
            k.op("pool", lambda e: e.memset(kT[64:128, :, :], 0.0), writes=["kTz"])
            k.op("pool", lambda e: e.memset(V[:], 1.0), writes=["Vones"])
            for h in range(8):
                k.op("dve", lambda e, h=h: e.tensor_scalar_mul(sinkrow[:, h, :], sinkrow[:, h, :], sink[0:1, h:h + 1]),
                     reads=["sink", "sinkrow"], writes=["sinkrow"])
            pr_ = Rot([0, 1, 2, 3])
            allg = [(g * 512, 512) for g in range(4)] + [(2048, 256)]
            ri = 0
            for (dst, dk, wk0, c0, wk1, c1, npair) in ((qT, "qT", "wB", 0, "wB1", 512, 4), (kT, "kT", "wB2", 1024, "wB3", 1152, 1)):
                for hp in range(npair):
                    for (t0, n) in allg:
                        pa, pb = pr_.next(), pr_.next()
                        proj(pa, wB, wk0, c0 + hp * 128, 128, t0, n)
                        proj(pb, wB, wk1, c1 + hp * 128, 128, t0, n)
                        t1, t2 = t1b[ri % 2], t2b[ri % 2]
                        k1, k2 = ("rt1", ri % 2), ("rt2", ri % 2)
                        ri += 1
                        k.op("dve", lambda e: e.tensor_tensor(t1[:, 0:n], PS[pa][:, 0:n], C[:, t0:t0 + n], ALU.mult),
                             reads=[("ps", pa), "ropeC"], writes=[k1])
                        k.op("dve", lambda e: e.tensor_tensor(t2[:, 0:n], PS[pb][:, 0:n], Sg[:, t0:t0 + n], ALU.mult),
                             reads=[("ps", pb), "ropeS"], writes=[k2])
                        h0, h1 = 2 * hp, 2 * hp + 1
                        k.op("pool", lambda e: e.tensor_tensor(dst[0:64, h0, t0:t0 + n], t1[0:64, 0:n], t2[0:64, 0:n],
                                                               ALU.add), reads=[k1, k2],
                             **(dict(writes=[(dk, h0)]) if t0 == 0 else dict(more=[(dk, h0)])))
                        k.op("dve", lambda e: e.tensor_tensor(dst[0:64, h1, t0:t0 + n], t1[64:128, 0:n], t2[64:128, 0:n],
                                                              ALU.add), reads=[k1, k2],
                             **(dict(writes=[(dk, h1)]) if t0 == 0 else dict(more=[(dk, h1)])))
            for t in range(NT):
                pi = pr_.next()
                for c in range(8):
                    k.op("pe", lambda e, c=c, t=t, pi=pi: e.matmul(PS[pi][:, 0:128], hT[:, c, t * 128:(t + 1) * 128],
                                                                   wB[:, c, 1280:1408], start=(c == 0), stop=(c == 7)),
                         reads=[("wB4", c), hTk[c][t]], writes=[("ps", pi)])
                k.op("act", lambda e, t=t, pi=pi: e.copy(V[:, t, :, 0:64],
                                                         PS[pi][:, 0:128].rearrange("p (a b) -> p a b", a=2)),
                     reads=[("ps", pi), "Vones"], writes=[("V", t)])
            psS = Rot([0, 1, 2, 3])
            psN = Rot([4, 5, 6, 7])
            eR = Rot([0, 1, 2, 3])
            qtiles = list(range(16)) + ([] if last else [16, 17])
            steps = []

            def mkA(j, m, g, nq, st):
                def A():
                    pS = psS.next()
                    ei = eR.next()
                    Eb, ek = E[ei], ("E", ei)
                    st["Eb"], st["ek"] = Eb, ek
                    k.op("pe", lambda e: e.matmul(
                        PS[pS][:, :].rearrange("p (a b) -> p a b", a=4), kT[:, g, j * 128:(j + 1) * 128],
                        qT[:, 4 * g:4 * g + 4, nq * 128:(nq + 1) * 128], start=True, stop=True),
                        reads=[("kT", g), "qTz", "kTz"] + [("qT", 4 * g + i) for i in range(4)], writes=[("ps", pS)])
                    k.op("act", lambda e: e.activation(Eb[:], PS[pS][:, :], AF.Exp, scale=0.125),
                         reads=[("ps", pS)], writes=[ek])
                    if m:
                        k.op("pool", lambda e: e.tensor_tensor(Eb[:], Eb[:], tri[:, (m - 1) * 512:m * 512], ALU.mult),
                             reads=[ek, "tri"], writes=[ek])
                return A

            def mkB(j, g, nq, st, idx, nk, pn):
                def B():
                    Eb, ek = st["Eb"], st["ek"]
                    k.op("pe", lambda e: e.matmul(PS[pn][:, :], V[:, j, g, :], Eb[:], start=(idx == 0), stop=False),
                         reads=[("V", j), ek], writes=[("ps", pn)])
                    if idx != nk - 1:
                        return
                    k.op("pe", lambda e: e.matmul(
                        PS[pn][:, :].rearrange("p (a b) -> p a b", a=4), sinkl[0:1, :],
                        sinkrow[0:1, 4 * g:4 * g + 4, :], start=False, stop=True),
                        reads=["sinkl", "sinkrow"], writes=[("ps", pn)])
                    R = rc[g]
                    k.op("dve", lambda e: e.reciprocal(R[:], PS[pn][64:128, :]), reads=[("ps", pn)], writes=[("rc", g)])
                    for i in range(4):
                        h = 4 * g + i
                        p0 = (h % 2) * 64
                        k.op("dve", lambda e, i=i, h=h, p0=p0: e.tensor_tensor(
                            brT[p0:p0 + 64, h // 2, nq * 128:(nq + 1) * 128], PS[pn][0:64, i * 128:(i + 1) * 128],
                            R[:, i * 128:(i + 1) * 128], ALU.mult), reads=[("ps", pn), ("rc", g)],
                            more=[("brT", h // 2)])
                return B

            for nq in qtiles:
                if nq < 16:
                    keys = [(j, m) for (j, m) in ((nq - 1, 1), (nq, 0), (nq + 1, 2)) if 0 <= j < 16] + [(16, 0), (17, 0)]
                else:
                    keys = [(16, 0), (17, 0)]
                for g in range(2):
                    pn = psN.next()
                    for idx, (j, m) in enumerate(keys):
                        st = {}
                        steps.append((mkA(j, m, g, nq, st), mkB(j, g, nq, st, idx, len(keys), pn)))
            self.run_pipe(steps, 3)
            for c in range(4):
                k.dma("sp", self.brs[4 + c], brT[:, c, :], reads=[("brT", c)], writes=[("brs", 4 + c)])
            k.barrier()

    def mix_mla(self, l, ms, hT, brT, proj, hTk, pc, qgroups, allgroups, last):
        k, I, PS = self.k, self.I, self.PS
        SC = 96.0 ** -0.5
        with ExitStack() as es:
            brT = self.sb(es, "brP", [128, 4, T], BF16)
            wuq = self.sb(es, "wuq", [128, 3, 768], BF16)
            wuqs = self.sb(es, "wuqs", [128, 3, 768], BF16)
            wukv = self.sb(es, "wukv", [128, 2, 1024], BF16)
            cqT = self.sb(es, "cqT", [128, 3, T], BF16)
            ckvT = self.sb(es, "ckvT", [128, 2, T], BF16)
            krT = self.sb(es, "krT", [96, T], BF16)
            V = self.sb(es, "Vc", [128, NT, 8, 128], BF16)
            C = self.sb(es, "ropeCc", [96, T])
            Sg = self.sb(es, "ropeSc", [96, T])
            t1b = self.sb(es, "t1c", [96, 512])
            t2b = self.sb(es, "t2c", [96, 512])
            p1 = ExitStack()
            wC = self.sb(p1, "wC", [128, 8, 800], BF16)
            self.load_w(wC, "wC", I["w_in"][l][:, O_CQ:O_CQ + 672], 8, 672, colblk=672, dcol0=0)
            self.load_w(wC, "wCs", I["w_sw"][l][:, W_CKR - 96:W_CKR + 32], 8, 128, colblk=128, dcol0=672)
            k.dma("sp", C[64:96, :], I["rope_c"][0, 64:96], writes=["ropeC"])
            k.dma("sp", Sg[64:96, :], I["rope_c"][1, 64:96], writes=["ropeS"])
            k.op("pool", lambda e: e.memset(V[:], 1.0), writes=["Vones"])
            self.load_w(wuq, "wuq", I["mla_w_uq"][l], 3, 768, colblk=768)
            self.load_w(wuqs, "wuqs", I["mla_w_uq_sw"][l], 3, 768, colblk=768)
            self.load_w(wukv, "wukv", I["mla_w_ukv"][l], 2, 1024, colblk=1024)
            sq = [self.sb(p1, "sq", [128, 3, 512]) for i in range(2)]
            raw = [self.sb(p1, "raw", [128, 3, 512]) for i in range(2)]
            rsg = [self.sb(p1, "rsg", [128, 512]) for i in range(2)]
            pr_ = Rot([0, 1, 2, 3])
            pssR = Rot([4, 5])
            c1steps = []
            si = 0
            for (dstT, dkey, col0, nch, dim, rg) in ((cqT, "cqT", 0, 3, 384.0, R_QN), (ckvT, "ckvT", 384, 2, 256.0, R_KVN)):
                for (t0, n) in allgroups:
                    bi = si % 2
                    si += 1

                    def A(dstT=dstT, dkey=dkey, col0=col0, nch=nch, t0=t0, n=n, bi=bi):
                        for c in range(nch):
                            pi = pr_.next()
                            proj(pi, wC, "wC", col0 + c * 128, 128, t0, n)
                            k.op("dve", lambda e: e.tensor_copy(raw[bi][:, c, 0:n], PS[pi][:, 0:n]),
                                 reads=[("ps", pi)], writes=[("raw", bi, c)])
                            k.op("pool", lambda e: e.tensor_tensor(sq[bi][:, c, 0:n], raw[bi][:, c, 0:n],
                                                                   raw[bi][:, c, 0:n], ALU.mult),
                                 reads=[("raw", bi, c)], writes=[("sq", bi, c)])

                    def B(dstT=dstT, dkey=dkey, nch=nch, dim=dim, rg=rg, t0=t0, n=n, bi=bi):
                        pss = pssR.next()
                        for c in range(nch):
                            k.op("pe", lambda e: e.matmul(PS[pss][:, 0:n], self.ones32[:], sq[bi][:, c, 0:n],
                                                          start=(c == 0), stop=(c == nch - 1)),
                                 reads=["ones32", ("sq", bi, c)], writes=[("ps", pss)])
                        k.op("act", lambda e: e.activation(rsg[bi][:, 0:n], PS[pss][:, 0:n], AF.Ln,
                                                           bias=self.eps_col[:, 0:1], scale=1.0 / dim),
                             reads=[("ps", pss), "eps"], writes=[("rsg", bi)])
                        k.op("act", lambda e: e.activation(rsg[bi][:, 0:n], rsg[bi][:, 0:n], AF.Exp, scale=-0.5),
                             reads=[("rsg", bi)], writes=[("rsg", bi)])
                        for c in range(nch):
                            k.op("dve", lambda e: e.scalar_tensor_tensor(
                                dstT[:, c, t0:t0 + n], raw[bi][:, c, 0:n], pc[:, rg + c:rg + c + 1], rsg[bi][:, 0:n],
                                ALU.mult, ALU.mult), reads=[("raw", bi, c), ("rsg", bi), ("pcol", l)],
                                **(dict(writes=[(dkey, c)]) if t0 == 0 else dict(more=[(dkey, c)])))
                    c1steps.append((A, B))
            self.run_pipe(c1steps, 1)
            for (t0, n) in allgroups:
                pa, pb = pr_.next(), pr_.next()
                proj(pa, wC, "wC", 576, 96, t0, n)
                for c in range(8):
                    k.op("pe", lambda e, c=c: e.matmul(PS[pb][0:96, 0:n], wC[:, c, 704:800], hT[:, c, t0:t0 + n],
                                                       start=(c == 0), stop=(c == 7)),
                         reads=[("wCs", c)] + hTk[c][t0 // 128:(t0 + n) // 128], writes=[("ps", pb)])
                self.rope_evac(krT[64:96, t0:t0 + n], pa, pb, 64, 96, n, C, Sg, t0, t1b, t2b,
                               dict(writes=["krT"]) if t0 == 0 else dict(more=["krT"]))
            for t in range(NT):
                pi = pr_.next()
                for c in range(2):
                    k.op("pe", lambda e, c=c, t=t, pi=pi: e.matmul(
                        PS[pi][:, :].rearrange("p (a b) -> p a b", a=8), ckvT[:, c, t * 128:(t + 1) * 128],
                        wukv[:, c, :].rearrange("p (h x) -> p h x", x=128)[:, :, 64:128], start=(c == 0), stop=(c == 1)),
                        reads=[("ckvT", c), ("wukv", c)], writes=[("ps", pi)])
                k.op("act", lambda e, t=t, pi=pi: e.copy(V[:, t, :, 0:64],
                                                         PS[pi][:, :].rearrange("p (a b) -> p a b", a=8)),
                     reads=[("ps", pi), "Vones"], writes=[("Vc", t)])
            k.barrier()
            p1.close()
            if self.chk("mla_v"):
                return
            qTh = [self.sb(es, "qTh", [96, T], BF16) for i in range(2)]
            kTh = [self.sb(es, "kTh", [96, T], BF16) for i in range(2)]
            E = [self.sb(es, "Ec", [128, 1024], BF16) for i in range(4)]
            rc = [self.sb(es, "rcc", [64, 512]) for i in range(2)]
            psS = Rot([0, 2, 4])
            psN = Rot([6, 7])
            eR = Rot([0, 1, 2, 3])
            steps = []

            def mkA(jp, t0, n, qh, kh, qk_, kk_, st):
                def A():
                    p0 = psS.next()
                    ei = eR.next()
                    Eb, ek = E[ei], ("Ec", ei)
                    st["Eb"], st["ek"] = Eb, ek
                    for u, j in enumerate(jp):
                        k.op("pe", lambda e: e.matmul(PS[p0 + u][:, 0:n], kh[0:96, j * 128:(j + 1) * 128],
                                                      qh[0:96, t0:t0 + n], start=True, stop=True), reads=[qk_, kk_],
                             writes=[("ps", p0 + u)])
                    src = self.PSALL[:, p0 * 512:(p0 + 2) * 512].rearrange("p (a b) -> p a b", a=2)[:, :, 0:n]
                    dst = Eb[:].rearrange("p (a b) -> p a b", a=2)[:, :, 0:n]
                    k.op("act", lambda e: e.activation(dst, src, AF.Exp, scale=SC),
                         reads=[("ps", p0), ("ps", p0 + 1)], writes=[ek])
                return A

            def mkB(jp, t0, n, h, st, first, lastp, pn):
                def B():
                    Eb, ek = st["Eb"], st["ek"]
                    for u, j in enumerate(jp):
                        k.op("pe", lambda e: e.matmul(PS[pn][:, 0:n], V[:, j, h, :], Eb[:, u * 512:u * 512 + n],
                                                      start=(first and u == 0), stop=(lastp and u == 1)),
                             reads=[("Vc", j), ek], writes=[("ps", pn)])
                    if not lastp:
                        return
                    R = rc[h % 2]
                    k.op("dve", lambda e: e.reciprocal(R[:, 0:n], PS[pn][64:128, 0:n]), reads=[("ps", pn)],
                         writes=[("rcc", h % 2)])
                    p0 = (h % 2) * 64
                    k.op("dve", lambda e: e.tensor_tensor(brT[p0:p0 + 64, h // 2, t0:t0 + n], PS[pn][0:64, 0:n], R[:, 0:n],
                                                          ALU.mult), reads=[("ps", pn), ("rcc", h % 2)],
                         more=[("brT", h // 2)])
                return B

            def prep(h):
                qh, kh = qTh[h % 2], kTh[h % 2]
                qk_, kk_ = ("qTh", h % 2), ("kTh", h % 2)
                for gi, (t0, n) in enumerate(allgroups):
                    kwq = dict(writes=[qk_]) if gi == 0 else dict(more=[qk_])
                    kwk = dict(writes=[kk_]) if gi == 0 else dict(more=[kk_])
                    pi = pr_.next()
                    for c in range(2):
                        k.op("pe", lambda e, c=c: e.matmul(PS[pi][:, 0:n], wukv[:, c, h * 128:(h + 1) * 128],
                                                           ckvT[:, c, t0:t0 + n], start=(c == 0), stop=(c == 1)),
                             reads=[("wukv", c), ("ckvT", c)], writes=[("ps", pi)])
                    k.op("dve", lambda e: e.tensor_copy(kh[0:64, t0:t0 + n], PS[pi][0:64, 0:n]),
                         reads=[("ps", pi)], **kwk)
                    k.op("pool", lambda e: e.tensor_copy(kh[64:96, t0:t0 + n], krT[64:96, t0:t0 + n]), reads=["krT"],
                         more=[kk_])
                    pa, pb = pr_.next(), pr_.next()
                    for c in range(3):
                        k.op("pe", lambda e, c=c: e.matmul(PS[pa][0:96, 0:n], wuq[:, c, h * 96:(h + 1) * 96],
                                                           cqT[:, c, t0:t0 + n], start=(c == 0), stop=(c == 2)),
                             reads=[("wuq", c), ("cqT", c)], writes=[("ps", pa)])
                    for c in range(3):
                        k.op("pe", lambda e, c=c: e.matmul(PS[pb][0:96, 0:n], wuqs[:, c, h * 96:(h + 1) * 96],
                                                           cqT[:, c, t0:t0 + n], start=(c == 0), stop=(c == 2)),
                             reads=[("wuqs", c), ("cqT", c)], writes=[("ps", pb)])
                    k.op("act", lambda e: e.copy(qh[0:64, t0:t0 + n], PS[pa][0:64, 0:n]), reads=[("ps", pa)], **kwq)
                    self.rope_evac(qh[64:96, t0:t0 + n], pa, pb, 64, 96, n, C, Sg, t0, t1b, t2b, dict(more=[qk_]))

            def withpre(pre, A):
                def f():
                    pre()
                    A()
                return f

            prep(0)
            for h in range(8):
                qh, kh = qTh[h % 2], kTh[h % 2]
                qk_, kk_ = ("qTh", h % 2), ("kTh", h % 2)
                firstA = True
                for (t0, n) in qgroups:
                    keys = list(range(NT)) if t0 < S else [16, 17]
                    pn = psN.next()
                    pairs = [keys[i:i + 2] for i in range(0, len(keys), 2)]
                    for idx, jp in enumerate(pairs):
                        st = {}
                        A = mkA(jp, t0, n, qh, kh, qk_, kk_, st)
                        if firstA and h + 1 < 8:
                            A = withpre(lambda h=h: prep(h + 1), A)
                        firstA = False
                        steps.append((A, mkB(jp, t0, n, h, st, idx == 0, idx == len(pairs) - 1, pn)))
            self.run_pipe(steps, 2)
            for c in range(4):
                k.dma("sp", self.brs[8 + c], brT[:, c, :], reads=[("brT", c)], writes=[("brs", 8 + c)])
            k.barrier()

    def mix_ret(self, l, ms, hT, brT, proj, hTk, pc, qgroups, allgroups, last):
        k, I, PS = self.k, self.I, self.PS
        LNS = float(np.log(128.0 ** -0.5))
        with ExitStack() as es:
            brT = self.sb(es, "brP", [128, 4, T], BF16)
            wt = [self.sb(es, "wD", [128, 8, 512], BF16) for i in range(3)]
            qT = self.sb(es, "qTd", [128, 4, T], BF16)
            kT = self.sb(es, "kTd", [128, 4, T], BF16)
            V = self.sb(es, "Vd", [128, NT, 512], BF16)
            dist = self.sb(es, "dist", [128, 512])
            mults = self.sb(es, "mults", [128, 18])
            dec = self.sb(es, "dec", [128, 8])
            lng_ = self.sb(es, "lngm", [128, 8])
            nlng = self.sb(es, "nlng", [128, 8])
            tab = self.sb(es, "tab", [128, 8, 18])
            p1 = ExitStack()
            C = self.sb(p1, "ropeCd", [128, T])
            Sg = self.sb(p1, "ropeSd", [128, T])
            t1b = self.sb(p1, "t1d", [128, 512])
            t2b = self.sb(p1, "t2d", [128, 512])
            k.dma("sp", C[:], I["rope_d"][0], writes=["ropeC"])
            k.dma("sp", Sg[:], I["rope_d"][1], writes=["ropeS"])
            k.dma("sp", dist[:], I["c_dist"], writes=["dist"])
            k.dma("sp", mults[:], I["c_mults"], writes=["mults"])
            k.dma("sp", dec[:], I["ret_decay"][l].partition_broadcast(128), writes=["dec"])
            k.op("act", lambda e: e.activation(nlng[:], dec[:], AF.Exp, scale=-1.0), reads=["dec"], writes=["nlng"])
            k.op("act", lambda e: e.activation(nlng[:], nlng[:], AF.Ln, bias=1.0, scale=1.0), reads=["nlng"],
                 writes=["nlng"])
            k.op("dve", lambda e: e.tensor_scalar_mul(lng_[:], nlng[:], -1.0), reads=["nlng"], writes=["lngm"])
            for j in range(8):
                k.op("dve", lambda e, j=j: e.tensor_scalar(tab[:, j, :], mults[:], lng_[:, j:j + 1], LNS, ALU.mult,
                                                           ALU.add), reads=["mults", "lngm"],
                     **(dict(writes=["tab"]) if j == 0 else dict(more=["tab"])))
            pr_ = Rot([0, 1, 2, 3])
            wi = 0

            def getw(src, ncols=512):
                nonlocal wi
                w = wt[wi % 3]
                key = "wD%d" % (wi % 3)
                wi += 1
                self.load_w(w, key, src, 8, ncols, colblk=512)
                return w, key

            for (dstT, dkey, o_w, o_s) in ((qT, "qTd", O_DQ, W_DQ), (kT, "kTd", O_DK, W_DK)):
                w0, k0 = getw(I["w_in"][l][:, o_w:o_w + 512])
                w1, k1 = getw(I["w_sw"][l][:, o_s:o_s + 512])
                for h in range(4):
                    for (t0, n) in allgroups:
                        pa, pb = pr_.next(), pr_.next()
                        proj(pa, w0, k0, h * 128, 128, t0, n)
                        proj(pb, w1, k1, h * 128, 128, t0, n)
                        self.rope_evac(dstT[:, h, t0:t0 + n], pa, pb, 0, 128, n, C, Sg, t0, t1b, t2b,
                                       dict(writes=[(dkey, h)]) if t0 == 0 else dict(more=[(dkey, h)]))
            wv, kv_ = getw(I["w_in"][l][:, O_DV:O_DV + 512])
            for t in range(NT):
                pi = pr_.next()
                for c in range(8):
                    k.op("pe", lambda e, c=c, t=t, pi=pi: e.matmul(PS[pi][:, :], hT[:, c, t * 128:(t + 1) * 128],
                                                                   wv[:, c, :], start=(c == 0), stop=(c == 7)),
                         reads=[(kv_, c), hTk[c][t]], writes=[("ps", pi)])
                k.op("act", lambda e, t=t, pi=pi: e.copy(V[:, t, :], PS[pi][:, :]), reads=[("ps", pi)],
                     writes=[("Vd", t)])
            wg, kg_ = getw(I["w_in"][l][:, O_DG:O_DG + 512])
            k.barrier()
            p1.close()
            Wm = [self.sb(es, "Wm", [128, 512]) for i in range(4)]
            W2 = [self.sb(es, "W2", [128, 512]) for i in range(2)]
            W3 = self.sb(es, "W3", [128, 512])
            Wmix = [self.sb(es, "Wmix", [128, 4, 512]) for i in range(2)]
            dpn = self.sb(es, "dpn", [128, 8, 512])
            for i in range(8):
                k.dma("sp", dpn[:, i, :], I["c_dpn"][i], writes=[("dpn", i)])
            Pb = [self.sb(es, "Pb", [128, 512], BF16) for i in range(6)]
            oh = self.sb(es, "oh", [128, T])
            xc = [self.sb(es, "xc", [128, 512]) for i in range(1)]
            sqd = [self.sb(es, "sqd", [128, 512]) for i in range(1)]
            sl = [self.sb(es, "sl", [128, 512]) for i in range(1)]
            psS = Rot([0, 1, 2, 3])
            psN = Rot([4, 5])
            psG = Rot([6, 7])
            wR = Rot([0, 1, 2, 3])
            pR = Rot([0, 1, 2, 3, 4, 5])
            steps = []

            def mkmix(h, i):
                def f():
                    A_, B_ = W2[0], W2[1]
                    k.op("dve", lambda e: e.tensor_scalar_mul(B_[:], dpn[:, 2 * i + 1, :], lng_[:, 4 + h:5 + h]),
                         reads=[("dpn", 2 * i + 1), "lngm"], writes=["W2b"])
                    k.op("dve", lambda e: e.scalar_tensor_tensor(A_[:], dpn[:, 2 * i, :], lng_[:, h:h + 1], B_[:], ALU.mult,
                                                                 ALU.add), reads=[("dpn", 2 * i), "W2b", "lngm"],
                         writes=["W2a"])
                    k.op("act", lambda e: e.activation(Wmix[h % 2][:, i, :], A_[:], AF.Exp, bias=tab[:, h, 0:1],
                                                       scale=1.0), reads=["W2a", "tab"],
                         **(dict(writes=[("Wmix", h % 2)]) if i == 0 else dict(more=[("Wmix", h % 2)])))
                return f

            def mkA(h, t0, n, j, st):
                def A():
                    f_sc, b_sc = lng_[:, h:h + 1], nlng[:, 4 + h:5 + h]
                    Wsrc = None
                    if (t0 < S) == (j < 16):
                        o = (4 * (t0 // 512) - j) if t0 < S else -(j - 16)
                        if -4 < o < 1:
                            Wsrc, wk = Wmix[h % 2][:, -o, 0:n], ("Wmix", h % 2)
                    if Wsrc is None:
                        wi_ = wR.next()
                        W, wk = Wm[wi_], ("Wm", wi_)
                        Wsrc = W[:, 0:n]
                        if (t0 < S) == (j < 16):
                            if o >= 1:
                                k.op("act", lambda e: e.activation(W[:, 0:n], dist[:, 0:n], AF.Exp, scale=f_sc,
                                                                   bias=tab[:, h, o:o + 1]),
                                     reads=["dist", "lngm", "tab"], writes=[wk])
                            else:
                                k.op("act", lambda e: e.activation(W[:, 0:n], dist[:, 0:n], AF.Exp, scale=b_sc,
                                                                   bias=tab[:, 4 + h, -o:-o + 1]),
                                     reads=["dist", "nlng", "tab"], writes=[wk])
                        else:
                            G = t0 // 512
                            jc = j - 16
                            mf = 4 * G - jc + 2
                            mb = 16 - 4 * G + jc
                            B_ = W3
                            k.op("act", lambda e: e.activation(W[:, 0:n], dist[:, 0:n], AF.Exp, scale=f_sc,
                                                               bias=tab[:, h, mf:mf + 1]),
                                 reads=["dist", "lngm", "tab"], writes=[wk])
                            k.op("act", lambda e: e.activation(B_[:, 0:n], dist[:, 0:n], AF.Exp, scale=b_sc,
                                                               bias=tab[:, 4 + h, mb:mb + 1]),
                                 reads=["dist", "nlng", "tab"], writes=["W3"])
                            k.op("pool", lambda e: e.tensor_tensor(W[:, 0:n], W[:, 0:n], B_[:, 0:n], ALU.add),
                                 reads=[wk, "W3"], writes=[wk])
                    pS = psS.next()
                    pi_ = pR.next()
                    P, pk = Pb[pi_], ("Pb", pi_)
                    st["P"], st["pk"] = P, pk
                    k.op("pe", lambda e: e.matmul(PS[pS][:, 0:n], kT[:, h, j * 128:(j + 1) * 128], qT[:, h, t0:t0 + n],
                                                  start=True, stop=True), reads=[("kTd", h), ("qTd", h)],
                         writes=[("ps", pS)])
                    k.op("dve", lambda e: e.tensor_tensor(P[:, 0:n], PS[pS][:, 0:n], Wsrc, ALU.mult),
                         reads=[("ps", pS), wk], writes=[pk])
                return A

            def mkB(h, t0, n, j, st, idx, nk, pn):
                def B():
                    P, pk = st["P"], st["pk"]
                    k.op("pe", lambda e: e.matmul(PS[pn][:, 0:n], V[:, j, h * 128:(h + 1) * 128], P[:, 0:n],
                                                  start=(idx == 0), stop=(idx == nk - 1)), reads=[("Vd", j), pk],
                         writes=[("ps", pn)])
                return B

            def mkC(h, t0, n, pn):
                X_, Q_, L_ = xc[0], sqd[0], sl[0]
                xk_, qk2, lk_ = ("xc", 0), ("sqd", 0), ("sl", 0)
                stt = {}

                def P1():
                    k.op("act", lambda e: e.copy(oh[:, t0:t0 + n], PS[pn][:, 0:n]), reads=[("ps", pn)],
                         writes=[("oh", t0)])
                    pm = psG.next()
                    stt["pm"] = pm
                    k.op("pe", lambda e: e.matmul(PS[pm][:, 0:n], self.onesm[:], oh[:, t0:t0 + n], start=True, stop=True),
                         reads=["onesm", ("oh", t0)], writes=[("ps", pm)])

                def P2():
                    pm = stt["pm"]
                    k.op("dve", lambda e: e.tensor_tensor(X_[:, 0:n], oh[:, t0:t0 + n], PS[pm][:, 0:n], ALU.subtract),
                         reads=[("oh", t0), ("ps", pm)], writes=[xk_])
                    k.op("pool", lambda e: e.tensor_tensor(Q_[:, 0:n], X_[:, 0:n], X_[:, 0:n], ALU.mult),
                         reads=[xk_], writes=[qk2])

                def P3():
                    pv = psG.next()
                    k.op("pe", lambda e: e.matmul(PS[pv][:, 0:n], self.onesm[:], Q_[:, 0:n], start=True, stop=True),
                         reads=["onesm", qk2], writes=[("ps", pv)])
                    k.op("act", lambda e: e.activation(Q_[:, 0:n], PS[pv][:, 0:n], AF.Ln, bias=self.eps_col[:, 0:1],
                                                       scale=1.0), reads=[("ps", pv), "eps"], writes=[qk2])
                    k.op("act", lambda e: e.activation(Q_[:, 0:n], Q_[:, 0:n], AF.Exp, scale=-0.5), reads=[qk2],
                         writes=[qk2])
                    k.op("dve", lambda e: e.scalar_tensor_tensor(X_[:, 0:n], X_[:, 0:n], pc[:, R_GN + h:R_GN + h + 1],
                                                                 Q_[:, 0:n], ALU.mult, ALU.mult),
                         reads=[xk_, qk2, ("pcol", l)], writes=[xk_])

                def P4():
                    pg = psS.next()
                    proj(pg, wg, kg_, h * 128, 128, t0, n)
                    k.op("act", lambda e: e.activation(L_[:, 0:n], PS[pg][:, 0:n], AF.Exp, scale=-1.0), reads=[("ps", pg)],
                         writes=[lk_])
                    k.op("act", lambda e: e.copy(Q_[:, 0:n], PS[pg][:, 0:n]), reads=[("ps", pg), xk_], writes=[qk2])
                    k.op("act", lambda e: e.activation(L_[:, 0:n], L_[:, 0:n], AF.Ln, bias=1.0, scale=1.0), reads=[lk_],
                         writes=[lk_])
                    k.op("act", lambda e: e.activation(L_[:, 0:n], L_[:, 0:n], AF.Exp, scale=-1.0), reads=[lk_],
                         writes=[lk_])
                    k.op("pool", lambda e: e.tensor_tensor(L_[:, 0:n], Q_[:, 0:n], L_[:, 0:n], ALU.mult),
                         reads=[lk_, qk2], writes=[lk_])
                    k.op("pool", lambda e: e.tensor_tensor(brT[:, h, t0:t0 + n], X_[:, 0:n], L_[:, 0:n], ALU.mult),
                         reads=[xk_, lk_], more=[("brT", h)])
                return [(1, P1), (4, P2), (7, P3), (10, P4)]

            def withpre(pre, A):
                def f():
                    pre()
                    A()
                return f

            for i in range(4):
                mkmix(0, i)()
            for h in range(4):
                nstep = 0
                for (t0, n) in qgroups:
                    keys = list(range(NT)) if t0 < S else [16, 17]
                    pn = psN.next()
                    for idx, j in enumerate(keys):
                        st = {}
                        A = mkA(h, t0, n, j, st)
                        if h + 1 < 4 and nstep in (2, 8, 14, 20):
                            A = withpre(mkmix(h + 1, (nstep - 2) // 6), A)
                        nstep += 1
                        steps.append((A, mkB(h, t0, n, j, st, idx, len(keys), pn),
                                      mkC(h, t0, n, pn) if idx == len(keys) - 1 else None))
            self.run_pipe(steps, 4)
            for c in range(4):
                k.dma("sp", self.brs[12 + c], brT[:, c, :], reads=[("brT", c)], writes=[("brs", 12 + c)])
            k.barrier()

    def mix_merge(self, l, ms, hT, hTk, last):
        k, I, PS = self.k, self.I, self.PS
        Xt = lambda t: self.X[t * 128:(t + 1) * 128, :]
        groups = [(g * 512, 512) for g in range(4)] + ([] if last else [(2048, 256)])
        tiles = list(range(16)) + ([] if last else [16, 17])
        with ExitStack() as es0:
            mT = self.sb(es0, "mT", [128, 8, T], BF16)
            with ExitStack() as es:
                brT = self.sb(es, "brT", [128, 16, T], BF16)
                wbr = [self.sb(es, "wbr", [128, 16, 128], BF16) for i in range(2)]
                wg = [self.sb(es, "wgm", [128, 8, 512], BF16) for i in range(2)]
                acc = [self.sb(es, "acc", [128, 512]) for i in range(2)]
                sgm = [self.sb(es, "sgm", [128, 512]) for i in range(2)]
                tmp = [self.sb(es, "tmpm", [128, 512]) for i in range(2)]
                for gi_, (t0_, n_) in enumerate([(g * 512, 512) for g in range(4)] + [(2048, 256)]):
                    for j in range(16):
                        k.dma("sp", brT[:, j, t0_:t0_ + n_], self.brs[j][:, t0_:t0_ + n_], reads=[("brs", j)],
                              writes=[("brT", j, gi_)])
                psP = Rot([0, 1, 2, 3])
                psGt = Rot([4, 5, 6, 7])
                ai = 0
                for e_ in range(8):
                    w = wg[e_ % 2]
                    wk = "wgm%d" % (e_ % 2)
                    wb_ = wbr[e_ % 2]
                    wbk = "wbr%d" % (e_ % 2)
                    for c in range(8):
                        k.dma("pool", w[:, c, :].rearrange("p (a x) -> p a x", a=4),
                              I["w_in"][l][c * 128:(c + 1) * 128, O_GT:O_GT + 4 * D].rearrange(
                                  "p (a x) -> p a x", a=4)[:, :, e_ * 128:(e_ + 1) * 128], writes=[(wk, c)])
                    for kk in range(4):
                        k.dma("pool", wb_[:, kk * 4:(kk + 1) * 4, :],
                              I["w_branch"][l, kk].rearrange("(c p) d -> p c d", p=128)[:, :, e_ * 128:(e_ + 1) * 128],
                              writes=[(wbk, kk * 4 + c) for c in range(4)])
                    for (t0, n) in groups:
                        A = acc[ai % 2]
                        ak = ("acc", ai % 2)
                        ai += 1
                        for kk in range(4):
                            pp = psP.next()
                            pg = psGt.next()
                            for c in range(4):
                                k.op("pe", lambda e, c=c: e.matmul(
                                    PS[pp][:, 0:n], wb_[:, kk * 4 + c, :], brT[:, kk * 4 + c, t0:t0 + n],
                                    start=(c == 0), stop=(c == 3)), reads=[(wbk, kk * 4 + c), ("brT", kk * 4 + c, t0 // 512)],
                                    writes=[("ps", pp)])
                            for c in range(8):
                                k.op("pe", lambda e, c=c: e.matmul(
                                    PS[pg][:, 0:n], w[:, c, kk * 128:(kk + 1) * 128], hT[:, c, t0:t0 + n],
                                    start=(c == 0), stop=(c == 7)), reads=[(wk, c)] + hTk[c][t0 // 128:(t0 + n) // 128],
                                    writes=[("ps", pg)])
                            sg_ = sgm[kk % 2]
                            sk_ = ("sgm", kk % 2)
                            k.op("act", lambda e: e.activation(sg_[:, 0:n], PS[pg][:, 0:n], AF.Sigmoid),
                                 reads=[("ps", pg)], writes=[sk_])
                            if kk == 0:
                                k.op("dve", lambda e: e.tensor_tensor(A[:, 0:n], sg_[:, 0:n], PS[pp][:, 0:n], ALU.mult),
                                     reads=[sk_, ("ps", pp)], writes=[ak])
                            else:
                                tm = tmp[kk % 2]
                                tk = ("tmpm", kk % 2)
                                k.op("dve", lambda e: e.tensor_tensor(tm[:, 0:n], sg_[:, 0:n], PS[pp][:, 0:n], ALU.mult),
                                     reads=[sk_, ("ps", pp)], writes=[tk])
                                if kk < 3:
                                    k.op("dve", lambda e: e.tensor_tensor(A[:, 0:n], A[:, 0:n], tm[:, 0:n], ALU.add),
                                         reads=[tk, ak], writes=[ak])
                                else:
                                    k.op("dve", lambda e: e.tensor_tensor(mT[:, e_, t0:t0 + n], A[:, 0:n], tm[:, 0:n],
                                                                           ALU.add),
                                         reads=[tk, ak], **(dict(writes=[("mT", e_)]) if t0 == 0 else dict(more=[("mT", e_)])))
                k.barrier()
            with ExitStack() as es:
                wout = self.sb(es, "wout", [128, 8, D], BF16)
                xin = [self.sb(es, "mxin2", [128, D]) for i in range(2)]
                rr = [self.sb(es, "mrr", [128, D]) for i in range(2)]
                grow = [self.sb(es, "mgrow", [128, D]) for r in range(2)]
                lng = self.sb(es, "mlng", [128, D])
                lnb = self.sb(es, "mlnb", [128, D])
                st = [self.sb(es, "mst", [128, 12]) for i in range(2)]
                mv = [self.sb(es, "mmv", [128, 4]) for i in range(2)]
                self.load_w(wout, "wout", I["w_out"][l], 8, D, colblk=1024)
                for r in range(2):
                    k.dma("sp", grow[r][:], self.modrow[l, r, 5 * D:6 * D].partition_broadcast(128), writes=[("grow", r)])
                k.dma("sp", lng[:], I["ln_g"][l, 1].partition_broadcast(128), writes=["lng"])
                k.dma("sp", lnb[:], I["ln_b"][l, 1].partition_broadcast(128), writes=["lnb"])
                pyr = Rot([0, 1, 2, 3])
                for ti, t in enumerate(tiles):
                    r = 0 if t < 16 else 1
                    xi = xin[ti % 2]
                    xk = ("mxin2", ti % 2)
                    R = rr[ti % 2]
                    rk = ("rr", ti % 2)
                    k.dma("sp", xi[:], Xt(t), reads=[("X", t)], writes=[xk])
                    for dh in range(2):
                        py = pyr.next()
                        for e_ in range(8):
                            k.op("pe", lambda e, e_=e_: e.matmul(
                                PS[py][:, :], mT[:, e_, t * 128:(t + 1) * 128], wout[:, e_, dh * 512:(dh + 1) * 512],
                                start=(e_ == 0), stop=(e_ == 7)), reads=[("mT", e_), ("wout", e_)], writes=[("ps", py)])
                        kw = dict(writes=[rk]) if dh == 0 else dict(more=[rk])
                        k.op("dve", lambda e: e.tensor_tensor(R[:, dh * 512:(dh + 1) * 512], PS[py][:, :],
                                                              grow[r][:, dh * 512:(dh + 1) * 512], ALU.mult),
                             reads=[("ps", py), ("grow", r)], **kw)
                    k.op("dve", lambda e: e.scalar_tensor_tensor(R[:], xi[:], ALPHA, R[:], ALU.mult, ALU.add),
                         reads=[xk, rk], writes=[rk])
                    self.layer_norm(R, rk, lng, lnb, st[ti % 2], mv[ti % 2], ("ln", ti % 2))
                    k.dma("pool", Xt(t), R[:], reads=[rk], writes=[("X", t)])
                    if self.debug:
                        k.dma("sp", self.dbg["x_%d_mx" % l][t * 128:(t + 1) * 128, :], R[:], reads=[rk])
                k.barrier()


def build(debug=False, n_layers=2, stop=None):
    b = Builder(debug=debug, n_layers=n_layers)
    b.stop = stop
    I = b.I
    try:
        _build_body(b, n_layers)
    except StopBuild:
        pass
    b.k.finish()
    b.top.close()
    return b


def _build_body(b, n_layers):
    I = b.I
    b.prologue()
    if b.chk("pro"):
        return
    Xt = lambda t: b.X[t * 128:(t + 1) * 128, :]
    for l in range(n_layers):
        last = (l == 1)
        if l == 0:
            src0 = lambda t: (I["x"][t * 128:(t + 1) * 128, :] if t < 16 else I["ctx"][(t - 16) * 128:(t - 15) * 128, :])
        else:
            src0 = Xt
        b.ffn(l, 0, src0, Xt, 0, 1, 2, 0, list(range(NT)), dbgname="x_%d_f0" % l)
        if b.chk("f0"):
            return
        b.mixer(l, last)
        if b.chk("mixer"):
            return
        if last:
            b.ffn(l, 1, Xt, lambda t: b.out[t * 128:(t + 1) * 128, :], 6, 7, 8, 2, list(range(16)), dbgname="x_%d_f1" % l)
        else:
            b.ffn(l, 1, Xt, Xt, 6, 7, 8, 2, list(range(NT)), dbgname="x_%d_f1" % l)


def _host_constants():
    c = {}
    c["c_ident"] = np.eye(128, dtype=np.float32)
    import ml_dtypes
    jj = np.arange(128)[:, None]
    ii = np.arange(128)[None, :]
    t1 = (jj >= ii).astype(np.float32)
    t2 = (jj <= ii).astype(np.float32)
    c["c_tri"] = np.concatenate([np.tile(t1, (1, 4)), np.tile(t2, (1, 4))], 1).astype(ml_dtypes.bfloat16)
    c["c_dist"] = (np.arange(512)[None, :] - np.arange(128)[:, None]).astype(np.float32)
    c["c_mults"] = np.tile((128.0 * np.arange(18))[None, :], (128, 1)).astype(np.float32)
    dpn = np.zeros((8, 128, 512), np.float32)
    for i in range(4):
        dd = c["c_dist"] - 128.0 * i
        dpn[2 * i] = np.maximum(dd, 0.0)
        dpn[2 * i + 1] = np.maximum(-dd, 0.0)
    c["c_dpn"] = dpn

    def rope(dim):
        da = dim // 2
        nf = da // 2
        inv = 10000.0 ** (-np.arange(nf, dtype=np.float64) / nf)
        tpos = np.arange(S)
        C = np.ones((dim, T), np.float64)
        Sg = np.zeros((dim, T), np.float64)
        perm = np.zeros(dim, np.int64)
        for f in range(dim):
            seg, u = f // da, f % da
            pos = (tpos // 64) if seg == 0 else (tpos % 64)
            ang = (pos.astype(np.float32) * np.float32(inv[u % nf])).astype(np.float64)
            C[f, :S] = np.cos(ang)
            if u < nf:
                Sg[f, :S] = -np.sin(ang)
                perm[f] = f + nf
            else:
                Sg[f, :S] = np.sin(ang)
                perm[f] = f - nf
        return np.stack([C, Sg]).astype(np.float32), perm

    rb, pb = rope(64)
    rcm, pcm = rope(32)
    rd, pd = rope(128)
    c["rope_b"] = rb
    rc96 = np.zeros((2, 96, T), np.float32)
    rc96[0] = 1.0
    rc96[:, 64:96] = rcm
    c["rope_c"] = rc96
    c["rope_d"] = rd
    return c, pb, pcm, pd


def host_inputs(inputs):
    consts, pb, pcm, pd = _host_constants()
    f = lambda a: np.ascontiguousarray(np.asarray(a, dtype=np.float32))
    w_in = f(inputs["w_in"])
    cols = []
    for h in range(8):
        cols += list(O_BQ + h * 64 + pb)
    for h in range(2):
        cols += list(O_BK + h * 64 + pb)
    cols += list(O_CKR + pcm)
    for h in range(4):
        cols += list(O_DQ + h * 128 + pd)
    for h in range(4):
        cols += list(O_DK + h * 128 + pd)
    cols = np.asarray(cols)
    assert cols.shape[0] == NSW
    w_sw = np.ascontiguousarray(w_in[:, :, cols])
    w_uq = f(inputs["mla_w_uq"])
    w_uq_sw = np.zeros_like(w_uq)
    for h in range(8):
        w_uq_sw[:, :, h * 96 + 64:h * 96 + 96] = w_uq[:, :, h * 96 + 64 + pcm]
    prow = np.zeros((2, 64, 128), np.float32)
    for l in range(2):
        prow[l, R_CW:R_CW + 16] = f(inputs["lru_conv_w"])[l].reshape(16, 128)
        prow[l, R_CB:R_CB + 4] = f(inputs["lru_conv_b"])[l].reshape(4, 128)
        prow[l, R_BA:R_BA + 8] = f(inputs["lru_b_a"])[l].reshape(8, 128)
        prow[l, R_BX:R_BX + 8] = f(inputs["lru_b_x"])[l].reshape(8, 128)
        prow[l, R_LAM:R_LAM + 8] = f(inputs["lru_lambda"])[l].reshape(8, 128)
        prow[l, R_QN:R_QN + 3] = f(inputs["mla_q_norm"])[l].reshape(3, 128)
        prow[l, R_KVN:R_KVN + 2] = f(inputs["mla_kv_norm"])[l].reshape(2, 128)
        prow[l, R_GN:R_GN + 4] = f(inputs["ret_gn_g"])[l].reshape(4, 128)
    shared = dict(consts)
    shared.update({
        "w_mod": f(inputs["w_mod"]), "b_mod": f(inputs["b_mod"]).reshape(2, 1, 9 * D),
        "ln_g": f(inputs["ln_g"]), "ln_b": f(inputs["ln_b"]),
        "ffn_w_in": f(inputs["ffn_w_in"]), "ffn_w_out": f(inputs["ffn_w_out"]),
        "w_in": w_in, "w_sw": w_sw, "prow": prow,
        "lru_w_a": f(inputs["lru_w_a"]), "lru_w_x": f(inputs["lru_w_x"]),
        "gqa_sink": f(inputs["gqa_sink"]), "ret_decay": f(inputs["ret_decay"]).reshape(2, 8),
        "mla_w_uq": w_uq, "mla_w_uq_sw": w_uq_sw, "mla_w_ukv": f(inputs["mla_w_ukv"]),
        "w_branch": f(inputs["w_branch"]), "w_out": f(inputs["w_out"]),
    })
    x = f(inputs["x"]); ctx = f(inputs["ctx"]); c = f(inputs["c"]); c_ctx = f(inputs["c_ctx"])
    maps = []
    for b in range(x.shape[0]):
        m = dict(shared)
        m["x"] = x[b]
        m["ctx"] = ctx[b]
        m["cc"] = np.concatenate([c[b].reshape(8, 128), c_ctx.reshape(8, 128)], 0)
        maps.append(m)
    return maps


def kernel(**inputs):
    maps = host_inputs(inputs)
    b = build()
    res = run_bass_kernel_spmd(b.nc, maps, core_ids=list(range(8)))
    return np.stack([np.asarray(r["out"], dtype=np.float32) for r in res.results], 0)
```
